# Optimizing a Trainium2 kernel written in Bass

```python
import math
import jax, jax.numpy as jnp
from jax import lax
import numpy as np

D_MODEL = 1024
BATCH = 32
SEQ = 2048
DEPTH = 4

HEAD_DIM = 64
D_MIX = D_MODEL
N_GROUPS = 4
GROUP_WIDTH = D_MIX // N_GROUPS
HEADS_PER_GROUP = GROUP_WIDTH // HEAD_DIM
Q_BLOCK = 128
D_FF = 4 * D_MODEL
EPS = 1e-6
NEG_INF = -1e30

SB_HEADS = HEADS_PER_GROUP
NSA_HEADS = HEADS_PER_GROUP
NSA_KV_DIM = HEAD_DIM
CMP_LEN = 32
CMP_STRIDE = 16
CMP_HIDDEN = 2 * HEAD_DIM
SEL_LEN = 64
SEL_TOPN = 8
SEL_Q_BLOCK = 64
FORCE_SCORE = 1e9
NSA_WINDOW = 512
N_NSA_BRANCHES = 3
DIFF_HEADS = HEADS_PER_GROUP
DIFF_QK_DIM = HEAD_DIM // 2
DIFF_V_DIM = HEAD_DIM
SWA_HEADS = HEADS_PER_GROUP
SWA_KV_HEADS = 2
SWA_WINDOW = 128
N_ALIBI_HEADS = NSA_HEADS + DIFF_HEADS + SWA_HEADS

IN_SIZES = (
    SB_HEADS * HEAD_DIM, SB_HEADS * HEAD_DIM, SB_HEADS * HEAD_DIM,
    NSA_HEADS * HEAD_DIM, NSA_KV_DIM, NSA_KV_DIM, NSA_KV_DIM, NSA_KV_DIM, NSA_KV_DIM, NSA_KV_DIM,
    N_NSA_BRANCHES * NSA_HEADS,
    DIFF_HEADS * 2 * DIFF_QK_DIM, DIFF_HEADS * 2 * DIFF_QK_DIM, DIFF_HEADS * DIFF_V_DIM,
    SWA_HEADS * HEAD_DIM, SWA_KV_HEADS * HEAD_DIM, SWA_KV_HEADS * HEAD_DIM,
)
IN_COLS = sum(IN_SIZES)

kernel_name = "hybrid_parallel_heads_sb_nsa_diff_swa"


def _rmsnorm(x, gain=None):
    xf = x.astype(jnp.float32)
    y = xf * lax.rsqrt(jnp.mean(xf * xf, axis=-1, keepdims=True) + EPS)
    if gain is not None:
        y = y * gain.astype(jnp.float32)
    return y.astype(x.dtype)


def _alibi_slopes():
    m = 2.0 ** (-8.0 * np.arange(1, N_ALIBI_HEADS + 1) / N_ALIBI_HEADS)
    m = m.astype(np.float32).reshape(HEADS_PER_GROUP, 3)
    return jnp.asarray(m[:, 0]), jnp.asarray(m[:, 1]), jnp.asarray(m[:, 2])


def _sweep(block_fn, n_blocks):
    out = lax.map(block_fn, jnp.arange(n_blocks))
    nb, b, qb = out.shape[:3]
    return jnp.moveaxis(out, 0, 1).reshape((b, nb * qb) + out.shape[3:])


def _stick_breaking(q, k, v):
    B, S, H, d = q.shape
    scale = d ** -0.5
    kpos = jnp.arange(S)

    def block(i):
        start = i * Q_BLOCK
        qb = lax.dynamic_slice_in_dim(q, start, Q_BLOCK, axis=1)
        qpos = start + jnp.arange(Q_BLOCK)
        past = kpos[None, :] < qpos[:, None]
        z = jnp.einsum('bqhd,bkhd->bhqk', qb, k).astype(jnp.float32) * scale
        log_beta = jax.nn.log_sigmoid(z)
        log_keep = jnp.where(past, jax.nn.log_sigmoid(-z), 0.0)
        acc = lax.cumsum(log_keep, axis=3, reverse=True) - log_keep
        a = jnp.where(past, jnp.exp(log_beta + acc), 0.0)
        return jnp.einsum('bhqk,bkhd->bqhd', a.astype(v.dtype), v)

    return _sweep(block, S // Q_BLOCK)


def _banded_attention(q, k, v, slopes, window, sinks=None):
    B, S, H, d = q.shape
    kvh = k.shape[2]
    grp = H // kvh
    n_prev = -(-window // Q_BLOCK)
    pad = n_prev * Q_BLOCK
    band = pad + Q_BLOCK
    kp = jnp.pad(k, ((0, 0), (pad, 0), (0, 0), (0, 0)))
    vp = jnp.pad(v, ((0, 0), (pad, 0), (0, 0), (0, 0)))
    scale = d ** -0.5
    sl = slopes.reshape(kvh, grp)[None, :, :, None, None]

    def block(i):
        start = i * Q_BLOCK
        qb = lax.dynamic_slice_in_dim(q, start, Q_BLOCK, axis=1).reshape(B, Q_BLOCK, kvh, grp, d)
        kb = lax.dynamic_slice_in_dim(kp, start, band, axis=1)
        vb = lax.dynamic_slice_in_dim(vp, start, band, axis=1)
        qpos = start + jnp.arange(Q_BLOCK)
        kpos = start - pad + jnp.arange(band)
        dist = qpos[:, None] - kpos[None, :]
        mask = (dist >= 0) & (dist < window) & (kpos >= 0)[None, :]
        z = jnp.einsum('bqgrd,bkgd->bgrqk', qb, kb).astype(jnp.float32) * scale
        z = z - sl * dist.astype(jnp.float32)
        z = jnp.where(mask, z, NEG_INF)
        if sinks is not None:
            sk = jnp.broadcast_to(sinks.astype(jnp.float32).reshape(kvh, grp)[None, :, :, None, None],
                                  z.shape[:-1] + (1,))
            p = jax.nn.softmax(jnp.concatenate([z, sk], axis=-1), axis=-1)[..., :-1]
        else:
            p = jax.nn.softmax(z, axis=-1)
        o = jnp.einsum('bgrqk,bkgd->bqgrd', p.astype(vb.dtype), vb)
        return o.reshape(B, Q_BLOCK, H, d)

    return _sweep(block, S // Q_BLOCK)


def _native_sparse_attention(q, k_cmp, v_cmp, k_sel, v_sel, k_win, v_win, gate_logits,
                             pe_k, w1_k, w2_k, pe_v, w1_v, w2_v, slopes):
    B, S, H, d = q.shape
    scale = d ** -0.5
    tpos = jnp.arange(S)

    n_cmp = (S - CMP_LEN) // CMP_STRIDE + 1
    starts = CMP_STRIDE * np.arange(n_cmp)
    idx = starts[:, None] + np.arange(CMP_LEN)[None, :]

    def compress(t, pe, w1, w2):
        blk = (t[:, idx] + pe).reshape(B, n_cmp, CMP_LEN * t.shape[-1])
        return jax.nn.gelu(blk @ w1) @ w2

    kc = compress(k_cmp, pe_k, w1_k, w2_k)
    vc = compress(v_cmp, pe_v, w1_v, w2_v)
    dist_c = (tpos[:, None] - jnp.asarray(starts + CMP_LEN - 1)[None, :]).astype(jnp.float32)
    valid_c = dist_c >= 0
    z = jnp.einsum('bthd,bid->bhti', q, kc).astype(jnp.float32) * scale - slopes[:, None, None] * dist_c
    z = jnp.where(valid_c, z, NEG_INF)
    p_c = jax.nn.softmax(z, axis=-1) * valid_c
    o_cmp = jnp.einsum('bhti,bid->bthd', p_c.astype(vc.dtype), vc)

    n_sel = S // SEL_LEN
    n_top = min(SEL_TOPN, n_sel)
    to_sel = jax.nn.one_hot(starts // SEL_LEN, n_sel, dtype=jnp.float32)
    imp = jnp.einsum('bhti,ij->btj', p_c, to_sel)
    blk_id = jnp.arange(n_sel)[None, :]
    cur = (tpos // SEL_LEN)[:, None]
    forced = (blk_id == 0) | (blk_id == cur) | (blk_id == cur - 1)
    score = jnp.where(blk_id > cur, NEG_INF, jnp.where(forced, FORCE_SCORE, imp))
    _, sel = lax.top_k(score, n_top)
    ks_blk = k_sel.reshape(B, n_sel, SEL_LEN, d)
    vs_blk = v_sel.reshape(B, n_sel, SEL_LEN, d)
    sl5 = slopes[None, :, None, None, None]

    def sel_block(i):
        start = i * SEL_Q_BLOCK
        qb = lax.dynamic_slice_in_dim(q, start, SEL_Q_BLOCK, axis=1)
        sb = lax.dynamic_slice_in_dim(sel, start, SEL_Q_BLOCK, axis=1)
        kg = jax.vmap(lambda blk, s: blk[s])(ks_blk, sb)
        vg = jax.vmap(lambda blk, s: blk[s])(vs_blk, sb)
        qpos = start + jnp.arange(SEL_Q_BLOCK)
        kpos = sb[..., None] * SEL_LEN + jnp.arange(SEL_LEN)
        dist = (qpos[None, :, None, None] - kpos).astype(jnp.float32)[:, None]
        zs = jnp.einsum('bqhd,bqnld->bhqnl', qb, kg).astype(jnp.float32) * scale - sl5 * dist
        zs = jnp.where(dist >= 0, zs, NEG_INF)
        p = jax.nn.softmax(zs.reshape(B, H, SEL_Q_BLOCK, -1), axis=-1).reshape(zs.shape)
        return jnp.einsum('bhqnl,bqnld->bqhd', p.astype(vg.dtype), vg)

    o_sel = _sweep(sel_block, S // SEL_Q_BLOCK)

    o_win = _banded_attention(q, k_win[:, :, None], v_win[:, :, None], slopes, NSA_WINDOW)

    g = jax.nn.sigmoid(gate_logits.astype(jnp.float32))
    return (g[..., 0:1] * o_cmp + g[..., 1:2] * o_sel + g[..., 2:3] * o_win).astype(q.dtype)


def _differential_attention(q, k, v, slopes, lam):
    B, S, H, _, dqk = q.shape
    scale = dqk ** -0.5
    kpos = jnp.arange(S)
    sl = slopes[None, :, None, None, None]

    def block(i):
        start = i * Q_BLOCK
        qb = lax.dynamic_slice_in_dim(q, start, Q_BLOCK, axis=1)
        qpos = start + jnp.arange(Q_BLOCK)
        dist = (qpos[:, None] - kpos[None, :]).astype(jnp.float32)
        z = jnp.einsum('bqhcd,bkhcd->bhcqk', qb, k).astype(jnp.float32) * scale - sl * dist
        z = jnp.where(dist >= 0, z, NEG_INF)
        p = jax.nn.softmax(z, axis=-1)
        w = p[:, :, 0] - lam * p[:, :, 1]
        return jnp.einsum('bhqk,bkhd->bqhd', w.astype(v.dtype), v)

    return _sweep(block, S // Q_BLOCK)


def setup_inputs(seed: int = 0) -> dict:
    key = jax.random.key(seed)
    ks = jax.random.split(key, 24)
    f32 = jnp.float32

    def nrm(k, shape, scale):
        return jax.random.normal(k, shape, f32) * scale

    def gain(k, shape):
        return 1.0 + 0.01 * jax.random.normal(k, shape, f32)

    flat = CMP_LEN * NSA_KV_DIM
    return {
        "x": nrm(ks[0], (BATCH, SEQ, D_MODEL), 1.0),
        "norm_attn": gain(ks[1], (DEPTH, D_MODEL)),
        "w_in": nrm(ks[2], (DEPTH, D_MODEL, IN_COLS), D_MODEL ** -0.5),
        "cmp_pe_k": nrm(ks[3], (DEPTH, CMP_LEN, NSA_KV_DIM), 0.1),
        "cmp_w1_k": nrm(ks[4], (DEPTH, flat, CMP_HIDDEN), flat ** -0.5),
        "cmp_w2_k": nrm(ks[5], (DEPTH, CMP_HIDDEN, NSA_KV_DIM), CMP_HIDDEN ** -0.5),
        "cmp_pe_v": nrm(ks[6], (DEPTH, CMP_LEN, NSA_KV_DIM), 0.1),
        "cmp_w1_v": nrm(ks[7], (DEPTH, flat, CMP_HIDDEN), flat ** -0.5),
        "cmp_w2_v": nrm(ks[8], (DEPTH, CMP_HIDDEN, NSA_KV_DIM), CMP_HIDDEN ** -0.5),
        "diff_lq1": nrm(ks[9], (DEPTH, DIFF_QK_DIM), 0.1),
        "diff_lk1": nrm(ks[10], (DEPTH, DIFF_QK_DIM), 0.1),
        "diff_lq2": nrm(ks[11], (DEPTH, DIFF_QK_DIM), 0.1),
        "diff_lk2": nrm(ks[12], (DEPTH, DIFF_QK_DIM), 0.1),
        "sinks": nrm(ks[13], (DEPTH, SWA_HEADS), 0.5),
        "g_mix": gain(ks[14], (DEPTH, D_MIX)),
        "w_out": nrm(ks[15], (DEPTH, D_MIX, D_MODEL), D_MIX ** -0.5),
        "norm_mlp": gain(ks[16], (DEPTH, D_MODEL)),
        "w_up": nrm(ks[17], (DEPTH, D_MODEL, D_FF), D_MODEL ** -0.5),
        "w_down": nrm(ks[18], (DEPTH, D_FF, D_MODEL), 0.5 * D_FF ** -0.5),
        "norm_final": gain(ks[19], (D_MODEL,)),
    }


def reference(x, norm_attn, w_in, cmp_pe_k, cmp_w1_k, cmp_w2_k, cmp_pe_v, cmp_w1_v, cmp_w2_v,
              diff_lq1, diff_lk1, diff_lq2, diff_lk2, sinks, g_mix, w_out, norm_mlp, w_up, w_down,
              norm_final):
    B, S, _ = x.shape
    slopes_nsa, slopes_diff, slopes_swa = _alibi_slopes()
    split_at = [int(c) for c in np.cumsum(IN_SIZES)[:-1]]

    def heads(t, n):
        return t.reshape(B, S, n, -1)

    for l in range(DEPTH):
        h = _rmsnorm(x, norm_attn[l])
        proj = jnp.einsum('bsd,dc->bsc', h, w_in[l])
        (qa, ka, va, qn, kcn, vcn, ksn, vsn, kwn, vwn, gn,
         qc, kc, vc, qd, kd, vd) = jnp.split(proj, split_at, axis=-1)

        o_sb = _stick_breaking(heads(qa, SB_HEADS), heads(ka, SB_HEADS), heads(va, SB_HEADS))

        o_nsa = _native_sparse_attention(
            heads(qn, NSA_HEADS), kcn, vcn, ksn, vsn, kwn, vwn, heads(gn, NSA_HEADS),
            cmp_pe_k[l], cmp_w1_k[l], cmp_w2_k[l], cmp_pe_v[l], cmp_w1_v[l], cmp_w2_v[l], slopes_nsa)

        lam_init = 0.8 - 0.6 * math.exp(-0.3 * l)
        lam = (jnp.exp(jnp.sum(diff_lq1[l].astype(jnp.float32) * diff_lk1[l].astype(jnp.float32)))
               - jnp.exp(jnp.sum(diff_lq2[l].astype(jnp.float32) * diff_lk2[l].astype(jnp.float32)))
               + lam_init)
        o_diff = _differential_attention(qc.reshape(B, S, DIFF_HEADS, 2, DIFF_QK_DIM),
                                         kc.reshape(B, S, DIFF_HEADS, 2, DIFF_QK_DIM),
                                         heads(vc, DIFF_HEADS), slopes_diff, lam)
        o_diff = _rmsnorm(o_diff) * (1.0 - lam_init)

        o_swa = _banded_attention(heads(qd, SWA_HEADS), heads(kd, SWA_KV_HEADS), heads(vd, SWA_KV_HEADS),
                                  slopes_swa, SWA_WINDOW, sinks[l])

        mix = jnp.concatenate([_rmsnorm(o_sb.reshape(B, S, -1)),
                               _rmsnorm(o_nsa.reshape(B, S, -1)),
                               o_diff.reshape(B, S, -1),
                               _rmsnorm(o_swa.reshape(B, S, -1))], axis=-1)
        x = x + jnp.einsum('bsc,cd->bsd', mix * g_mix[l], w_out[l]).astype(x.dtype)

        h = _rmsnorm(x, norm_mlp[l])
        u = jax.nn.relu(jnp.einsum('bsd,df->bsf', h, w_up[l]))
        x = x + jnp.einsum('bsf,fd->bsd', u * u, w_down[l]).astype(x.dtype)

    return _rmsnorm(x, norm_final)
```

```python
import math
import numpy as np
import ml_dtypes
import concourse.bass as bass
import concourse.mybir as mybir
from concourse.bass_utils import run_bass_kernel_spmd

F32 = mybir.dt.float32
BF16 = mybir.dt.bfloat16
AF = mybir.ActivationFunctionType
ALU = mybir.AluOpType
AX = mybir.AxisListType

S = 2048
D = 1024
NT = 16
NQB = 4
DEPTH = 4
DFF = 4096
EPS = 1e-6
NEG = -30000.0
NFM = 20
NTM = 768
SC64 = 64 ** -0.5
SC32 = 32 ** -0.5

QA0, QA1, KA0, KA1, QN0, QN1, KVC, KSD, KWD, GN, QC0, QC1, KC00, KC10, KC01, KC11, QD0, QD1, KD0, KD1 = range(20)
TM_VA, TM_VS, TM_VW, TM_VC, TM_VD = (0, 256), (256, 320), (320, 384), (384, 640), (640, 768)


def _slopes():
    m = 2.0 ** (-8.0 * np.arange(1, 13) / 12.0)
    m = m.astype(np.float32).reshape(4, 3)
    return m[:, 0].copy(), m[:, 1].copy(), m[:, 2].copy()


class _Cols:
    def __init__(self):
        self.off = {}
        self.n = 0
        self.parts = []

    def add(self, name, arr):
        arr = np.asarray(arr, dtype=np.float32)
        assert arr.shape[0] <= 128
        a = np.zeros((128, arr.shape[1]), np.float32)
        a[:arr.shape[0]] = arr
        self.off[name] = (self.n, arr.shape[1])
        self.n += arr.shape[1]
        self.parts.append(a)

    def build(self):
        return np.concatenate(self.parts, axis=1)


def _alibi_heads():
    sn, sd, ss = _slopes()
    slopes = np.concatenate([sn, sd, ss]).astype(np.float64)
    scales = np.array([SC64] * 4 + [SC32] * 4 + [SC64] * 4)
    return slopes, scales


def _build_consts():
    slopes, scales = _alibi_heads()
    cb = _Cols()
    p = np.arange(128)
    cb.add("ident", np.eye(128))
    cb.add("ones", np.ones((128, 128)))
    cb.add("u8", np.where(p[:, None] >= p[None, :], -8.0, 0.0))
    cb.add("ones8", np.full((128, 128), -8.0))
    cb.add("mdiag", np.where(p[:, None] <= p[None, :], 0.0, NEG))
    cb.add("mstrict", np.where(p[:, None] < p[None, :], 0.0, NEG))
    cb.add("manti", np.where(p[:, None] > p[None, :], 0.0, NEG))
    sel = np.zeros((12, 12 * 128))
    for h in range(12):
        sel[h, h * 128:(h + 1) * 128] = 1.0
    cb.add("rowsel", sel)
    tq = np.arange(512)
    rrow = np.stack([-(slopes[h] * tq) / scales[h] for h in range(12)], 0)
    rrow_bf = rrow.astype(np.float32).astype(ml_dtypes.bfloat16).astype(np.float32)
    cb.add("rrow", rrow_bf)
    i = np.arange(127)
    t = np.arange(S)
    cb.add("mc", np.where((16 * i[:, None] + 31) <= t[None, :], 0.0, NEG))
    g = np.zeros((127, 64))
    g[i, i // 4] = 1.0
    g[:, 32:64] = 1.0
    cb.add("gaug", g)
    m = np.arange(S)
    cb.add("esel", (m[None, :] // 64 == np.arange(32)[:, None]).astype(np.float32))
    cand = np.zeros((128, 16, 32))
    forced = np.zeros((128, 16, 32))
    j = np.arange(32)
    for tt in range(16):
        cur = (tt * 128 + p) // 64
        cand[:, tt, :] = ((j[None, :] >= 1) & (j[None, :] <= cur[:, None] - 2))
        forced[:, tt, :] = ((j[None, :] == 0) | (j[None, :] == cur[:, None]) | (j[None, :] == cur[:, None] - 1))
    cb.add("cand", cand.reshape(128, 512))
    cb.add("candm1", cand.reshape(128, 512) - 1.0)
    cb.add("forced", forced.reshape(128, 512))
    gs = np.zeros((12, 12 * 64))
    for k in range(12):
        gs[k, k * 64:(k + 1) * 64] = 1.0
    cb.add("gatesel", gs)
    bd = np.zeros((128, 128))
    bd[:64, :64] = 1.0
    bd[64:, 64:] = 1.0
    cb.add("blockdiag", bd)
    ss = np.zeros((4, 4 * 128))
    for h in range(4):
        ss[h, h * 128 + 64:(h + 1) * 128] = 1.0
    cb.add("sinksel", ss)
    cbarr = cb.build()

    cf = _Cols()
    cf.add("identf", np.eye(128))
    bp = np.zeros((128, 12 * 19))
    for h in range(12):
        for r in range(19):
            bp[:, h * 19 + r] = slopes[h] * ((r - 15) * 128 + p)
    cf.add("biasp", bp)
    bc = np.zeros((128, 16))
    for h in range(4):
        for qb in range(4):
            bc[:, h * 4 + qb] = slopes[h] * (16 * p + 31 - 512 * qb)
    cf.add("biasc", bc)
    er = np.stack([scales[8 + h] * rrow_bf[8 + h] + slopes[8 + h] * tq for h in range(4)], 0)
    cf.add("epsrow", er)
    cf.add("epsc", np.full((128, 1), EPS))
    cf.add("one", np.full((128, 1), 1.0))
    cfarr = cf.build()
    return cb.off, cbarr.astype(ml_dtypes.bfloat16), cf.off, cfarr.astype(np.float32)


class _Rec:
    def __init__(self):
        self.calls = []

    def __getattr__(self, name):
        def f(*a, **k):
            self.calls.append((name, a, k))
            return self
        return f


class Buf:
    __slots__ = ("name", "w", "r")

    def __init__(self, name):
        self.name = name
        self.w = None
        self.r = {}


class Chan:
    def __init__(self, sem):
        self.sem = sem
        self.count = 0


class KB:
    CE = ("pe", "act", "dve", "pool")

    def __init__(self, nc, sems):
        self.nc = nc
        self.sems = list(sems)
        self.streams = ("pe", "act", "dve", "pool", "sp")
        self.prog = {k: [] for k in self.streams}
        self.ccount = {k: 0 for k in self.CE}
        self.csem = {k: self.sems.pop() for k in self.CE}
        self.waited = {k: {} for k in self.streams}
        self.chans = []
        self.bufs = []

    def buf(self, name):
        b = Buf(name)
        self.bufs.append(b)
        return b

    def chan(self):
        c = Chan(self.sems.pop())
        self.chans.append(c)
        return c

    def _sem(self, k):
        return self.csem[k] if isinstance(k, str) else k.sem

    def _waits(self, stream, reads, writes, dma):
        need = {}
        for b in reads:
            if b.w is not None and b.w[1] > need.get(b.w[0], 0):
                need[b.w[0]] = b.w[1]
        for b in writes:
            if b.w is not None and b.w[1] > need.get(b.w[0], 0):
                need[b.w[0]] = b.w[1]
            for k, v in b.r.items():
                if v > need.get(k, 0):
                    need[k] = v
        out = []
        wd = self.waited[stream]
        for k, v in need.items():
            if k == "pe" and stream == "pe":
                continue
            if dma is not None and k is dma:
                continue
            if wd.get(k, 0) >= v:
                continue
            wd[k] = v
            out.append((self._sem(k), v))
        return out

    def op(self, stream, fn, reads=(), writes=(), dma=None):
        waits = self._waits(stream, reads, writes, dma)
        if dma is None:
            self.ccount[stream] += 1
            key, val, sem, inc = stream, self.ccount[stream], self.csem[stream], 1
        else:
            dma.count += 16
            key, val, sem, inc = dma, dma.count, dma.sem, 16
        for b in reads:
            if b.r.get(key, 0) < val:
                b.r[key] = val
        for b in writes:
            b.w = (key, val)
            b.r = {}

        rec = _Rec()
        fn(rec)
        assert len(rec.calls) == 1
        name, a, k = rec.calls[0]

        def run(e, waits=waits, name=name, a=a, k=k, sem=sem, inc=inc):
            for (sm, v) in waits:
                e.wait_ge(sm, v)
            try:
                inst = getattr(e, name)(*a, **k)
            except Exception:
                print("FAILED OP", name, k.keys(), [(kk, getattr(v, 'shape', v), getattr(v, 'dtype', None)) for kk, v in k.items()])
                raise
            inst.then_inc(sem, inc)
        self.prog[stream].append(run)

    def barrier(self):
        for st in self.streams:
            waits = []
            wd = self.waited[st]
            for k in self.CE:
                if k == "pe" and st == "pe":
                    continue
                v = self.ccount[k]
                if v > wd.get(k, 0):
                    wd[k] = v
                    waits.append((self.csem[k], v))
            for c in self.chans:
                if c.count > wd.get(c, 0):
                    wd[c] = c.count
                    waits.append((c.sem, c.count))

            def run(e, waits=waits):
                for (sm, v) in waits:
                    e.wait_ge(sm, v)
            self.prog[st].append(run)
        for b in self.bufs:
            b.w = None
            b.r = {}


def _ap(t, offset, pattern):
    return bass.AP(tensor=t.tensor, offset=offset, ap=[list(x) for x in pattern])


def build_program(nseq=4, nlayer=DEPTH, mixers=("sb", "nsa", "diff", "swa"), ffn=True, dbg=False):
    cboff, cbarr, cfoff, cfarr = _build_consts()
    NCB = cbarr.shape[1]
    NCF = cfarr.shape[1]
    sn, sd, ssw = _slopes()

    nc = bass.Bass("TRN2", target_bir_lowering=False)
    x_d = nc.dram_tensor("x", [nseq, S, D], F32, kind="ExternalInput").ap()
    out_d = nc.dram_tensor("out", [nseq, S, D], F32, kind="ExternalOutput").ap()
    cb_d = nc.dram_tensor("cb16", [129, NCB], BF16, kind="ExternalInput").ap()[0:128, :]
    cf_d = nc.dram_tensor("cf32", [128, NCF], F32, kind="ExternalInput").ap()
    wfm_d = nc.dram_tensor("wfm", [nlayer * NFM * 128 + 1, 1024], F32, kind="ExternalInput").ap()[0:nlayer * NFM * 128, :]
    wtm_d = nc.dram_tensor("wtm", [nlayer * 128 + 1, 8 * NTM], F32, kind="ExternalInput").ap()[0:nlayer * 128, :]
    wout_d = nc.dram_tensor("wout", [nlayer * 4 * 128 + 1, 2048], F32, kind="ExternalInput").ap()[0:nlayer * 4 * 128, :]
    wup_d = nc.dram_tensor("wup", [nlayer * 8 * 128 + 1, 4096], F32, kind="ExternalInput").ap()[0:nlayer * 8 * 128, :]
    wdn_d = nc.dram_tensor("wdn", [nlayer * 8 * 128 + 1, 4096], F32, kind="ExternalInput").ap()[0:nlayer * 8 * 128, :]
    w1_d = nc.dram_tensor("w1", [nlayer * 128 + 1, 4096], F32, kind="ExternalInput").ap()[0:nlayer * 128, :]
    w2_d = nc.dram_tensor("w2", [nlayer * 128, 192], F32, kind="ExternalInput").ap()
    gt_d = nc.dram_tensor("gt", [128, nlayer * 24], F32, kind="ExternalInput").ap()
    lam_d = nc.dram_tensor("lamv", [128, nlayer * 128], F32, kind="ExternalInput").ap()
    sink_d = nc.dram_tensor("sinkt", [4, nlayer], F32, kind="ExternalInput").ap()
    pe_d = nc.dram_tensor("pet", [128, nlayer * 32], F32, kind="ExternalInput").ap()
    nf_d = nc.dram_tensor("normf", [128, 1024], F32, kind="ExternalInput").ap()
    dbg16_d = nc.dram_tensor("dbg16", [128, 4096], BF16, kind="ExternalOutput").ap() if dbg else None
    dbg32_d = nc.dram_tensor("dbg32", [128, 2048], F32, kind="ExternalOutput").ap() if dbg else None
    wfm_b = nc.dram_tensor("wfm_b", [nlayer * NFM * 128, 1024], BF16, kind="Internal").ap()
    wtm_b = nc.dram_tensor("wtm_b", [nlayer * 128, 8 * NTM], BF16, kind="Internal").ap()
    wout_b = nc.dram_tensor("wout_b", [nlayer * 4 * 128, 2048], BF16, kind="Internal").ap()
    wup_b = nc.dram_tensor("wup_b", [nlayer * 8 * 128, 4096], BF16, kind="Internal").ap()
    wdn_b = nc.dram_tensor("wdn_b", [nlayer * 8 * 128, 4096], BF16, kind="Internal").ap()
    w1_b = nc.dram_tensor("w1_b", [nlayer * 128, 4096], BF16, kind="Internal").ap()
    w2_b = nc.dram_tensor("w2_b", [nlayer * 128, 192], BF16, kind="Internal").ap()

    import contextlib
    es = contextlib.ExitStack()
    with es:
        def sb(name, shape, dt):
            return es.enter_context(nc.sbuf_tensor(name, shape, dt))

        XS = sb("XS", [128, NT, D], F32)
        HT = sb("HT", [128, 8, S], BF16)
        CB = sb("CB", [128, NCB], BF16)
        CF = sb("CF", [128, NCF], F32)
        GT = sb("GT", [128, nlayer * 24], F32)
        LAMV = sb("LAMV", [128, nlayer * 128], F32)
        LAMS = sb("LAMS", [128, 4 * nlayer], F32)
        LTMP = sb("LTMP", [128, 64], F32)
        SINKT = sb("SINKT", [4, nlayer], F32)
        PET = sb("PET", [128, nlayer * 32], F32)
        SSQ = sb("SSQ", [128, 16], F32)
        RSTD = sb("RSTD", [128, 16], F32)
        WS = sb("WS", [128, 40960], BF16)
        PS = [es.enter_context(nc.psum_tensor("ps%d" % i, [128, 512], F32)) for i in range(8)]
        sems = [es.enter_context(nc.semaphore("s%d" % i)) for i in range(60)]
        block = es.enter_context(nc.Block())

        kb = KB(nc, sems)
        op = kb.op

        class Carver:
            def __init__(self):
                self.o = 0

            def take(self, nbf16):
                o = self.o
                self.o += nbf16
                assert self.o <= 40960, self.o
                return o

        def ws16(o, n):
            return WS[:, o:o + n]

        def ws32(o, n):
            return WS[:, o:o + 2 * n].bitcast(F32)

        def cbv(name, r0=0, r1=128, c0=0, c1=None):
            o, n = cboff[name]
            if c1 is None:
                c1 = n
            return CB[r0:r1, o + c0:o + c1]

        def cfv(name, r0=0, r1=128, c0=0, c1=None):
            o, n = cfoff[name]
            if c1 is None:
                c1 = n
            return CF[r0:r1, o + c0:o + c1]

        cv = Carver()
        o_fmo = cv.take(5 * S)
        o_vst = cv.take(16 * 256)
        o_wfm = cv.take(3 * 1024)
        o_wtm = cv.take(8 * 256)
        o_wgn = cv.take(1024)
        o_pt = cv.take(4 * 512)
        o_un = cv.take(6144)
        o_ost = cv.take(2 * 2 * 512)
        o_nt = cv.take(2048)
        o_mix = cv.take(2 * 2 * 512)
        o_wo = cv.take(2 * 1024)
        o_rd = cv.take(2 * 2 * 512)
        o_hn = cv.take(1024)
        o_junk = cv.take(1024)
        att_end = cv.o
        cv2 = Carver()
        o_ut = cv2.take(16 * 512)
        o_wup = cv2.take(2 * 8 * 512)
        o_wdn = cv2.take(4 * 8 * 512)
        o_rl = cv2.take(2 * 2 * 512)
        o_hn2 = cv2.take(1024)
        o_junk2 = cv2.take(1024)
        o_fin = cv2.take(2 * 1024)
        o_nf = cv2.take(2 * 1024)

        B_X = [kb.buf("x%d" % i) for i in range(NT)]
        B_HT = [kb.buf("ht%d" % i) for i in range(NQB)]
        B_PS = [kb.buf("ps%d" % i) for i in range(8)]
        B_CONST = kb.buf("const")
        B_SSQ = [kb.buf("ssq%d" % i) for i in range(4)]
        B_RSTD = [kb.buf("rstd%d" % i) for i in range(4)]
        B_HN = kb.buf("hn")
        B_JUNK = kb.buf("junk")
        B_FMO = [[kb.buf("fmo%d_%d" % (i, j)) for j in range(NQB)] for i in range(5)]
        B_VST = [kb.buf("vst%d" % i) for i in range(NT)]
        B_WFM = [kb.buf("wfm%d" % i) for i in range(3)]
        C_WFM = [kb.chan() for _ in range(3)]
        B_WTM = kb.buf("wtm")
        C_WTM = kb.chan()
        B_WGN = kb.buf("wgn")
        C_WGN = kb.chan()
        C_W2X = kb.chan()
        B_PT = [kb.buf("pt%d" % i) for i in range(4)]
        B_OST = kb.buf("ost")
        B_NT = kb.buf("nt")
        B_MIX = [kb.buf("mix%d" % i) for i in range(2)]
        B_WO = kb.buf("wo")
        C_WO = kb.chan()
        B_RD = [kb.buf("rd%d" % i) for i in range(2)]
        B_UN = [kb.buf("un%d" % i) for i in range(12)]
        B_UT = kb.buf("ut")
        B_WUP = [kb.buf("wup%d" % i) for i in range(2)]
        C_WUP = [kb.chan() for _ in range(2)]
        B_WDN = [kb.buf("wdn%d" % i) for i in range(4)]
        C_WDN = [kb.chan() for _ in range(4)]
        B_RL = [kb.buf("rl%d" % i) for i in range(2)]
        B_FIN = [kb.buf("fin%d" % i) for i in range(2)]
        C_FIN = [kb.chan() for _ in range(2)]
        B_NF = kb.buf("nf")
        C_X = [kb.chan() for _ in range(4)]
        C_MISC = kb.chan()
        C_CAST = kb.chan()
        B_LAM = kb.buf("lam")
        B_SINK = kb.buf("sink")

        ring = {"z": 0, "o": 0, "m": 0, "pt": 0, "wfm": 0, "rd": 0, "mix": 0, "rl": 0}
        ZB, OB, MB = (0, 1, 2), (3, 4), (5, 6, 7)

        def nxt(kind, n):
            v = ring[kind]
            ring[kind] = (v + 1) % n
            return v

        def zbank():
            return ZB[nxt("z", 3)]

        def obank():
            return OB[nxt("o", 2)]

        def mbank():
            return MB[nxt("m", 3)]

        def mm(out, lhsT, rhs, start, stop, reads, writes):
            op("pe", lambda e: e.matmul(out, lhsT=lhsT, rhs=rhs, start=start, stop=stop), reads=reads, writes=writes)

        def dma(chan, out, in_, reads=(), writes=(), stream="sp"):
            op(stream, lambda e: e.dma_start(out=out, in_=in_), reads=reads, writes=writes, dma=chan)

        IDENT = cbv("ident")
        ONESB = cbv("ones")
        C_DBG = kb.chan()
        dbg_state = {}

        def tap16(name, src, reads, c0=0):
            if dbg and dbg == name and name not in dbg_state:
                dbg_state[name] = 1
                n = src.shape[-1]
                dma(C_DBG, dbg16_d[0:src.shape[0], c0:c0 + n], src, reads=reads)

        def tap32(name, src, reads, c0=0):
            if dbg and dbg == name and name not in dbg_state:
                dbg_state[name] = 1
                n = src.shape[-1]
                dma(C_DBG, dbg32_d[0:src.shape[0], c0:c0 + n], src, reads=reads)

        dma(C_MISC, CB[:], cb_d[:, :], writes=[B_CONST])
        dma(C_MISC, CF[:], cf_d[:, :], writes=[B_CONST])
        dma(C_MISC, GT[:], gt_d[:, :], writes=[B_CONST])
        dma(C_MISC, LAMV[:], lam_d[:, :], writes=[B_CONST])
        dma(C_MISC, SINKT[:], sink_d[:, :], writes=[B_CONST])
        dma(C_MISC, PET[:], pe_d[:, :], writes=[B_CONST])

        B_WB = kb.buf("wcast")

        def cast_dma(o_, i_):
            if C_CAST.count >= 64:
                v = C_CAST.count - 48
                kb.prog["pool"].append(lambda e, v=v: e.wait_ge(C_CAST.sem, v))
            dma(C_CAST, o_, i_, writes=[B_WB], stream="pool")

        def cast_all(src, dst):
            rows, cols = src.shape
            step = 128
            for r0 in range(0, rows, step):
                r1 = min(rows, r0 + step)
                if cols <= 2048:
                    cast_dma(dst[r0:r1, :], src[r0:r1, :])
                else:
                    for c0 in range(0, cols, 2048):
                        c1 = min(cols, c0 + 2048)
                        cast_dma(dst[r0:r1, c0:c1], src[r0:r1, c0:c1])

        for (a, b) in ((wfm_d, wfm_b), (wtm_d, wtm_b), (wout_d, wout_b), (w1_d, w1_b), (w2_d, w2_b),
                       (wup_d, wup_b), (wdn_d, wdn_b)):
            cast_all(a, b)

        for l in range(nlayer):
            lam_init = 0.8 - 0.6 * math.exp(-0.3 * l)
            lv = LAMV[:, l * 128:(l + 1) * 128]
            op("dve", lambda e, lv=lv: e.tensor_tensor(out=LTMP[:, 0:32], in0=lv[:, 0:32], in1=lv[:, 32:64], op=ALU.mult),
               reads=[B_CONST], writes=[B_LAM])
            op("dve", lambda e, l=l: e.tensor_reduce(out=LAMS[:, 4 * l + 2:4 * l + 3], in_=LTMP[:, 0:32], axis=AX.X, op=ALU.add),
               reads=[B_LAM], writes=[B_LAM])
            op("dve", lambda e, lv=lv: e.tensor_tensor(out=LTMP[:, 32:64], in0=lv[:, 64:96], in1=lv[:, 96:128], op=ALU.mult),
               reads=[B_CONST, B_LAM], writes=[B_LAM])
            op("dve", lambda e, l=l: e.tensor_reduce(out=LAMS[:, 4 * l + 3:4 * l + 4], in_=LTMP[:, 32:64], axis=AX.X, op=ALU.add),
               reads=[B_LAM], writes=[B_LAM])
            op("act", lambda e, l=l: e.activation(out=LAMS[:, 4 * l + 2:4 * l + 4], in_=LAMS[:, 4 * l + 2:4 * l + 4], func=AF.Exp),
               reads=[B_LAM], writes=[B_LAM])
            op("dve", lambda e, l=l: e.tensor_tensor(out=LAMS[:, 4 * l:4 * l + 1], in0=LAMS[:, 4 * l + 2:4 * l + 3],
                                                      in1=LAMS[:, 4 * l + 3:4 * l + 4], op=ALU.subtract),
               reads=[B_LAM], writes=[B_LAM])
            op("dve", lambda e, l=l, li=lam_init: e.tensor_scalar(out=LAMS[:, 4 * l + 1:4 * l + 2], in0=LAMS[:, 4 * l:4 * l + 1],
                                                                   scalar1=li, scalar2=-1.0, op0=ALU.add, op1=ALU.mult),
               reads=[B_LAM], writes=[B_LAM])
        kb.barrier()

        def norm_to_ht(g0, o_hn_, o_junk_):
            hn = ws16(o_hn_, 1024)
            junk = ws16(o_junk_, 1024)
            for tb in range(NQB):
                for j in range(4):
                    tt = 4 * tb + j
                    op("act", lambda e, tt=tt: e.activation(out=junk, in_=XS[:, tt, :], func=AF.Square,
                                                            accum_out=SSQ[:, tt:tt + 1]),
                       reads=[B_X[tt]], writes=[B_JUNK, B_SSQ[tb]])
                sl = slice(4 * tb, 4 * tb + 4)
                op("act", lambda e, sl=sl: e.activation(out=RSTD[:, sl], in_=SSQ[:, sl], func=AF.Ln,
                                                        scale=1.0 / D, bias=cfv("epsc")),
                   reads=[B_SSQ[tb], B_CONST], writes=[B_RSTD[tb]])
                op("act", lambda e, sl=sl: e.activation(out=RSTD[:, sl], in_=RSTD[:, sl], func=AF.Exp, scale=-0.5),
                   reads=[B_RSTD[tb]], writes=[B_RSTD[tb]])
                for j in range(4):
                    tt = 4 * tb + j
                    op("dve", lambda e, tt=tt: e.tensor_scalar(out=hn, in0=XS[:, tt, :], scalar1=RSTD[:, tt:tt + 1],
                                                               scalar2=None, op0=ALU.mult),
                       reads=[B_X[tt], B_RSTD[tb]], writes=[B_HN])
                    m = mbank()
                    psb = PS[m][:].bitcast(BF16)
                    for c in range(8):
                        op("pe", lambda e, c=c, psb=psb: e.transpose(out=psb[:, c * 128:(c + 1) * 128],
                                                                     in_=hn[:, c * 128:(c + 1) * 128], identity=IDENT),
                           reads=[B_HN, B_CONST], writes=[B_PS[m]])
                    gain = GT[:, g0:g0 + 8].unsqueeze(2).broadcast_to([128, 8, 128])
                    op("dve", lambda e, tt=tt, psb=psb, gain=gain: e.tensor_tensor(
                        out=HT[:, :, tt * 128:(tt + 1) * 128],
                        in0=psb.rearrange("p (c n) -> p c n", c=8), in1=gain, op=ALU.mult),
                       reads=[B_PS[m], B_CONST], writes=[B_HT[tb]])

        evac_flip = [0]

        def evac_copy(out, in_, reads, writes):
            evac_flip[0] ^= 1
            if evac_flip[0]:
                op("dve", lambda e: e.tensor_copy(out=out, in_=in_), reads=reads, writes=writes)
            else:
                op("act", lambda e: e.activation(out=out, in_=in_, func=AF.Copy), reads=reads, writes=writes)

        def load_wfm(l, chunk):
            s = nxt("wfm", 3)
            w = ws16(o_wfm + s * 1024, 1024).rearrange("p (k n) -> p k n", k=8)
            r0 = (l * NFM + chunk) * 128
            dma(C_WFM[s], w, wfm_b[r0:r0 + 128, :].rearrange("p (k n) -> p k n", k=8), writes=[B_WFM[s]])
            return w, B_WFM[s]

        def fmo(slot):
            return ws16(o_fmo + slot * S, S)

        def proj_fm(l, chunk, slot, M=128):
            w, bw = load_wfm(l, chunk)
            for tb in range(NQB):
                m = mbank()
                for kc in range(8):
                    mm(PS[m][0:M, :], w[:, kc, 0:M], HT[:, kc, tb * 512:(tb + 1) * 512], kc == 0, kc == 7,
                       reads=[bw, B_HT[tb]], writes=[B_PS[m]])
                evac_copy(fmo(slot)[0:M, tb * 512:(tb + 1) * 512], PS[m][0:M, :], reads=[B_PS[m]], writes=[B_FMO[slot][tb]])

        def vst(tt):
            return WS[:, o_vst + tt * 256:o_vst + (tt + 1) * 256]

        def proj_tm(l, c0, ncols, evac):
            w = ws16(o_wtm, 8 * 256).rearrange("p (k n) -> p k n", k=8)
            src = wtm_b[l * 128:(l + 1) * 128, :].rearrange("p (k n) -> p k n", k=8)
            dma(C_WTM, w[:, :, 0:ncols], src[:, :, c0:c0 + ncols], writes=[B_WTM])
            for tt in range(NT):
                m = mbank()
                for kc in range(8):
                    mm(PS[m][:, 0:ncols], HT[:, kc, tt * 128:(tt + 1) * 128], w[:, kc, 0:ncols], kc == 0, kc == 7,
                       reads=[B_WTM, B_HT[tt // 4]], writes=[B_PS[m]])
                evac(tt, PS[m][:, 0:ncols], B_PS[m])

        def vones():
            v = WS[:, o_vst:o_vst + 16 * 256].rearrange("p (t b c) -> p t b c", t=16, b=2)
            for b in range(2):
                op("pool", lambda e, b=b: e.memset(v[:, :, b, 64:128], 1.0), writes=B_VST)

        def tiles_for(qb, wt):
            res = []
            lo = 0 if wt is None else max(0, 4 * qb - wt)
            for kt in range(lo, 4 * qb + 4):
                jlo, jhi, dj, aj = None, None, None, None
                for j in range(4):
                    delta = 4 * qb + j - kt
                    if delta < 0 or (wt is not None and delta > wt):
                        continue
                    if jlo is None:
                        jlo = j
                    jhi = j
                    if delta == 0:
                        dj = j
                    if wt is not None and delta == wt:
                        aj = j
                res.append((kt, jlo, jhi, dj, aj))
            return res

        def softmax_map(qb, tiles, qslot, qbase, k_lhsT, k_reads, ah, scale, v_lhsT, v_reads, ob, extra=None,
                        diag_name="mdiag"):
            q = fmo(qslot)[qbase:qbase + 64, qb * 512:(qb + 1) * 512]
            rowsel = cbv("rowsel", 0, 12, ah * 128, (ah + 1) * 128)
            rrow = cbv("rrow", 0, 12)
            n = len(tiles)
            for idx, (kt, jlo, jhi, dj, aj) in enumerate(tiles):
                c0, c1 = jlo * 128, (jhi + 1) * 128
                z = zbank()
                zt = PS[z]
                steps = [(zt[:, c0:c1], k_lhsT(kt), q[:, c0:c1], [B_FMO[qslot][qb]] + k_reads(kt)),
                         (zt[:, c0:c1], rowsel, rrow[:, c0:c1], [B_CONST])]
                if extra is not None:
                    steps += extra(kt, zt, c0, c1)
                if dj is not None:
                    steps.append((zt[:, dj * 128:(dj + 1) * 128], IDENT, cbv(diag_name), [B_CONST]))
                if aj is not None:
                    steps.append((zt[:, aj * 128:(aj + 1) * 128], IDENT, cbv("manti"), [B_CONST]))
                for si, (o_, l_, r_, rd_) in enumerate(steps):
                    mm(o_, l_, r_, si == 0, si == len(steps) - 1, reads=rd_, writes=[B_PS[z]])
                p = nxt("pt", 4)
                pt = ws16(o_pt + p * 512, 512)
                rel = kt - 4 * qb + 15
                bias = cfv("biasp", 0, 128, ah * 19 + rel, ah * 19 + rel + 1)
                op("act", lambda e, pt=pt, zt=zt, c0=c0, c1=c1, bias=bias: e.activation(
                    out=pt[:, c0:c1], in_=zt[:, c0:c1], func=AF.Exp, scale=scale, bias=bias),
                   reads=[B_PS[z], B_CONST], writes=[B_PT[p]])
                tap16("pt", pt, [B_PT[p]], 0)
                mm(PS[ob][:, c0:c1], v_lhsT(kt), pt[:, c0:c1], idx == 0, idx == n - 1,
                   reads=[B_PT[p]] + v_reads(kt), writes=[B_PS[ob]])

        def ost():
            return ws32(o_ost, 1024).rearrange("p (c n) -> p c n", c=2)

        def normalize_to_ost(ob, c, half, first=True, wmul=None):
            r = nxt("rd", 2)
            rd = ws32(o_rd + r * 1024, 512)
            op("dve", lambda e: e.reciprocal(out=rd[0:64, :], in_=PS[ob][64:128, :]), reads=[B_PS[ob]], writes=[B_RD[r]])
            dst = ost()[half * 64:(half + 1) * 64, c, :]
            op("dve", lambda e: e.tensor_tensor(out=dst, in0=PS[ob][0:64, :], in1=rd[0:64, :], op=ALU.mult),
               reads=[B_PS[ob], B_RD[r]], writes=[B_OST])

        def group_norm_and_project(l, g, qb, headwise=False, post=1.0):
            o2 = ost()
            sq = ws16(o_nt, 1024).rearrange("p (c n) -> p c n", c=2)
            rs = ws32(o_nt + 1024, 512)
            for c in range(2):
                op("act", lambda e, c=c: e.activation(out=sq[:, c, :], in_=o2[:, c, :], func=AF.Square),
                   reads=[B_OST], writes=[B_NT])
            gm = GT[:, l * 24 + 16 + 2 * g:l * 24 + 16 + 2 * g + 2]
            mi = nxt("mix", 2)
            mix = ws16(o_mix + mi * 1024, 1024).rearrange("p (c n) -> p c n", c=2)
            nfeat = 64.0 if headwise else 256.0
            if not headwise:
                m = mbank()
                for c in range(2):
                    mm(PS[m][:, :], ONESB, sq[:, c, :], c == 0, c == 1, reads=[B_NT, B_CONST], writes=[B_PS[m]])
                tap16("sq", ws16(o_nt, 1024), [B_NT], 0)
                if dbg == "ssq" and "ssq" not in dbg_state:
                    tmpd = ws32(o_un + 1024, 512)
                    op("dve", lambda e, m=m: e.tensor_copy(out=tmpd, in_=PS[m][:, :]), reads=[B_PS[m]], writes=[B_UN[2]])
                    tap32("ssq", tmpd, [B_UN[2]], 0)
                op("act", lambda e: e.activation(out=rs, in_=PS[m][:, :], func=AF.Ln, scale=1.0 / nfeat, bias=cfv("epsc")),
                   reads=[B_PS[m], B_CONST], writes=[B_NT])
                if dbg == "lnv":
                    tap32("lnv", rs, [B_NT], 0)
                op("act", lambda e: e.activation(out=rs, in_=rs, func=AF.Exp, scale=-0.5), reads=[B_NT], writes=[B_NT])
                for c in range(2):
                    op("dve", lambda e, c=c: e.scalar_tensor_tensor(out=mix[:, c, :], in0=o2[:, c, :], scalar=gm[:, c:c + 1],
                                                                    in1=rs, op0=ALU.mult, op1=ALU.mult),
                       reads=[B_OST, B_NT, B_CONST], writes=[B_MIX[mi]])
            else:
                for c in range(2):
                    m = mbank()
                    mm(PS[m][:, :], cbv("blockdiag"), sq[:, c, :], True, True, reads=[B_NT, B_CONST], writes=[B_PS[m]])
                    op("act", lambda e, m=m: e.activation(out=rs, in_=PS[m][:, :], func=AF.Ln, scale=1.0 / nfeat,
                                                          bias=cfv("epsc")),
                       reads=[B_PS[m], B_CONST], writes=[B_NT])
                    op("act", lambda e: e.activation(out=rs, in_=rs, func=AF.Exp, scale=-0.5), reads=[B_NT], writes=[B_NT])
                    op("dve", lambda e: e.tensor_scalar(out=rs, in0=rs, scalar1=post, scalar2=None, op0=ALU.mult),
                       reads=[B_NT], writes=[B_NT])
                    op("dve", lambda e, c=c: e.scalar_tensor_tensor(out=mix[:, c, :], in0=o2[:, c, :], scalar=gm[:, c:c + 1],
                                                                    in1=rs, op0=ALU.mult, op1=ALU.mult),
                       reads=[B_OST, B_NT, B_CONST], writes=[B_MIX[mi]])
            tap32("rs", rs, [B_NT], 0)
            tap16("mix", ws16(o_mix + mi * 1024, 1024), [B_MIX[mi]], 0)
            tap16("wo", ws16(o_wo, 2048), [B_WO], 0)
            wo = ws16(o_wo, 2048).rearrange("p (c n) -> p c n", c=2)
            for j in range(4):
                tt = 4 * qb + j
                for cbk in range(2):
                    m = mbank()
                    for c in range(2):
                        mm(PS[m][:, :], mix[:, c, j * 128:(j + 1) * 128], wo[:, c, cbk * 512:(cbk + 1) * 512], c == 0, c == 1,
                           reads=[B_MIX[mi], B_WO], writes=[B_PS[m]])
                    op("dve", lambda e, tt=tt, cbk=cbk, m=m: e.tensor_tensor(
                        out=XS[:, tt, cbk * 512:(cbk + 1) * 512], in0=PS[m][:, :], in1=XS[:, tt, cbk * 512:(cbk + 1) * 512],
                        op=ALU.add), reads=[B_PS[m], B_X[tt]], writes=[B_X[tt]])

        def load_wout(l, g):
            wo = ws16(o_wo, 2048).rearrange("p (c n) -> p c n", c=2)
            r0 = (l * 4 + g) * 128
            dma(C_WO, wo, wout_b[r0:r0 + 128, :].rearrange("p (c n) -> p c n", c=2), writes=[B_WO])

        def mixer_swa(l):
            load_wout(l, 3)
            vones()

            def evac_v(tt, ps, bps):
                v = vst(tt).rearrange("p (b c) -> p b c", b=2)
                evac_copy(v[:, :, 0:64], ps.rearrange("p (b c) -> p b c", b=2), reads=[bps], writes=[B_VST[tt]])
            proj_tm(l, TM_VD[0], 128, evac_v)
            proj_fm(l, QD0, 0)
            proj_fm(l, QD1, 1)
            proj_fm(l, KD0, 2)
            proj_fm(l, KD1, 3)
            tap16("fmo", fmo(0), [B_FMO[0][i] for i in range(4)], 0)
            dbg_state.pop("fmo", None)
            tap16("fmo", fmo(2), [B_FMO[2][i] for i in range(4)], 2048)
            tap16("vst", WS[:, o_vst:o_vst + 4096], B_VST, 0)
            sinkrow = ws16(o_un, 512)
            op("act", lambda e: e.activation(out=sinkrow[0:4, :], in_=cfv("epsrow", 0, 4), func=AF.Exp,
                                             bias=SINKT[0:4, l:l + 1]),
               reads=[B_CONST], writes=[B_UN[0]])
            for qb in range(NQB):
                tiles = tiles_for(qb, 1)
                for h in range(4):
                    pair, half = h // 2, h % 2
                    kslot = 2 + pair
                    ob = obank()

                    def k_lhsT(kt, kslot=kslot, half=half):
                        return fmo(kslot)[half * 64:(half + 1) * 64, kt * 128:(kt + 1) * 128]

                    def k_reads(kt, kslot=kslot):
                        return [B_FMO[kslot][kt // 4]]

                    def v_lhsT(kt, pair=pair):
                        return vst(kt)[:, pair * 128:(pair + 1) * 128]

                    def v_reads(kt):
                        return [B_VST[kt]]
                    softmax_map(qb, tiles, pair, half * 64, k_lhsT, k_reads, 8 + h, SC64, v_lhsT, v_reads, ob)
                    mm(PS[ob][:, :], cbv("sinksel", 0, 4, h * 128, (h + 1) * 128), sinkrow[0:4, :], False, True,
                       reads=[B_UN[0], B_CONST], writes=[B_PS[ob]])
                    if dbg == "psob" and "psob" not in dbg_state:
                        tmpd = ws32(o_un + 1024, 512)
                        op("dve", lambda e, ob=ob: e.tensor_copy(out=tmpd, in_=PS[ob][:, :]), reads=[B_PS[ob]], writes=[B_UN[2]])
                        tap32("psob", tmpd, [B_UN[2]], 0)
                    normalize_to_ost(ob, pair, half)
                tap32("ost", ws32(o_ost, 1024), [B_OST], 0)
                group_norm_and_project(l, 3, qb)

        def mixer_diff2(l):
            lam_init = 0.8 - 0.6 * math.exp(-0.3 * l)
            post = 1.0 - lam_init
            neglam = LAMS[:, 4 * l + 1:4 * l + 2]
            for pair in range(2):
                wo = ws16(o_wo, 2048).rearrange("p (c n) -> p c n", c=2)
                r0 = (l * 4 + 2) * 128
                src = wout_b[r0:r0 + 128, :].rearrange("p (c n) -> p c n", c=2)
                dma(C_WO, wo[:, 0, :], src[:, pair, :], writes=[B_WO])
                vones()

                def evac_v(tt, ps, bps):
                    v = vst(tt).rearrange("p (b c) -> p b c", b=2)
                    evac_copy(v[:, :, 0:64], ps.rearrange("p (b c) -> p b c", b=2), reads=[bps], writes=[B_VST[tt]])
                proj_tm(l, TM_VC[0] + pair * 128, 128, evac_v)
                proj_fm(l, QC0 + pair, 0)
                proj_fm(l, KC00 + 2 * pair, 1)
                proj_fm(l, KC10 + 2 * pair, 2)
                for qb in range(NQB):
                    tiles = tiles_for(qb, None)
                    for half in range(2):
                        h = 2 * pair + half
                        obs = []
                        for cmap in range(2):
                            ob = obank()
                            obs.append(ob)
                            kslot = 1 + cmap

                            def k_lhsT(kt, kslot=kslot, half=half):
                                return fmo(kslot)[half * 64:(half + 1) * 64, kt * 128:(kt + 1) * 128]

                            def k_reads(kt, kslot=kslot):
                                return [B_FMO[kslot][kt // 4]]

                            def v_lhsT(kt, half=half):
                                return vst(kt)[:, half * 128:(half + 1) * 128]

                            def v_reads(kt):
                                return [B_VST[kt]]
                            softmax_map(qb, tiles, 0, half * 64, k_lhsT, k_reads, 4 + h, SC32, v_lhsT, v_reads, ob)
                        t0 = ws32(o_un, 512)
                        t1 = ws32(o_un + 1024, 512)
                        rds = []
                        for i in range(2):
                            r = nxt("rd", 2)
                            rd = ws32(o_rd + r * 1024, 512)
                            rds.append((r, rd))
                            op("dve", lambda e, rd=rd, ob=obs[i]: e.reciprocal(out=rd[0:64, :], in_=PS[ob][64:128, :]),
                               reads=[B_PS[obs[i]]], writes=[B_RD[r]])
                        op("dve", lambda e, obs=obs, rds=rds: e.tensor_tensor(out=t0[0:64, :], in0=PS[obs[0]][0:64, :],
                                                                              in1=rds[0][1][0:64, :], op=ALU.mult),
                           reads=[B_PS[obs[0]], B_RD[rds[0][0]]], writes=[B_UN[0]])
                        op("dve", lambda e, obs=obs, rds=rds: e.tensor_tensor(out=t1[0:64, :], in0=PS[obs[1]][0:64, :],
                                                                              in1=rds[1][1][0:64, :], op=ALU.mult),
                           reads=[B_PS[obs[1]], B_RD[rds[1][0]]], writes=[B_UN[1]])
                        dst = ost()[half * 64:(half + 1) * 64, 0, :]
                        op("dve", lambda e, dst=dst: e.scalar_tensor_tensor(out=dst, in0=t1[0:64, :], scalar=neglam[0:64, :],
                                                                            in1=t0[0:64, :], op0=ALU.mult, op1=ALU.add),
                           reads=[B_UN[0], B_UN[1], B_LAM], writes=[B_OST])
                    o2 = ost()
                    sq = ws16(o_nt, 1024).rearrange("p (c n) -> p c n", c=2)
                    rs = ws32(o_nt + 1024, 512)
                    op("act", lambda e: e.activation(out=sq[:, 0, :], in_=o2[:, 0, :], func=AF.Square),
                       reads=[B_OST], writes=[B_NT])
                    m = mbank()
                    mm(PS[m][:, :], cbv("blockdiag"), sq[:, 0, :], True, True, reads=[B_NT, B_CONST], writes=[B_PS[m]])
                    op("act", lambda e, m=m: e.activation(out=rs, in_=PS[m][:, :], func=AF.Ln, scale=1.0 / 64.0,
                                                          bias=cfv("epsc")),
                       reads=[B_PS[m], B_CONST], writes=[B_NT])
                    op("act", lambda e: e.activation(out=rs, in_=rs, func=AF.Exp, scale=-0.5), reads=[B_NT], writes=[B_NT])
                    op("dve", lambda e: e.tensor_scalar(out=rs, in0=rs, scalar1=post, scalar2=None, op0=ALU.mult),
                       reads=[B_NT], writes=[B_NT])
                    gm = GT[:, l * 24 + 16 + 4 + pair:l * 24 + 16 + 4 + pair + 1]
                    mi = nxt("mix", 2)
                    mix = ws16(o_mix + mi * 1024, 1024).rearrange("p (c n) -> p c n", c=2)
                    op("dve", lambda e, mix=mix, gm=gm: e.scalar_tensor_tensor(out=mix[:, 0, :], in0=o2[:, 0, :], scalar=gm,
                                                                                 in1=rs, op0=ALU.mult, op1=ALU.mult),
                       reads=[B_OST, B_NT, B_CONST], writes=[B_MIX[mi]])
                    for j in range(4):
                        tt = 4 * qb + j
                        for cbk in range(2):
                            m = mbank()
                            mm(PS[m][:, :], mix[:, 0, j * 128:(j + 1) * 128], wo[:, 0, cbk * 512:(cbk + 1) * 512], True, True,
                               reads=[B_MIX[mi], B_WO], writes=[B_PS[m]])
                            op("dve", lambda e, tt=tt, cbk=cbk, m=m: e.tensor_tensor(
                                out=XS[:, tt, cbk * 512:(cbk + 1) * 512], in0=PS[m][:, :],
                                in1=XS[:, tt, cbk * 512:(cbk + 1) * 512], op=ALU.add),
                               reads=[B_PS[m], B_X[tt]], writes=[B_X[tt]])
                kb.barrier()

        def mixer_sb(l):
            load_wout(l, 0)

            def evac_v(tt, ps, bps):
                evac_copy(vst(tt), ps, reads=[bps], writes=[B_VST[tt]])
            proj_tm(l, TM_VA[0], 256, evac_v)
            proj_fm(l, QA0, 0)
            proj_fm(l, QA1, 1)
            proj_fm(l, KA0, 2)
            proj_fm(l, KA1, 3)
            U8 = cbv("u8")
            O8 = cbv("ones8")
            for qb in range(NQB):
                for pair in range(2):
                    obs = [obank(), obank()]
                    R32 = [ws32(o_un + i * 1024, 512) for i in range(2)]
                    RB = [ws16(o_un + 2048 + i * 512, 512) for i in range(2)]
                    E32 = [ws32(o_un + 3072 + i * 1024, 512) for i in range(2)]
                    bR32 = [B_UN[0], B_UN[1]]
                    bRB = [B_UN[2], B_UN[3]]
                    bE = [B_UN[4], B_UN[5]]
                    for half in range(2):
                        op("pool", lambda e, half=half: e.memset(R32[half], 0.0), writes=[bR32[half]])
                    kts = list(range(4 * qb + 3, -1, -1))
                    for idx, kt in enumerate(kts):
                        jlo = max(0, kt - 4 * qb)
                        dj = jlo if kt >= 4 * qb else None
                        c0, c1 = jlo * 128, 512
                        for half in range(2):
                            z = zbank()
                            zt = PS[z]
                            q = fmo(pair)[half * 64:(half + 1) * 64, qb * 512:(qb + 1) * 512]
                            k = fmo(2 + pair)[half * 64:(half + 1) * 64, kt * 128:(kt + 1) * 128]
                            mm(zt[:, c0:c1], k, q[:, c0:c1], True, False, reads=[B_FMO[pair][qb], B_FMO[2 + pair][kt // 4]],
                               writes=[B_PS[z]])
                            if dj is not None:
                                mm(zt[:, dj * 128:(dj + 1) * 128], IDENT, cbv("mstrict"), False, False, reads=[B_CONST],
                                   writes=[B_PS[z]])
                            e32 = E32[half]
                            op("act", lambda e, e32=e32, zt=zt, c0=c0, c1=c1: e.activation(
                                out=e32[:, c0:c1], in_=zt[:, c0:c1], func=AF.Exp, scale=SC64),
                               reads=[B_PS[z]], writes=[bE[half]])
                            p = nxt("pt", 4)
                            sp = ws16(o_pt + p * 512, 512)
                            op("act", lambda e, e32=e32, sp=sp, c0=c0, c1=c1: e.activation(
                                out=sp[:, c0:c1], in_=e32[:, c0:c1], func=AF.Ln, bias=cfv("one")),
                               reads=[bE[half], B_CONST], writes=[B_PT[p]])
                            last_u = (idx == 0)
                            mm(zt[:, c0:c1], U8, sp[:, c0:c1], False, last_u, reads=[B_PT[p], B_CONST], writes=[B_PS[z]])
                            if idx > 0:
                                mm(zt[:, :], O8, RB[half], False, True, reads=[bRB[half], B_CONST], writes=[B_PS[z]])
                            p2 = nxt("pt", 4)
                            at = ws16(o_pt + p2 * 512, 512)
                            op("act", lambda e, at=at, zt=zt, c0=c0, c1=c1: e.activation(
                                out=at[:, c0:c1], in_=zt[:, c0:c1], func=AF.Exp, scale=SC64),
                               reads=[B_PS[z]], writes=[B_PT[p2]])
                            mm(PS[obs[half]][:, c0:c1], vst(kt)[:, pair * 128:(pair + 1) * 128], at[:, c0:c1], idx == 0,
                               idx == len(kts) - 1, reads=[B_PT[p2], B_VST[kt]], writes=[B_PS[obs[half]]])
                            if idx < len(kts) - 1:
                                op("pool", lambda e, half=half, sp=sp, c0=c0, c1=c1: e.tensor_tensor(
                                    out=R32[half][:, c0:c1], in0=R32[half][:, c0:c1], in1=sp[:, c0:c1], op=ALU.add),
                                   reads=[B_PT[p], bR32[half]], writes=[bR32[half]])
                                op("pool", lambda e, half=half: e.tensor_copy(out=RB[half], in_=R32[half]),
                                   reads=[bR32[half]], writes=[bRB[half]])
                    for half in range(2):
                        dst = ost()[half * 64:(half + 1) * 64, pair, :]
                        op("dve", lambda e, dst=dst, half=half, obs=obs: e.tensor_copy(
                            out=dst, in_=PS[obs[half]][half * 64:(half + 1) * 64, :]),
                           reads=[B_PS[obs[half]]], writes=[B_OST])
                group_norm_and_project(l, 0, qb)

        def mixer_nsa(l):
            load_wout(l, 1)
            vones()

            def evac_v(tt, ps, bps):
                v = vst(tt).rearrange("p (b c) -> p b c", b=2)
                evac_copy(v[:, :, 0:64], ps.rearrange("p (b c) -> p b c", b=2), reads=[bps], writes=[B_VST[tt]])
            proj_tm(l, TM_VS[0], 128, evac_v)
            proj_fm(l, QN0, 0)
            proj_fm(l, QN1, 1)
            proj_fm(l, KVC, 2)
            wgn = ws16(o_wgn, 1024).rearrange("p (k n) -> p k n", k=8)
            r0 = (l * NFM + GN) * 128
            dma(C_WGN, wgn, wfm_b[r0:r0 + 128, :].rearrange("p (k n) -> p k n", k=8), writes=[B_WGN])
            blk = ws16(o_un, 1016).rearrange("p (a i) -> p a i", a=8)
            gel = [ws16(o_un + 1024, 128), ws16(o_un + 1152, 128)]
            KC = ws16(o_un + 1280, 128)
            VC = ws16(o_un + 1408, 128)
            w2 = ws16(o_un + 3584, 192)
            C_W2 = C_W2X
            dma(C_W2, w2, w2_b[l * 128:(l + 1) * 128, :], writes=[B_UN[7]])
            hb = [mbank(), mbank()]
            kvc = fmo(2)
            for quarter in range(4):
                s = nxt("wfm", 3)
                w1q = ws16(o_wfm + s * 1024, 1024).rearrange("p (a n) -> p a n", a=8)
                dma(C_WFM[s], w1q, w1_b[l * 128:(l + 1) * 128, quarter * 1024:(quarter + 1) * 1024].rearrange(
                    "p (a n) -> p a n", a=8), writes=[B_WFM[s]])
                src = _ap(kvc, kvc.offset + 8 * quarter, [kvc.ap[0], [1, 8], [16, 127]])
                pe = PET[:, l * 32 + 8 * quarter:l * 32 + 8 * quarter + 8].unsqueeze(2).broadcast_to([128, 8, 127])
                op("dve", lambda e, src=src, pe=pe: e.tensor_tensor(out=blk, in0=src, in1=pe, op=ALU.add),
                   reads=[B_FMO[2][0], B_FMO[2][1], B_FMO[2][2], B_FMO[2][3], B_CONST], writes=[B_UN[0]])
                for a in range(8):
                    pidx = quarter * 8 + a
                    for kv in range(2):
                        mm(PS[hb[kv]][:, 0:127], w1q[kv * 64:(kv + 1) * 64, a, :], blk[kv * 64:(kv + 1) * 64, a, :],
                           pidx == 0, pidx == 31, reads=[B_WFM[s], B_UN[0]], writes=[B_PS[hb[kv]]])
            tmpa = ws32(o_un + 3840, 512)
            for kv in range(2):
                xh = tmpa[:, 256:383]
                ta = tmpa[:, 0:127]
                tb_ = tmpa[:, 128:255]
                op("dve", lambda e, xh=xh, kv=kv: e.tensor_copy(out=xh, in_=PS[hb[kv]][:, 0:127]),
                   reads=[B_PS[hb[kv]]], writes=[B_UN[8]])
                op("dve", lambda e, xh=xh, ta=ta: e.tensor_tensor(out=ta, in0=xh, in1=xh, op=ALU.mult),
                   reads=[B_UN[8]], writes=[B_UN[8]])
                op("dve", lambda e, ta=ta: e.tensor_scalar(out=ta, in0=ta, scalar1=0.044715, scalar2=1.0, op0=ALU.mult, op1=ALU.add),
                   reads=[B_UN[8]], writes=[B_UN[8]])
                op("dve", lambda e, xh=xh, ta=ta: e.tensor_tensor(out=ta, in0=xh, in1=ta, op=ALU.mult),
                   reads=[B_UN[8]], writes=[B_UN[8]])
                op("act", lambda e, ta=ta, tb_=tb_: e.activation(out=tb_, in_=ta, func=AF.Exp, scale=-1.5957691216),
                   reads=[B_UN[8]], writes=[B_UN[9]])
                op("dve", lambda e, tb_=tb_: e.tensor_scalar(out=tb_, in0=tb_, scalar1=1.0, scalar2=None, op0=ALU.add),
                   reads=[B_UN[9]], writes=[B_UN[9]])
                op("dve", lambda e, tb_=tb_: e.reciprocal(out=tb_, in_=tb_), reads=[B_UN[9]], writes=[B_UN[9]])
                op("dve", lambda e, xh=xh, tb_=tb_, kv=kv: e.tensor_tensor(out=gel[kv][:, 0:127], in0=xh, in1=tb_, op=ALU.mult),
                   reads=[B_UN[8], B_UN[9]], writes=[B_UN[1 + kv]])
            m = mbank()
            mm(PS[m][:, 0:127], w2[:, 0:128], gel[0][:, 0:127], True, True, reads=[B_UN[7], B_UN[1]], writes=[B_PS[m]])
            op("dve", lambda e, m=m: e.tensor_copy(out=KC[:, 0:127], in_=PS[m][:, 0:127]), reads=[B_PS[m]], writes=[B_UN[3]])
            m = mbank()
            mm(PS[m][0:127, 0:64], gel[1][:, 0:127], w2[:, 128:192], True, True, reads=[B_UN[7], B_UN[2]], writes=[B_PS[m]])
            op("pool", lambda e: e.memset(VC[:, 64:128], 1.0), writes=[B_UN[4]])
            op("dve", lambda e, m=m: e.tensor_copy(out=VC[0:127, 0:64], in_=PS[m][0:127, 0:64]), reads=[B_PS[m], B_UN[4]],
               writes=[B_UN[4]])
            proj_fm(l, KSD, 2)
            proj_fm(l, KWD, 3)
            gate = ws16(o_un + 1536, 512)
            selbT = ws16(o_un + 2048, 512)
            imp = ws32(o_un + 2560, 512)
            for qb in range(NQB):
                m = mbank()
                for kc in range(8):
                    mm(PS[m][0:12, :], wgn[:, kc, 0:12], HT[:, kc, qb * 512:(qb + 1) * 512], kc == 0, kc == 7,
                       reads=[B_WGN, B_HT[qb]], writes=[B_PS[m]])
                tg = tmpa
                op("act", lambda e, m=m: e.activation(out=tg[0:12, :], in_=PS[m][0:12, :], func=AF.Exp, scale=-1.0),
                   reads=[B_PS[m]], writes=[B_UN[8]])
                op("dve", lambda e: e.tensor_scalar(out=tg[0:12, :], in0=tg[0:12, :], scalar1=1.0, scalar2=None, op0=ALU.add),
                   reads=[B_UN[8]], writes=[B_UN[8]])
                op("dve", lambda e: e.reciprocal(out=tg[0:12, :], in_=tg[0:12, :]), reads=[B_UN[8]], writes=[B_UN[8]])
                op("dve", lambda e: e.tensor_copy(out=gate[0:12, :], in_=tg[0:12, :]), reads=[B_UN[8]], writes=[B_UN[5]])
                nk = min(127, 32 * qb + 31)
                cmp_norm = []
                for h in range(4):
                    pair, half = h // 2, h % 2
                    z = zbank()
                    zt = PS[z]
                    q = fmo(pair)[half * 64:(half + 1) * 64, qb * 512:(qb + 1) * 512]
                    mm(zt[0:nk, :], KC[half * 64:(half + 1) * 64, 0:nk], q, True, False, reads=[B_FMO[pair][qb], B_UN[3]],
                       writes=[B_PS[z]])
                    mm(zt[0:nk, :], cbv("rowsel", 0, 12, h * 128, h * 128 + nk), cbv("rrow", 0, 12), False, False,
                       reads=[B_CONST], writes=[B_PS[z]])
                    mm(zt[0:nk, :], cbv("ident", 0, nk, 0, nk), cbv("mc", 0, nk, qb * 512, (qb + 1) * 512), False, True,
                       reads=[B_CONST], writes=[B_PS[z]])
                    p = nxt("pt", 4)
                    pt = ws16(o_pt + p * 512, 512)
                    bias = cfv("biasc", 0, nk, h * 4 + qb, h * 4 + qb + 1)
                    op("act", lambda e, pt=pt, zt=zt, bias=bias, nk=nk: e.activation(
                        out=pt[0:nk, :], in_=zt[0:nk, :], func=AF.Exp, scale=SC64, bias=bias),
                       reads=[B_PS[z], B_CONST], writes=[B_PT[p]])
                    ob = obank()
                    mm(PS[ob][:, :], VC[0:nk, :], pt[0:nk, :], True, True, reads=[B_PT[p], B_UN[4]], writes=[B_PS[ob]])
                    m = mbank()
                    mm(PS[m][0:64, :], cbv("gaug", 0, nk), pt[0:nk, :], True, True, reads=[B_PT[p], B_CONST], writes=[B_PS[m]])
                    r = nxt("rd", 2)
                    rd = ws32(o_rd + r * 1024, 512)
                    op("dve", lambda e, rd=rd, ob=ob: e.tensor_scalar(out=rd[0:64, :], in0=PS[ob][64:128, :], scalar1=1e-30,
                                                                      scalar2=None, op0=ALU.max),
                       reads=[B_PS[ob]], writes=[B_RD[r]])
                    op("dve", lambda e, rd=rd: e.reciprocal(out=rd[0:64, :], in_=rd[0:64, :]), reads=[B_RD[r]], writes=[B_RD[r]])
                    if h == 0:
                        op("dve", lambda e, rd=rd, m=m: e.tensor_tensor(out=imp[0:32, :], in0=PS[m][0:32, :], in1=rd[0:32, :],
                                                                        op=ALU.mult),
                           reads=[B_PS[m], B_RD[r]], writes=[B_UN[6]])
                    else:
                        tq_ = tmpa
                        op("dve", lambda e, rd=rd, m=m: e.tensor_tensor(out=tq_[0:32, :], in0=PS[m][0:32, :], in1=rd[0:32, :],
                                                                        op=ALU.mult),
                           reads=[B_PS[m], B_RD[r]], writes=[B_UN[8]])
                        op("dve", lambda e: e.tensor_tensor(out=imp[0:32, :], in0=imp[0:32, :], in1=tq_[0:32, :], op=ALU.add),
                           reads=[B_UN[8], B_UN[6]], writes=[B_UN[6]])
                    gm_ = mbank()
                    mm(PS[gm_][0:64, :], cbv("gatesel", 0, 12, (h * 3 + 0) * 64, (h * 3 + 1) * 64), gate[0:12, :], True, True,
                       reads=[B_UN[5], B_CONST], writes=[B_PS[gm_]])
                    wgt = ws32(o_un + 4864, 512)
                    op("dve", lambda e, rd=rd, wgt=wgt, gm_=gm_: e.tensor_tensor(out=wgt[0:64, :], in0=PS[gm_][0:64, :],
                                                                               in1=rd[0:64, :], op=ALU.mult),
                       reads=[B_PS[gm_], B_RD[r]], writes=[B_UN[9]])
                    dst = ost()[half * 64:(half + 1) * 64, pair, :]
                    op("dve", lambda e, dst=dst, ob=ob, wgt=wgt: e.tensor_tensor(out=dst, in0=PS[ob][0:64, :], in1=wgt[0:64, :],
                                                                               op=ALU.mult),
                       reads=[B_PS[ob], B_UN[9]], writes=[B_OST])
                for j in range(4):
                    tt = 4 * qb + j
                    m = mbank()
                    op("pe", lambda e, m=m, j=j: e.transpose(out=PS[m][:, 0:32], in_=imp[0:32, j * 128:(j + 1) * 128],
                                                             identity=cfv("identf", 0, 32, 0, 32)),
                       reads=[B_UN[6], B_CONST], writes=[B_PS[m]])
                    ta = tmpa[:, 0:32]
                    t8 = tmpa[:, 32:40]
                    tsel = tmpa[:, 64:96]
                    cand = cbv("cand", 0, 128, tt * 32, (tt + 1) * 32)
                    candm1 = cbv("candm1", 0, 128, tt * 32, (tt + 1) * 32)
                    forced = cbv("forced", 0, 128, tt * 32, (tt + 1) * 32)
                    op("dve", lambda e, m=m, cand=cand: e.tensor_tensor(out=ta, in0=PS[m][:, 0:32], in1=cand, op=ALU.mult),
                       reads=[B_PS[m], B_CONST], writes=[B_UN[8]])
                    op("dve", lambda e, candm1=candm1: e.tensor_tensor(out=ta, in0=ta, in1=candm1, op=ALU.add),
                       reads=[B_UN[8], B_CONST], writes=[B_UN[8]])
                    op("dve", lambda e: e.max(out=t8, in_=ta), reads=[B_UN[8]], writes=[B_UN[9]])
                    op("dve", lambda e: e.tensor_scalar(out=tsel, in0=ta, scalar1=t8[:, 4:5], scalar2=None, op0=ALU.is_ge),
                       reads=[B_UN[8], B_UN[9]], writes=[B_UN[10]])
                    op("dve", lambda e, cand=cand: e.tensor_tensor(out=tsel, in0=tsel, in1=cand, op=ALU.mult),
                       reads=[B_UN[10], B_CONST], writes=[B_UN[10]])
                    op("dve", lambda e, forced=forced: e.tensor_tensor(out=tsel, in0=tsel, in1=forced, op=ALU.add),
                       reads=[B_UN[10], B_CONST], writes=[B_UN[10]])
                    op("dve", lambda e: e.tensor_scalar(out=tsel, in0=tsel, scalar1=-1.0, scalar2=-NEG, op0=ALU.add, op1=ALU.mult),
                       reads=[B_UN[10]], writes=[B_UN[10]])
                    m2 = mbank()
                    op("pe", lambda e, m2=m2: e.transpose(out=PS[m2][0:32, 0:128], in_=tsel, identity=cfv("identf")),
                       reads=[B_UN[10], B_CONST], writes=[B_PS[m2]])
                    op("dve", lambda e, m2=m2, j=j: e.tensor_copy(out=selbT[0:32, j * 128:(j + 1) * 128], in_=PS[m2][0:32, 0:128]),
                       reads=[B_PS[m2]], writes=[B_UN[11]])
                for h in range(4):
                    pair, half = h // 2, h % 2
                    for br in (1, 2):
                        ob = obank()
                        kslot = 2 if br == 1 else 3

                        def k_lhsT(kt, kslot=kslot, half=half):
                            return fmo(kslot)[half * 64:(half + 1) * 64, kt * 128:(kt + 1) * 128]

                        def k_reads(kt, kslot=kslot):
                            return [B_FMO[kslot][kt // 4]]

                        def v_lhsT(kt, br=br):
                            return vst(kt)[:, (br - 1) * 128:br * 128]

                        def v_reads(kt):
                            return [B_VST[kt]]
                        extra = None
                        if br == 1:
                            def extra(kt, zt, c0, c1):
                                return [(zt[:, c0:c1], cbv("esel", 0, 32, kt * 128, (kt + 1) * 128), selbT[0:32, c0:c1],
                                         [B_UN[11], B_CONST])]
                        softmax_map(qb, tiles_for(qb, None if br == 1 else 4), pair, half * 64, k_lhsT, k_reads, h, SC64,
                                    v_lhsT, v_reads, ob, extra=extra)
                        r = nxt("rd", 2)
                        rd = ws32(o_rd + r * 1024, 512)
                        op("dve", lambda e, rd=rd, ob=ob: e.reciprocal(out=rd[0:64, :], in_=PS[ob][64:128, :]),
                           reads=[B_PS[ob]], writes=[B_RD[r]])
                        gm_ = mbank()
                        mm(PS[gm_][0:64, :], cbv("gatesel", 0, 12, (h * 3 + br) * 64, (h * 3 + br + 1) * 64), gate[0:12, :],
                           True, True, reads=[B_UN[5], B_CONST], writes=[B_PS[gm_]])
                        op("dve", lambda e, rd=rd, gm_=gm_: e.tensor_tensor(out=rd[0:64, :], in0=PS[gm_][0:64, :], in1=rd[0:64, :],
                                                                            op=ALU.mult),
                           reads=[B_PS[gm_], B_RD[r]], writes=[B_RD[r]])
                        tq_ = tmpa[half * 64:(half + 1) * 64, :]
                        op("dve", lambda e, rd=rd, ob=ob, tq_=tq_: e.tensor_tensor(out=tq_, in0=PS[ob][0:64, :], in1=rd[0:64, :],
                                                                                   op=ALU.mult),
                           reads=[B_PS[ob], B_RD[r]], writes=[B_UN[8]])
                        dst = ost()[half * 64:(half + 1) * 64, pair, :]
                        op("dve", lambda e, dst=dst, tq_=tq_: e.tensor_tensor(out=dst, in0=dst, in1=tq_, op=ALU.add),
                           reads=[B_UN[8], B_OST], writes=[B_OST])
                group_norm_and_project(l, 1, qb)

        def ffn_phase(l):
            norm_to_ht(l * 24 + 8, o_hn2, o_junk2)
            ut = ws16(o_ut, 16 * 512).rearrange("p (f n) -> p f n", f=16)
            rq = [0, 0]
            for tb in range(NQB):
                for hf in range(2):
                    for blk4 in range(4):
                        blk = hf * 4 + blk4
                        s = rq[0]
                        rq[0] = (s + 1) % 2
                        wup = ws16(o_wup + s * 4096, 4096).rearrange("p (k n) -> p k n", k=8)
                        r0 = (l * 8 + blk) * 128
                        dma(C_WUP[s], wup, wup_b[r0:r0 + 128, :].rearrange("p (k n) -> p k n", k=8), writes=[B_WUP[s]])
                        for f4 in range(4):
                            fl = blk4 * 4 + f4
                            m = mbank()
                            for kc in range(8):
                                mm(PS[m][:, :], wup[:, kc, f4 * 128:(f4 + 1) * 128], HT[:, kc, tb * 512:(tb + 1) * 512],
                                   kc == 0, kc == 7, reads=[B_WUP[s], B_HT[tb]], writes=[B_PS[m]])
                            ri = nxt("rl", 2)
                            rl = ws32(o_rl + ri * 1024, 512)
                            op("act", lambda e, rl=rl, m=m: e.activation(out=rl, in_=PS[m][:, :], func=AF.Relu),
                               reads=[B_PS[m]], writes=[B_RL[ri]])
                            op("pool", lambda e, rl=rl, fl=fl: e.tensor_tensor(out=ut[:, fl, :], in0=rl, in1=rl, op=ALU.mult),
                               reads=[B_RL[ri]], writes=[B_UT])
                    for cbk in range(2):
                        slots = []
                        for g2 in range(2):
                            s = rq[1]
                            rq[1] = (s + 1) % 4
                            wdn = ws16(o_wdn + s * 4096, 4096).rearrange("p (f n) -> p f n", f=8)
                            grp = hf * 2 + g2
                            r0 = (l * 8 + cbk * 4 + grp) * 128
                            dma(C_WDN[s], wdn, wdn_b[r0:r0 + 128, :].rearrange("p (f n) -> p f n", f=8), writes=[B_WDN[s]])
                            slots.append((s, wdn))
                        for j in range(4):
                            tt = 4 * tb + j
                            m = mbank()
                            for g2 in range(2):
                                s, wdn = slots[g2]
                                for f8 in range(8):
                                    fl = g2 * 8 + f8
                                    mm(PS[m][:, :], ut[:, fl, j * 128:(j + 1) * 128], wdn[:, f8, :], fl == 0, fl == 15,
                                       reads=[B_UT, B_WDN[s]], writes=[B_PS[m]])
                            op("dve", lambda e, tt=tt, cbk=cbk, m=m: e.tensor_tensor(
                                out=XS[:, tt, cbk * 512:(cbk + 1) * 512], in0=PS[m][:, :],
                                in1=XS[:, tt, cbk * 512:(cbk + 1) * 512], op=ALU.add),
                               reads=[B_PS[m], B_X[tt]], writes=[B_X[tt]])

        def final_norm(si):
            nf = ws32(o_nf, 1024)
            dma(C_MISC, nf, nf_d[:, :], writes=[B_NF])
            junk = ws16(o_junk2, 1024)
            for tb in range(NQB):
                for j in range(4):
                    tt = 4 * tb + j
                    op("act", lambda e, tt=tt: e.activation(out=junk, in_=XS[:, tt, :], func=AF.Square,
                                                            accum_out=SSQ[:, tt:tt + 1]),
                       reads=[B_X[tt]], writes=[B_JUNK, B_SSQ[tb]])
                sl = slice(4 * tb, 4 * tb + 4)
                op("act", lambda e, sl=sl: e.activation(out=RSTD[:, sl], in_=SSQ[:, sl], func=AF.Ln,
                                                        scale=1.0 / D, bias=cfv("epsc")),
                   reads=[B_SSQ[tb], B_CONST], writes=[B_RSTD[tb]])
                op("act", lambda e, sl=sl: e.activation(out=RSTD[:, sl], in_=RSTD[:, sl], func=AF.Exp, scale=-0.5),
                   reads=[B_RSTD[tb]], writes=[B_RSTD[tb]])
                for j in range(4):
                    tt = 4 * tb + j
                    fi = 0
                    fin = ws32(o_fin, 1024)
                    op("dve", lambda e, tt=tt, fin=fin: e.scalar_tensor_tensor(
                        out=fin, in0=XS[:, tt, :], scalar=RSTD[:, tt:tt + 1], in1=nf, op0=ALU.mult, op1=ALU.mult),
                       reads=[B_X[tt], B_RSTD[tb], B_NF], writes=[B_FIN[fi]])
                    dma(C_FIN[fi], out_d[si, tt * 128:(tt + 1) * 128, :], fin, reads=[B_FIN[fi]])

        for si in range(nseq):
            for q4 in range(4):
                dma(C_X[q4], XS[:, 4 * q4:4 * q4 + 4, :],
                    x_d[si, q4 * 512:(q4 + 1) * 512, :].rearrange("(t p) d -> p t d", p=128),
                    writes=[B_X[4 * q4 + i] for i in range(4)])
            for l in range(nlayer):
                norm_to_ht(l * 24, o_hn, o_junk)
                for mx in mixers:
                    if mx == "sb":
                        mixer_sb(l)
                    elif mx == "nsa":
                        mixer_nsa(l)
                    elif mx == "diff":
                        mixer_diff2(l)
                    elif mx == "swa":
                        mixer_swa(l)
                    kb.barrier()
                if ffn:
                    ffn_phase(l)
                    kb.barrier()
            kb.barrier()
            final_norm(si)
            kb.barrier()

        build_program.last_counts = {k: len(v) for k, v in kb.prog.items()}
        @block.tensor
        def _(e):
            for f in kb.prog["pe"]:
                f(e)

        @block.scalar
        def _(e):
            for f in kb.prog["act"]:
                f(e)

        @block.vector
        def _(e):
            for f in kb.prog["dve"]:
                f(e)

        @block.gpsimd
        def _(e):
            for f in kb.prog["pool"]:
                f(e)

        @block.sync
        def _(e):
            for f in kb.prog["sp"]:
                f(e)
    return nc, (cbarr, cfarr)


def prep_weights(inp, nlayer=DEPTH):
    w_in = np.asarray(inp["w_in"], np.float32)
    L = nlayer
    wfm = np.zeros((L, NFM, 1024, 128), np.float32)

    def cols(a, b):
        return w_in[:L, :, a:b]
    wfm[:, QA0] = cols(0, 128)
    wfm[:, QA1] = cols(128, 256)
    wfm[:, KA0] = cols(256, 384)
    wfm[:, KA1] = cols(384, 512)
    wfm[:, QN0] = cols(768, 896)
    wfm[:, QN1] = cols(896, 1024)
    wfm[:, KVC] = cols(1024, 1152)
    wfm[:, KSD, :, 0:64] = cols(1152, 1216)
    wfm[:, KSD, :, 64:128] = cols(1152, 1216)
    wfm[:, KWD, :, 0:64] = cols(1280, 1344)
    wfm[:, KWD, :, 64:128] = cols(1280, 1344)
    wfm[:, GN, :, 0:12] = cols(1408, 1420)
    wfm[:, QC0] = cols(1420, 1548)
    wfm[:, QC1] = cols(1548, 1676)
    kc0 = 1676
    for pair in range(2):
        for half in range(2):
            h = 2 * pair + half
            base = kc0 + h * 64
            wfm[:, KC00 + 2 * pair, :, half * 64:half * 64 + 32] = cols(base, base + 32)
            wfm[:, KC10 + 2 * pair, :, half * 64 + 32:half * 64 + 64] = cols(base + 32, base + 64)
    wfm[:, QD0] = cols(2188, 2316)
    wfm[:, QD1] = cols(2316, 2444)
    wfm[:, KD0, :, 0:64] = cols(2444, 2508)
    wfm[:, KD0, :, 64:128] = cols(2444, 2508)
    wfm[:, KD1, :, 0:64] = cols(2508, 2572)
    wfm[:, KD1, :, 64:128] = cols(2508, 2572)
    wfm = wfm.reshape(L, NFM, 8, 128, 128).transpose(0, 1, 3, 2, 4).reshape(L * NFM * 128, 1024)

    wtm = np.concatenate([cols(512, 768), cols(1216, 1280), cols(1344, 1408), cols(1932, 2188), cols(2572, 2700)], axis=2)
    wtm = wtm.reshape(L, 8, 128, NTM).transpose(0, 2, 1, 3).reshape(L * 128, 8 * NTM)

    w_out = np.asarray(inp["w_out"], np.float32)[:L]
    wout = w_out.reshape(L, 4, 2, 128, 1024).transpose(0, 1, 3, 2, 4).reshape(L * 4 * 128, 2048)
    w_up = np.asarray(inp["w_up"], np.float32)[:L]
    wup = w_up.reshape(L, 8, 128, 8, 512).transpose(0, 3, 2, 1, 4).reshape(L * 8 * 128, 4096)
    w_dn = np.asarray(inp["w_down"], np.float32)[:L]
    wdn = w_dn.reshape(L, 4, 8, 128, 2, 512).transpose(0, 4, 1, 3, 2, 5).reshape(L * 8 * 128, 4096)
    w1k = np.asarray(inp["cmp_w1_k"], np.float32)[:L].reshape(L, 32, 64, 128).transpose(0, 2, 1, 3)
    w1v = np.asarray(inp["cmp_w1_v"], np.float32)[:L].reshape(L, 32, 64, 128).transpose(0, 2, 1, 3)
    w1 = np.concatenate([w1k, w1v], axis=1).reshape(L * 128, 4096)
    w2k = np.asarray(inp["cmp_w2_k"], np.float32)[:L]
    w2v = np.asarray(inp["cmp_w2_v"], np.float32)[:L]
    w2 = np.concatenate([w2k, w2k, w2v], axis=2).reshape(L * 128, 192)
    gt = np.zeros((128, L * 24), np.float32)
    for l in range(L):
        gt[:, l * 24 + 0:l * 24 + 8] = np.asarray(inp["norm_attn"], np.float32)[l].reshape(8, 128).T
        gt[:, l * 24 + 8:l * 24 + 16] = np.asarray(inp["norm_mlp"], np.float32)[l].reshape(8, 128).T
        gt[:, l * 24 + 16:l * 24 + 24] = np.asarray(inp["g_mix"], np.float32)[l].reshape(8, 128).T
    lamv = np.zeros((128, L * 128), np.float32)
    for l in range(L):
        for i, k in enumerate(("diff_lq1", "diff_lk1", "diff_lq2", "diff_lk2")):
            lamv[:, l * 128 + i * 32:l * 128 + (i + 1) * 32] = np.asarray(inp[k], np.float32)[l][None, :]
    sinkt = np.ascontiguousarray(np.asarray(inp["sinks"], np.float32)[:L].T)
    pet = np.zeros((128, L * 32), np.float32)
    for l in range(L):
        pet[0:64, l * 32:(l + 1) * 32] = np.asarray(inp["cmp_pe_k"], np.float32)[l].T
        pet[64:128, l * 32:(l + 1) * 32] = np.asarray(inp["cmp_pe_v"], np.float32)[l].T
    normf = np.ascontiguousarray(np.broadcast_to(np.asarray(inp["norm_final"], np.float32)[None, :], (128, 1024)))
    return dict(wfm=np.ascontiguousarray(wfm), wtm=np.ascontiguousarray(wtm), wout=np.ascontiguousarray(wout),
                wup=np.ascontiguousarray(wup), wdn=np.ascontiguousarray(wdn), w1=np.ascontiguousarray(w1),
                w2=np.ascontiguousarray(w2), gt=gt, lamv=lamv, sinkt=sinkt, pet=pet, normf=normf)


_CACHE = {}


def kernel(**inputs):
    x = np.asarray(inputs["x"], np.float32)
    ncores = 8
    nseq = x.shape[0] // ncores
    if "prog" not in _CACHE:
        _CACHE["prog"] = build_program(nseq=nseq)
    nc, (cbarr, cfarr) = _CACHE["prog"]
    w = prep_weights(inputs)
    in_maps = []
    for c in range(ncores):
        in_maps.append(core_inputs(w, x[c * nseq:(c + 1) * nseq], cbarr, cfarr, c))
    res = run_bass_kernel_spmd(nc, in_maps, core_ids=list(range(ncores)))
    out = np.concatenate([np.asarray(r["out"]) for r in res.results], axis=0)
    return out.astype(np.float32)


def core_inputs(w, xs, cbarr, cfarr, c):
    big = ("wfm", "wtm", "wout", "wup", "wdn", "w1")
    if True:
        m = dict(w)
        for k in big:
            a = np.empty((w[k].shape[0] + 1, w[k].shape[1]), np.float32)
            a[:-1] = w[k]
            a[-1] = float(c)
            m[k] = a
        cb = np.empty((129, cbarr.shape[1]), cbarr.dtype)
        cb[:128] = cbarr
        cb[128] = float(c)
        m["cb16"] = cb
        m["x"] = np.ascontiguousarray(xs)
        m["cf32"] = cfarr
    return m
```

```python
import math
import numpy as np
import ml_dtypes
import concourse.bass as bass
import concourse.mybir as mybir
from concourse.bass_utils import run_bass_kernel_spmd

F32 = mybir.dt.float32
BF16 = mybir.dt.bfloat16
AF = mybir.ActivationFunctionType
ALU = mybir.AluOpType
AX = mybir.AxisListType

S = 2048
D = 1024
NT = 16
NQB = 4
DEPTH = 4
DFF = 4096
EPS = 1e-6
NEG = -30000.0
NFM = 20
NTM = 768
SC64 = 64 ** -0.5
SC32 = 32 ** -0.5

QA0, QA1, KA0, KA1, QN0, QN1, KVC, KSD, KWD, GN, QC0, QC1, KC00, KC10, KC01, KC11, QD0, QD1, KD0, KD1 = range(20)
TM_VA, TM_VS, TM_VW, TM_VC, TM_VD = (0, 256), (256, 320), (320, 384), (384, 640), (640, 768)


def _slopes():
    m = 2.0 ** (-8.0 * np.arange(1, 13) / 12.0)
    m = m.astype(np.float32).reshape(4, 3)
    return m[:, 0].copy(), m[:, 1].copy(), m[:, 2].copy()


class _Cols:
    def __init__(self):
        self.off = {}
        self.n = 0
        self.parts = []

    def add(self, name, arr):
        arr = np.asarray(arr, dtype=np.float32)
        assert arr.shape[0] <= 128
        a = np.zeros((128, arr.shape[1]), np.float32)
        a[:arr.shape[0]] = arr
        self.off[name] = (self.n, arr.shape[1])
        self.n += arr.shape[1]
        self.parts.append(a)

    def build(self):
        return np.concatenate(self.parts, axis=1)


def _alibi_heads():
    sn, sd, ss = _slopes()
    slopes = np.concatenate([sn, sd, ss]).astype(np.float64)
    scales = np.array([SC64] * 4 + [SC32] * 4 + [SC64] * 4)
    return slopes, scales


USE_ROW = [True, False, False, False, True, False, False, False, True, False, False, False]
CUT = [1, 3, 13, None, 2, 5, None, None, None, None, None, None]


def _build_consts():
    slopes, scales = _alibi_heads()
    cb = _Cols()
    p = np.arange(128)
    cb.add("ident", np.eye(128))
    cb.add("ones", np.ones((128, 128)))
    cb.add("u8", np.where(p[:, None] >= p[None, :], -8.0, 0.0))
    cb.add("ones8", np.full((128, 128), -8.0))
    cb.add("mdiag", np.where(p[:, None] <= p[None, :], 0.0, NEG))
    cb.add("mstrict", np.where(p[:, None] < p[None, :], 0.0, NEG))
    cb.add("manti", np.where(p[:, None] > p[None, :], 0.0, NEG))
    sel = np.zeros((12, 12 * 128))
    for h in range(12):
        sel[h, h * 128:(h + 1) * 128] = 1.0
    cb.add("rowsel", sel)
    tq = np.arange(512)
    rrow = np.stack([-(slopes[h] * tq) / scales[h] for h in range(12)], 0)
    rrow_bf = rrow.astype(np.float32).astype(ml_dtypes.bfloat16).astype(np.float32)
    cb.add("rrow", rrow_bf)
    i = np.arange(127)
    t = np.arange(S)
    cb.add("mc", np.where((16 * i[:, None] + 31) <= t[None, :], 0.0, NEG))
    g = np.zeros((127, 64))
    g[i, i // 4] = 1.0
    g[:, 32:64] = 1.0
    cb.add("gaug", g)
    m = np.arange(S)
    cb.add("esel", (m[None, :] // 64 == np.arange(32)[:, None]).astype(np.float32))
    cand = np.zeros((128, 16, 32))
    forced = np.zeros((128, 16, 32))
    j = np.arange(32)
    for tt in range(16):
        cur = (tt * 128 + p) // 64
        cand[:, tt, :] = ((j[None, :] >= 1) & (j[None, :] <= cur[:, None] - 2))
        forced[:, tt, :] = ((j[None, :] == 0) | (j[None, :] == cur[:, None]) | (j[None, :] == cur[:, None] - 1))
    cb.add("cand", cand.reshape(128, 512))
    cb.add("candm1", cand.reshape(128, 512) - 1.0)
    cb.add("forced", forced.reshape(128, 512))
    gs = np.zeros((12, 12 * 64))
    for k in range(12):
        gs[k, k * 64:(k + 1) * 64] = 1.0
    cb.add("gatesel", gs)
    bd = np.zeros((128, 128))
    bd[:64, :64] = 1.0
    bd[64:, 64:] = 1.0
    cb.add("blockdiag", bd)
    ss = np.zeros((4, 4 * 128))
    for h in range(4):
        ss[h, h * 128 + 64:(h + 1) * 128] = 1.0
    cb.add("sinksel", ss)
    cbarr = cb.build()

    cf = _Cols()
    cf.add("identf", np.eye(128))
    bp = np.zeros((128, 12 * 19))
    for h in range(12):
        for r in range(19):
            bp[:, h * 19 + r] = slopes[h] * ((r - 15) * 128 + p)
    cf.add("biasp", bp)
    cf.add("biasm", bp - 256.0 * np.repeat(slopes[:, None], 19, axis=1).reshape(1, 12 * 19))
    bc = np.zeros((128, 16))
    for h in range(4):
        for qb in range(4):
            bc[:, h * 4 + qb] = slopes[h] * (16 * p + 31 - 512 * qb)
    cf.add("biasc", bc)
    bcm = bc.copy()
    for h in range(4):
        bcm[:, h * 4:(h + 1) * 4] -= 256.0 * slopes[h]
    cf.add("biascm", bcm)
    er = np.stack([(scales[8 + h] * rrow_bf[8 + h] + slopes[8 + h] * tq) if USE_ROW[8 + h]
                   else slopes[8 + h] * (tq - 256.0) for h in range(4)], 0)
    cf.add("epsrow", er)
    cf.add("epsc", np.full((128, 1), EPS))
    cf.add("one", np.full((128, 1), 1.0))
    cfarr = cf.build()
    return cb.off, cbarr.astype(ml_dtypes.bfloat16), cf.off, cfarr.astype(np.float32)


class _Rec:
    def __init__(self):
        self.calls = []

    def __getattr__(self, name):
        def f(*a, **k):
            self.calls.append((name, a, k))
            return self
        return f


class Buf:
    __slots__ = ("name", "w", "r")

    def __init__(self, name):
        self.name = name
        self.w = None
        self.r = {}


class Chan:
    def __init__(self, sem):
        self.sem = sem
        self.count = 0


class KB:
    CE = ("pe", "act", "dve", "pool")

    def __init__(self, nc, sems):
        self.nc = nc
        self.sems = list(sems)
        self.streams = ("pe", "act", "dve", "pool", "sp")
        self.prog = {k: [] for k in self.streams}
        self.ccount = {k: 0 for k in self.CE}
        self.csem = {k: self.sems.pop() for k in self.CE}
        self.waited = {k: {} for k in self.streams}
        self.chans = []
        self.bufs = []

    def buf(self, name):
        b = Buf(name)
        self.bufs.append(b)
        return b

    def chan(self):
        c = Chan(self.sems.pop())
        self.chans.append(c)
        return c

    def _sem(self, k):
        return self.csem[k] if isinstance(k, str) else k.sem

    def _waits(self, stream, reads, writes, dma):
        need = {}
        for b in reads:
            if b.w is not None and b.w[1] > need.get(b.w[0], 0):
                need[b.w[0]] = b.w[1]
        for b in writes:
            if b.w is not None and b.w[1] > need.get(b.w[0], 0):
                need[b.w[0]] = b.w[1]
            for k, v in b.r.items():
                if v > need.get(k, 0):
                    need[k] = v
        out = []
        wd = self.waited[stream]
        for k, v in need.items():
            if k == "pe" and stream == "pe":
                continue
            if dma is not None and k is dma:
                continue
            if wd.get(k, 0) >= v:
                continue
            wd[k] = v
            out.append((self._sem(k), v))
        return out

    def op(self, stream, fn, reads=(), writes=(), dma=None):
        waits = self._waits(stream, reads, writes, dma)
        if dma is None:
            self.ccount[stream] += 1
            key, val, sem, inc = stream, self.ccount[stream], self.csem[stream], 1
        else:
            dma.count += 16
            key, val, sem, inc = dma, dma.count, dma.sem, 16
        for b in reads:
            if b.r.get(key, 0) < val:
                b.r[key] = val
        for b in writes:
            b.w = (key, val)
            b.r = {}

        rec = _Rec()
        fn(rec)
        assert len(rec.calls) == 1
        name, a, k = rec.calls[0]

        def run(e, waits=waits, name=name, a=a, k=k, sem=sem, inc=inc):
            for (sm, v) in waits:
                e.wait_ge(sm, v)
            try:
                inst = getattr(e, name)(*a, **k)
            except Exception:
                print("FAILED OP", name, k.keys(), [(kk, getattr(v, 'shape', v), getattr(v, 'dtype', None)) for kk, v in k.items()])
                raise
            inst.then_inc(sem, inc)
        self.prog[stream].append(run)

    def barrier(self):
        for st in self.streams:
            waits = []
            wd = self.waited[st]
            for k in self.CE:
                if k == "pe" and st == "pe":
                    continue
                v = self.ccount[k]
                if v > wd.get(k, 0):
                    wd[k] = v
                    waits.append((self.csem[k], v))
            for c in self.chans:
                if c.count > wd.get(c, 0):
                    wd[c] = c.count
                    waits.append((c.sem, c.count))

            def run(e, waits=waits):
                for (sm, v) in waits:
                    e.wait_ge(sm, v)
            self.prog[st].append(run)
        for b in self.bufs:
            b.w = None
            b.r = {}


def _ap(t, offset, pattern):
    return bass.AP(tensor=t.tensor, offset=offset, ap=[list(x) for x in pattern])


def build_program(nseq=4, nlayer=DEPTH, mixers=("sb", "nsa", "diff", "swa"), ffn=True, dbg=False):
    cboff, cbarr, cfoff, cfarr = _build_consts()
    NCB = cbarr.shape[1]
    NCF = cfarr.shape[1]
    sn, sd, ssw = _slopes()

    nc = bass.Bass("TRN2", target_bir_lowering=False)
    x_d = nc.dram_tensor("x", [nseq, S, D], F32, kind="ExternalInput").ap()
    out_d = nc.dram_tensor("out", [nseq, S, D], F32, kind="ExternalOutput").ap()
    cb_d = nc.dram_tensor("cb16", [129, NCB], BF16, kind="ExternalInput").ap()[0:128, :]
    cf_d = nc.dram_tensor("cf32", [128, NCF], F32, kind="ExternalInput").ap()
    wfm_d = nc.dram_tensor("wfm", [nlayer * NFM * 128 + 1, 1024], F32, kind="ExternalInput").ap()[0:nlayer * NFM * 128, :]
    wtm_d = nc.dram_tensor("wtm", [nlayer * 128 + 1, 8 * NTM], F32, kind="ExternalInput").ap()[0:nlayer * 128, :]
    wout_d = nc.dram_tensor("wout", [nlayer * 4 * 128 + 1, 2048], F32, kind="ExternalInput").ap()[0:nlayer * 4 * 128, :]
    wup_d = nc.dram_tensor("wup", [nlayer * 8 * 128 + 1, 4096], F32, kind="ExternalInput").ap()[0:nlayer * 8 * 128, :]
    wdn_d = nc.dram_tensor("wdn", [nlayer * 8 * 128 + 1, 4096], F32, kind="ExternalInput").ap()[0:nlayer * 8 * 128, :]
    w1_d = nc.dram_tensor("w1", [nlayer * 128 + 1, 4096], F32, kind="ExternalInput").ap()[0:nlayer * 128, :]
    w2_d = nc.dram_tensor("w2", [nlayer * 128, 192], F32, kind="ExternalInput").ap()
    gt_d = nc.dram_tensor("gt", [128, nlayer * 24], F32, kind="ExternalInput").ap()
    lam_d = nc.dram_tensor("lamv", [128, nlayer * 128], F32, kind="ExternalInput").ap()
    sink_d = nc.dram_tensor("sinkt", [4, nlayer], F32, kind="ExternalInput").ap()
    pe_d = nc.dram_tensor("pet", [128, nlayer * 32], F32, kind="ExternalInput").ap()
    nf_d = nc.dram_tensor("normf", [128, 1024], F32, kind="ExternalInput").ap()
    dbg16_d = nc.dram_tensor("dbg16", [128, 4096], BF16, kind="ExternalOutput").ap() if dbg else None
    dbg32_d = nc.dram_tensor("dbg32", [128, 2048], F32, kind="ExternalOutput").ap() if dbg else None
    wfm_b = nc.dram_tensor("wfm_b", [nlayer * NFM * 128, 1024], BF16, kind="Internal").ap()
    wtm_b = nc.dram_tensor("wtm_b", [nlayer * 128, 8 * NTM], BF16, kind="Internal").ap()
    wout_b = nc.dram_tensor("wout_b", [nlayer * 4 * 128, 2048], BF16, kind="Internal").ap()
    wup_b = nc.dram_tensor("wup_b", [nlayer * 8 * 128, 4096], BF16, kind="Internal").ap()
    wdn_b = nc.dram_tensor("wdn_b", [nlayer * 8 * 128, 4096], BF16, kind="Internal").ap()
    w1_b = nc.dram_tensor("w1_b", [nlayer * 128, 4096], BF16, kind="Internal").ap()
    w2_b = nc.dram_tensor("w2_b", [nlayer * 128, 192], BF16, kind="Internal").ap()

    import contextlib
    es = contextlib.ExitStack()
    with es:
        def sb(name, shape, dt):
            return es.enter_context(nc.sbuf_tensor(name, shape, dt))

        XS = sb("XS", [128, NT, D], F32)
        HT = sb("HT", [128, 8, S], BF16)
        CB = sb("CB", [128, NCB], BF16)
        CF = sb("CF", [128, NCF], F32)
        GT = sb("GT", [128, nlayer * 24], F32)
        LAMV = sb("LAMV", [128, nlayer * 128], F32)
        LAMS = sb("LAMS", [128, 4 * nlayer], F32)
        LTMP = sb("LTMP", [128, 64], F32)
        SINKT = sb("SINKT", [4, nlayer], F32)
        PET = sb("PET", [128, nlayer * 32], F32)
        SSQ = sb("SSQ", [128, 16], F32)
        RSTD = sb("RSTD", [128, 16], F32)
        WS = sb("WS", [128, 40960], BF16)
        PS = [es.enter_context(nc.psum_tensor("ps%d" % i, [128, 512], F32)) for i in range(8)]
        sems = [es.enter_context(nc.semaphore("s%d" % i)) for i in range(60)]
        block = es.enter_context(nc.Block())

        kb = KB(nc, sems)
        op = kb.op

        class Carver:
            def __init__(self):
                self.o = 0

            def take(self, nbf16):
                o = self.o
                self.o += nbf16
                assert self.o <= 40960, self.o
                return o

        def ws16(o, n):
            return WS[:, o:o + n]

        def ws32(o, n):
            return WS[:, o:o + 2 * n].bitcast(F32)

        def cbv(name, r0=0, r1=128, c0=0, c1=None):
            o, n = cboff[name]
            if c1 is None:
                c1 = n
            return CB[r0:r1, o + c0:o + c1]

        def cfv(name, r0=0, r1=128, c0=0, c1=None):
            o, n = cfoff[name]
            if c1 is None:
                c1 = n
            return CF[r0:r1, o + c0:o + c1]

        cv = Carver()
        o_fmo = cv.take(5 * S)
        o_vst = cv.take(16 * 256)
        o_wfm = cv.take(3 * 1024)
        o_wtm = cv.take(8 * 256)
        o_wgn = cv.take(1024)
        o_pt = cv.take(4 * 512)
        o_un = cv.take(6144)
        o_ost = cv.take(2 * 2 * 512)
        o_nt = cv.take(2048)
        o_mix = cv.take(2 * 2 * 512)
        o_wo = cv.take(2 * 1024)
        o_rd = cv.take(2 * 2 * 512)
        o_hn = cv.take(1024)
        o_junk = cv.take(1024)
        att_end = cv.o
        cv2 = Carver()
        o_ut = cv2.take(16 * 512)
        o_wup = cv2.take(2 * 8 * 512)
        o_wdn = cv2.take(4 * 8 * 512)
        o_rl = cv2.take(2 * 2 * 512)
        o_hn2 = cv2.take(1024)
        o_junk2 = cv2.take(1024)
        o_fin = cv2.take(2 * 1024)
        o_nf = cv2.take(2 * 1024)

        B_X = [kb.buf("x%d" % i) for i in range(NT)]
        B_HT = [kb.buf("ht%d" % i) for i in range(NQB)]
        B_PS = [kb.buf("ps%d" % i) for i in range(8)]
        B_CONST = kb.buf("const")
        B_SSQ = [kb.buf("ssq%d" % i) for i in range(4)]
        B_RSTD = [kb.buf("rstd%d" % i) for i in range(4)]
        B_HN = kb.buf("hn")
        B_JUNK = kb.buf("junk")
        B_FMO = [[kb.buf("fmo%d_%d" % (i, j)) for j in range(NQB)] for i in range(5)]
        B_VST = [kb.buf("vst%d" % i) for i in range(NT)]
        B_WFM = [kb.buf("wfm%d" % i) for i in range(3)]
        C_WFM = [kb.chan() for _ in range(3)]
        B_WTM = kb.buf("wtm")
        C_WTM = kb.chan()
        B_WGN = kb.buf("wgn")
        C_WGN = kb.chan()
        C_W2X = kb.chan()
        B_PT = [kb.buf("pt%d" % i) for i in range(4)]
        B_OST = kb.buf("ost")
        B_NT = kb.buf("nt")
        B_MIX = [kb.buf("mix%d" % i) for i in range(2)]
        B_WO = kb.buf("wo")
        C_WO = kb.chan()
        B_RD = [kb.buf("rd%d" % i) for i in range(2)]
        B_UN = [kb.buf("un%d" % i) for i in range(12)]
        B_UT = kb.buf("ut")
        B_WUP = [kb.buf("wup%d" % i) for i in range(2)]
        C_WUP = [kb.chan() for _ in range(2)]
        B_WDN = [kb.buf("wdn%d" % i) for i in range(4)]
        C_WDN = [kb.chan() for _ in range(4)]
        B_RL = [kb.buf("rl%d" % i) for i in range(2)]
        B_FIN = [kb.buf("fin%d" % i) for i in range(2)]
        C_FIN = [kb.chan() for _ in range(2)]
        B_NF = kb.buf("nf")
        C_X = [kb.chan() for _ in range(4)]
        C_MISC = kb.chan()
        C_CASTS = [kb.chan() for _ in range(4)]
        B_LAM = kb.buf("lam")
        B_SINK = kb.buf("sink")

        ring = {"z": 0, "o": 0, "m": 0, "pt": 0, "wfm": 0, "rd": 0, "mix": 0, "rl": 0}
        ZB, OB, MB = (0, 1, 2), (3, 4), (5, 6, 7)

        def nxt(kind, n):
            v = ring[kind]
            ring[kind] = (v + 1) % n
            return v

        def zbank():
            return ZB[nxt("z", 3)]

        def obank():
            return OB[nxt("o", 2)]

        def mbank():
            return MB[nxt("m", 3)]

        def mm(out, lhsT, rhs, start, stop, reads, writes):
            op("pe", lambda e: e.matmul(out, lhsT=lhsT, rhs=rhs, start=start, stop=stop), reads=reads, writes=writes)

        def dma(chan, out, in_, reads=(), writes=(), stream="sp"):
            op(stream, lambda e: e.dma_start(out=out, in_=in_), reads=reads, writes=writes, dma=chan)

        IDENT = cbv("ident")
        ONESB = cbv("ones")
        C_DBG = kb.chan()
        dbg_state = {}

        def tap16(name, src, reads, c0=0):
            if dbg and dbg == name and name not in dbg_state:
                dbg_state[name] = 1
                n = src.shape[-1]
                dma(C_DBG, dbg16_d[0:src.shape[0], c0:c0 + n], src, reads=reads)

        def tap32(name, src, reads, c0=0):
            if dbg and dbg == name and name not in dbg_state:
                dbg_state[name] = 1
                n = src.shape[-1]
                dma(C_DBG, dbg32_d[0:src.shape[0], c0:c0 + n], src, reads=reads)

        dma(C_MISC, CB[:], cb_d[:, :], writes=[B_CONST])
        dma(C_MISC, CF[:], cf_d[:, :], writes=[B_CONST])
        dma(C_MISC, GT[:], gt_d[:, :], writes=[B_CONST])
        dma(C_MISC, LAMV[:], lam_d[:, :], writes=[B_CONST])
        dma(C_MISC, SINKT[:], sink_d[:, :], writes=[B_CONST])
        dma(C_MISC, PET[:], pe_d[:, :], writes=[B_CONST])

        B_WB = kb.buf("wcast")

        cast_i = [0]

        def cast_dma(o_, i_):
            ch = C_CASTS[cast_i[0] % 4]
            cast_i[0] += 1
            if ch.count > 0:
                kb.prog["pool"].append(lambda e, sem=ch.sem, v=ch.count: e.wait_ge(sem, v))
            dma(ch, o_, i_, writes=[B_WB], stream="pool")

        def cast_all(src, dst):
            rows, cols = src.shape
            step = 128
            for r0 in range(0, rows, step):
                r1 = min(rows, r0 + step)
                if cols <= 2048:
                    cast_dma(dst[r0:r1, :], src[r0:r1, :])
                else:
                    for c0 in range(0, cols, 2048):
                        c1 = min(cols, c0 + 2048)
                        cast_dma(dst[r0:r1, c0:c1], src[r0:r1, c0:c1])

        for (a, b) in ((wfm_d, wfm_b), (wtm_d, wtm_b), (wout_d, wout_b), (w1_d, w1_b), (w2_d, w2_b),
                       (wup_d, wup_b), (wdn_d, wdn_b)):
            cast_all(a, b)

        for l in range(nlayer):
            lam_init = 0.8 - 0.6 * math.exp(-0.3 * l)
            lv = LAMV[:, l * 128:(l + 1) * 128]
            op("dve", lambda e, lv=lv: e.tensor_tensor(out=LTMP[:, 0:32], in0=lv[:, 0:32], in1=lv[:, 32:64], op=ALU.mult),
               reads=[B_CONST], writes=[B_LAM])
            op("dve", lambda e, l=l: e.tensor_reduce(out=LAMS[:, 4 * l + 2:4 * l + 3], in_=LTMP[:, 0:32], axis=AX.X, op=ALU.add),
               reads=[B_LAM], writes=[B_LAM])
            op("dve", lambda e, lv=lv: e.tensor_tensor(out=LTMP[:, 32:64], in0=lv[:, 64:96], in1=lv[:, 96:128], op=ALU.mult),
               reads=[B_CONST, B_LAM], writes=[B_LAM])
            op("dve", lambda e, l=l: e.tensor_reduce(out=LAMS[:, 4 * l + 3:4 * l + 4], in_=LTMP[:, 32:64], axis=AX.X, op=ALU.add),
               reads=[B_LAM], writes=[B_LAM])
            op("act", lambda e, l=l: e.activation(out=LAMS[:, 4 * l + 2:4 * l + 4], in_=LAMS[:, 4 * l + 2:4 * l + 4], func=AF.Exp),
               reads=[B_LAM], writes=[B_LAM])
            op("dve", lambda e, l=l: e.tensor_tensor(out=LAMS[:, 4 * l:4 * l + 1], in0=LAMS[:, 4 * l + 2:4 * l + 3],
                                                      in1=LAMS[:, 4 * l + 3:4 * l + 4], op=ALU.subtract),
               reads=[B_LAM], writes=[B_LAM])
            op("dve", lambda e, l=l, li=lam_init: e.tensor_scalar(out=LAMS[:, 4 * l + 1:4 * l + 2], in0=LAMS[:, 4 * l:4 * l + 1],
                                                                   scalar1=li, scalar2=-1.0, op0=ALU.add, op1=ALU.mult),
               reads=[B_LAM], writes=[B_LAM])
        kb.barrier()

        def norm_to_ht(g0, o_hn_, o_junk_):
            hn = ws16(o_hn_, 1024)
            junk = ws16(o_junk_, 1024)
            for tb in range(NQB):
                for j in range(4):
                    tt = 4 * tb + j
                    op("act", lambda e, tt=tt: e.activation(out=junk, in_=XS[:, tt, :], func=AF.Square,
                                                            accum_out=SSQ[:, tt:tt + 1]),
                       reads=[B_X[tt]], writes=[B_JUNK, B_SSQ[tb]])
                sl = slice(4 * tb, 4 * tb + 4)
                op("act", lambda e, sl=sl: e.activation(out=RSTD[:, sl], in_=SSQ[:, sl], func=AF.Ln,
                                                        scale=1.0 / D, bias=cfv("epsc")),
                   reads=[B_SSQ[tb], B_CONST], writes=[B_RSTD[tb]])
                op("act", lambda e, sl=sl: e.activation(out=RSTD[:, sl], in_=RSTD[:, sl], func=AF.Exp, scale=-0.5),
                   reads=[B_RSTD[tb]], writes=[B_RSTD[tb]])
                for j in range(4):
                    tt = 4 * tb + j
                    op("dve", lambda e, tt=tt: e.tensor_scalar(out=hn, in0=XS[:, tt, :], scalar1=RSTD[:, tt:tt + 1],
                                                               scalar2=None, op0=ALU.mult),
                       reads=[B_X[tt], B_RSTD[tb]], writes=[B_HN])
                    m = mbank()
                    psb = PS[m][:].bitcast(BF16)
                    for c in range(8):
                        op("pe", lambda e, c=c, psb=psb: e.transpose(out=psb[:, c * 128:(c + 1) * 128],
                                                                     in_=hn[:, c * 128:(c + 1) * 128], identity=IDENT),
                           reads=[B_HN, B_CONST], writes=[B_PS[m]])
                    gain = GT[:, g0:g0 + 8].unsqueeze(2).broadcast_to([128, 8, 128])
                    op("dve", lambda e, tt=tt, psb=psb, gain=gain: e.tensor_tensor(
                        out=HT[:, :, tt * 128:(tt + 1) * 128],
                        in0=psb.rearrange("p (c n) -> p c n", c=8), in1=gain, op=ALU.mult),
                       reads=[B_PS[m], B_CONST], writes=[B_HT[tb]])

        evac_flip = [0]

        def evac_copy(out, in_, reads, writes):
            evac_flip[0] ^= 1
            if evac_flip[0]:
                op("dve", lambda e: e.tensor_copy(out=out, in_=in_), reads=reads, writes=writes)
            else:
                op("act", lambda e: e.activation(out=out, in_=in_, func=AF.Copy), reads=reads, writes=writes)

        def load_wfm(l, chunk):
            s = nxt("wfm", 3)
            w = ws16(o_wfm + s * 1024, 1024).rearrange("p (k n) -> p k n", k=8)
            r0 = (l * NFM + chunk) * 128
            dma(C_WFM[s], w, wfm_b[r0:r0 + 128, :].rearrange("p (k n) -> p k n", k=8), writes=[B_WFM[s]])
            return w, B_WFM[s]

        def fmo(slot):
            return ws16(o_fmo + slot * S, S)

        def proj_fm(l, chunk, slot, M=128):
            w, bw = load_wfm(l, chunk)
            for tb in range(NQB):
                m = mbank()
                for kc in range(8):
                    mm(PS[m][0:M, :], w[:, kc, 0:M], HT[:, kc, tb * 512:(tb + 1) * 512], kc == 0, kc == 7,
                       reads=[bw, B_HT[tb]], writes=[B_PS[m]])
                evac_copy(fmo(slot)[0:M, tb * 512:(tb + 1) * 512], PS[m][0:M, :], reads=[B_PS[m]], writes=[B_FMO[slot][tb]])

        def vst(tt):
            return WS[:, o_vst + tt * 256:o_vst + (tt + 1) * 256]

        def proj_tm(l, c0, ncols, evac):
            w = ws16(o_wtm, 8 * 256).rearrange("p (k n) -> p k n", k=8)
            src = wtm_b[l * 128:(l + 1) * 128, :].rearrange("p (k n) -> p k n", k=8)
            dma(C_WTM, w[:, :, 0:ncols], src[:, :, c0:c0 + ncols], writes=[B_WTM])
            for tt in range(NT):
                m = mbank()
                for kc in range(8):
                    mm(PS[m][:, 0:ncols], HT[:, kc, tt * 128:(tt + 1) * 128], w[:, kc, 0:ncols], kc == 0, kc == 7,
                       reads=[B_WTM, B_HT[tt // 4]], writes=[B_PS[m]])
                evac(tt, PS[m][:, 0:ncols], B_PS[m])

        def vones():
            v = WS[:, o_vst:o_vst + 16 * 256].rearrange("p (t b c) -> p t b c", t=16, b=2)
            for b in range(2):
                op("pool", lambda e, b=b: e.memset(v[:, :, b, 64:128], 1.0), writes=B_VST)

        def tiles_for(qb, wt, cut=None):
            res = []
            lo = 0 if wt is None else max(0, 4 * qb - wt)
            for kt in range(lo, 4 * qb + 4):
                jlo, jhi, dj, aj = None, None, None, None
                for j in range(4):
                    delta = 4 * qb + j - kt
                    if delta < 0 or (wt is not None and delta > wt):
                        continue
                    if cut is not None and delta > cut:
                        continue
                    if jlo is None:
                        jlo = j
                    jhi = j
                    if delta == 0:
                        dj = j
                    if wt is not None and delta == wt:
                        aj = j
                if jlo is not None:
                    res.append((kt, jlo, jhi, dj, aj))
            return res

        def softmax_map(qb, tiles, qslot, qbase, k_lhsT, k_reads, ah, scale, v_lhsT, v_reads, ob, extra=None,
                        diag_name="mdiag"):
            q = fmo(qslot)[qbase:qbase + 64, qb * 512:(qb + 1) * 512]
            rowsel = cbv("rowsel", 0, 12, ah * 128, (ah + 1) * 128)
            rrow = cbv("rrow", 0, 12)
            n = len(tiles)
            for idx, (kt, jlo, jhi, dj, aj) in enumerate(tiles):
                c0, c1 = jlo * 128, (jhi + 1) * 128
                z = zbank()
                zt = PS[z]
                steps = [(zt[:, c0:c1], k_lhsT(kt), q[:, c0:c1], [B_FMO[qslot][qb]] + k_reads(kt))]
                if USE_ROW[ah]:
                    steps.append((zt[:, c0:c1], rowsel, rrow[:, c0:c1], [B_CONST]))
                if extra is not None:
                    steps += extra(kt, zt, c0, c1)
                if dj is not None:
                    steps.append((zt[:, dj * 128:(dj + 1) * 128], IDENT, cbv(diag_name), [B_CONST]))
                if aj is not None:
                    steps.append((zt[:, aj * 128:(aj + 1) * 128], IDENT, cbv("manti"), [B_CONST]))
                for si, (o_, l_, r_, rd_) in enumerate(steps):
                    mm(o_, l_, r_, si == 0, si == len(steps) - 1, reads=rd_, writes=[B_PS[z]])
                p = nxt("pt", 4)
                pt = ws16(o_pt + p * 512, 512)
                rel = kt - 4 * qb + 15
                bias = cfv("biasp" if USE_ROW[ah] else "biasm", 0, 128, ah * 19 + rel, ah * 19 + rel + 1)
                op("act", lambda e, pt=pt, zt=zt, c0=c0, c1=c1, bias=bias: e.activation(
                    out=pt[:, c0:c1], in_=zt[:, c0:c1], func=AF.Exp, scale=scale, bias=bias),
                   reads=[B_PS[z], B_CONST], writes=[B_PT[p]])
                tap16("pt", pt, [B_PT[p]], 0)
                mm(PS[ob][:, c0:c1], v_lhsT(kt), pt[:, c0:c1], idx == 0, idx == n - 1,
                   reads=[B_PT[p]] + v_reads(kt), writes=[B_PS[ob]])

        def ost():
            return ws32(o_ost, 1024).rearrange("p (c n) -> p c n", c=2)

        def normalize_to_ost(ob, c, half, first=True, wmul=None):
            r = nxt("rd", 2)
            rd = ws32(o_rd + r * 1024, 512)
            op("dve", lambda e: e.reciprocal(out=rd[0:64, :], in_=PS[ob][64:128, :]), reads=[B_PS[ob]], writes=[B_RD[r]])
            dst = ost()[half * 64:(half + 1) * 64, c, :]
            op("dve", lambda e: e.tensor_tensor(out=dst, in0=PS[ob][0:64, :], in1=rd[0:64, :], op=ALU.mult),
               reads=[B_PS[ob], B_RD[r]], writes=[B_OST])

        def group_norm_and_project(l, g, qb, headwise=False, post=1.0):
            o2 = ost()
            sq = ws16(o_nt, 1024).rearrange("p (c n) -> p c n", c=2)
            rs = ws32(o_nt + 1024, 512)
            for c in range(2):
                op("act", lambda e, c=c: e.activation(out=sq[:, c, :], in_=o2[:, c, :], func=AF.Square),
                   reads=[B_OST], writes=[B_NT])
            gm = GT[:, l * 24 + 16 + 2 * g:l * 24 + 16 + 2 * g + 2]
            mi = nxt("mix", 2)
            mix = ws16(o_mix + mi * 1024, 1024).rearrange("p (c n) -> p c n", c=2)
            nfeat = 64.0 if headwise else 256.0
            if not headwise:
                m = mbank()
                for c in range(2):
                    mm(PS[m][:, :], ONESB, sq[:, c, :], c == 0, c == 1, reads=[B_NT, B_CONST], writes=[B_PS[m]])
                tap16("sq", ws16(o_nt, 1024), [B_NT], 0)
                if dbg == "ssq" and "ssq" not in dbg_state:
                    tmpd = ws32(o_un + 1024, 512)
                    op("dve", lambda e, m=m: e.tensor_copy(out=tmpd, in_=PS[m][:, :]), reads=[B_PS[m]], writes=[B_UN[2]])
                    tap32("ssq", tmpd, [B_UN[2]], 0)
                op("act", lambda e: e.activation(out=rs, in_=PS[m][:, :], func=AF.Ln, scale=1.0 / nfeat, bias=cfv("epsc")),
                   reads=[B_PS[m], B_CONST], writes=[B_NT])
                if dbg == "lnv":
                    tap32("lnv", rs, [B_NT], 0)
                op("act", lambda e: e.activation(out=rs, in_=rs, func=AF.Exp, scale=-0.5), reads=[B_NT], writes=[B_NT])
                for c in range(2):
                    op("dve", lambda e, c=c: e.scalar_tensor_tensor(out=mix[:, c, :], in0=o2[:, c, :], scalar=gm[:, c:c + 1],
                                                                    in1=rs, op0=ALU.mult, op1=ALU.mult),
                       reads=[B_OST, B_NT, B_CONST], writes=[B_MIX[mi]])
            else:
                for c in range(2):
                    m = mbank()
                    mm(PS[m][:, :], cbv("blockdiag"), sq[:, c, :], True, True, reads=[B_NT, B_CONST], writes=[B_PS[m]])
                    op("act", lambda e, m=m: e.activation(out=rs, in_=PS[m][:, :], func=AF.Ln, scale=1.0 / nfeat,
                                                          bias=cfv("epsc")),
                       reads=[B_PS[m], B_CONST], writes=[B_NT])
                    op("act", lambda e: e.activation(out=rs, in_=rs, func=AF.Exp, scale=-0.5), reads=[B_NT], writes=[B_NT])
                    op("dve", lambda e: e.tensor_scalar(out=rs, in0=rs, scalar1=post, scalar2=None, op0=ALU.mult),
                       reads=[B_NT], writes=[B_NT])
                    op("dve", lambda e, c=c: e.scalar_tensor_tensor(out=mix[:, c, :], in0=o2[:, c, :], scalar=gm[:, c:c + 1],
                                                                    in1=rs, op0=ALU.mult, op1=ALU.mult),
                       reads=[B_OST, B_NT, B_CONST], writes=[B_MIX[mi]])
            tap32("rs", rs, [B_NT], 0)
            tap16("mix", ws16(o_mix + mi * 1024, 1024), [B_MIX[mi]], 0)
            tap16("wo", ws16(o_wo, 2048), [B_WO], 0)
            wo = ws16(o_wo, 2048).rearrange("p (c n) -> p c n", c=2)
            for j in range(4):
                tt = 4 * qb + j
                for cbk in range(2):
                    m = mbank()
                    for c in range(2):
                        mm(PS[m][:, :], mix[:, c, j * 128:(j + 1) * 128], wo[:, c, cbk * 512:(cbk + 1) * 512], c == 0, c == 1,
                           reads=[B_MIX[mi], B_WO], writes=[B_PS[m]])
                    op("dve", lambda e, tt=tt, cbk=cbk, m=m: e.tensor_tensor(
                        out=XS[:, tt, cbk * 512:(cbk + 1) * 512], in0=PS[m][:, :], in1=XS[:, tt, cbk * 512:(cbk + 1) * 512],
                        op=ALU.add), reads=[B_PS[m], B_X[tt]], writes=[B_X[tt]])

        def load_wout(l, g):
            wo = ws16(o_wo, 2048).rearrange("p (c n) -> p c n", c=2)
            r0 = (l * 4 + g) * 128
            dma(C_WO, wo, wout_b[r0:r0 + 128, :].rearrange("p (c n) -> p c n", c=2), writes=[B_WO])

        def mixer_swa(l):
            load_wout(l, 3)
            vones()

            def evac_v(tt, ps, bps):
                v = vst(tt).rearrange("p (b c) -> p b c", b=2)
                evac_copy(v[:, :, 0:64], ps.rearrange("p (b c) -> p b c", b=2), reads=[bps], writes=[B_VST[tt]])
            proj_tm(l, TM_VD[0], 128, evac_v)
            proj_fm(l, QD0, 0)
            proj_fm(l, QD1, 1)
            proj_fm(l, KD0, 2)
            proj_fm(l, KD1, 3)
            tap16("fmo", fmo(0), [B_FMO[0][i] for i in range(4)], 0)
            dbg_state.pop("fmo", None)
            tap16("fmo", fmo(2), [B_FMO[2][i] for i in range(4)], 2048)
            tap16("vst", WS[:, o_vst:o_vst + 4096], B_VST, 0)
            sinkrow = ws16(o_un, 512)
            op("act", lambda e: e.activation(out=sinkrow[0:4, :], in_=cfv("epsrow", 0, 4), func=AF.Exp,
                                             bias=SINKT[0:4, l:l + 1]),
               reads=[B_CONST], writes=[B_UN[0]])
            for qb in range(NQB):
                tiles = tiles_for(qb, 1)
                for h in range(4):
                    pair, half = h // 2, h % 2
                    kslot = 2 + pair
                    ob = obank()

                    def k_lhsT(kt, kslot=kslot, half=half):
                        return fmo(kslot)[half * 64:(half + 1) * 64, kt * 128:(kt + 1) * 128]

                    def k_reads(kt, kslot=kslot):
                        return [B_FMO[kslot][kt // 4]]

                    def v_lhsT(kt, pair=pair):
                        return vst(kt)[:, pair * 128:(pair + 1) * 128]

                    def v_reads(kt):
                        return [B_VST[kt]]
                    softmax_map(qb, tiles, pair, half * 64, k_lhsT, k_reads, 8 + h, SC64, v_lhsT, v_reads, ob)
                    mm(PS[ob][:, :], cbv("sinksel", 0, 4, h * 128, (h + 1) * 128), sinkrow[0:4, :], False, True,
                       reads=[B_UN[0], B_CONST], writes=[B_PS[ob]])
                    if dbg == "psob" and "psob" not in dbg_state:
                        tmpd = ws32(o_un + 1024, 512)
                        op("dve", lambda e, ob=ob: e.tensor_copy(out=tmpd, in_=PS[ob][:, :]), reads=[B_PS[ob]], writes=[B_UN[2]])
                        tap32("psob", tmpd, [B_UN[2]], 0)
                    normalize_to_ost(ob, pair, half)
                tap32("ost", ws32(o_ost, 1024), [B_OST], 0)
                group_norm_and_project(l, 3, qb)

        def mixer_diff2(l):
            lam_init = 0.8 - 0.6 * math.exp(-0.3 * l)
            post = 1.0 - lam_init
            neglam = LAMS[:, 4 * l + 1:4 * l + 2]
            for pair in range(2):
                wo = ws16(o_wo, 2048).rearrange("p (c n) -> p c n", c=2)
                r0 = (l * 4 + 2) * 128
                src = wout_b[r0:r0 + 128, :].rearrange("p (c n) -> p c n", c=2)
                dma(C_WO, wo[:, 0, :], src[:, pair, :], writes=[B_WO])
                vones()

                def evac_v(tt, ps, bps):
                    v = vst(tt).rearrange("p (b c) -> p b c", b=2)
                    evac_copy(v[:, :, 0:64], ps.rearrange("p (b c) -> p b c", b=2), reads=[bps], writes=[B_VST[tt]])
                proj_tm(l, TM_VC[0] + pair * 128, 128, evac_v)
                proj_fm(l, QC0 + pair, 0)
                proj_fm(l, KC00 + 2 * pair, 1)
                proj_fm(l, KC10 + 2 * pair, 2)
                for qb in range(NQB):
                    for half in range(2):
                        h = 2 * pair + half
                        tiles = tiles_for(qb, None, CUT[4 + h])
                        obs = []
                        for cmap in range(2):
                            ob = obank()
                            obs.append(ob)
                            kslot = 1 + cmap

                            def k_lhsT(kt, kslot=kslot, half=half):
                                return fmo(kslot)[half * 64:(half + 1) * 64, kt * 128:(kt + 1) * 128]

                            def k_reads(kt, kslot=kslot):
                                return [B_FMO[kslot][kt // 4]]

                            def v_lhsT(kt, half=half):
                                return vst(kt)[:, half * 128:(half + 1) * 128]

                            def v_reads(kt):
                                return [B_VST[kt]]
                            softmax_map(qb, tiles, 0, half * 64, k_lhsT, k_reads, 4 + h, SC32, v_lhsT, v_reads, ob)
                        t0 = ws32(o_un, 512)
                        t1 = ws32(o_un + 1024, 512)
                        rds = []
                        for i in range(2):
                            r = nxt("rd", 2)
                            rd = ws32(o_rd + r * 1024, 512)
                            rds.append((r, rd))
                            op("dve", lambda e, rd=rd, ob=obs[i]: e.reciprocal(out=rd[0:64, :], in_=PS[ob][64:128, :]),
                               reads=[B_PS[obs[i]]], writes=[B_RD[r]])
                        op("dve", lambda e, obs=obs, rds=rds: e.tensor_tensor(out=t0[0:64, :], in0=PS[obs[0]][0:64, :],
                                                                              in1=rds[0][1][0:64, :], op=ALU.mult),
                           reads=[B_PS[obs[0]], B_RD[rds[0][0]]], writes=[B_UN[0]])
                        op("dve", lambda e, obs=obs, rds=rds: e.tensor_tensor(out=t1[0:64, :], in0=PS[obs[1]][0:64, :],
                                                                              in1=rds[1][1][0:64, :], op=ALU.mult),
                           reads=[B_PS[obs[1]], B_RD[rds[1][0]]], writes=[B_UN[1]])
                        dst = ost()[half * 64:(half + 1) * 64, 0, :]
                        op("dve", lambda e, dst=dst: e.scalar_tensor_tensor(out=dst, in0=t1[0:64, :], scalar=neglam[0:64, :],
                                                                            in1=t0[0:64, :], op0=ALU.mult, op1=ALU.add),
                           reads=[B_UN[0], B_UN[1], B_LAM], writes=[B_OST])
                    o2 = ost()
                    sq = ws16(o_nt, 1024).rearrange("p (c n) -> p c n", c=2)
                    rs = ws32(o_nt + 1024, 512)
                    op("act", lambda e: e.activation(out=sq[:, 0, :], in_=o2[:, 0, :], func=AF.Square),
                       reads=[B_OST], writes=[B_NT])
                    m = mbank()
                    mm(PS[m][:, :], cbv("blockdiag"), sq[:, 0, :], True, True, reads=[B_NT, B_CONST], writes=[B_PS[m]])
                    op("act", lambda e, m=m: e.activation(out=rs, in_=PS[m][:, :], func=AF.Ln, scale=1.0 / 64.0,
                                                          bias=cfv("epsc")),
                       reads=[B_PS[m], B_CONST], writes=[B_NT])
                    op("act", lambda e: e.activation(out=rs, in_=rs, func=AF.Exp, scale=-0.5), reads=[B_NT], writes=[B_NT])
                    op("dve", lambda e: e.tensor_scalar(out=rs, in0=rs, scalar1=post, scalar2=None, op0=ALU.mult),
                       reads=[B_NT], writes=[B_NT])
                    gm = GT[:, l * 24 + 16 + 4 + pair:l * 24 + 16 + 4 + pair + 1]
                    mi = nxt("mix", 2)
                    mix = ws16(o_mix + mi * 1024, 1024).rearrange("p (c n) -> p c n", c=2)
                    op("dve", lambda e, mix=mix, gm=gm: e.scalar_tensor_tensor(out=mix[:, 0, :], in0=o2[:, 0, :], scalar=gm,
                                                                                 in1=rs, op0=ALU.mult, op1=ALU.mult),
                       reads=[B_OST, B_NT, B_CONST], writes=[B_MIX[mi]])
                    for j in range(4):
                        tt = 4 * qb + j
                        for cbk in range(2):
                            m = mbank()
                            mm(PS[m][:, :], mix[:, 0, j * 128:(j + 1) * 128], wo[:, 0, cbk * 512:(cbk + 1) * 512], True, True,
                               reads=[B_MIX[mi], B_WO], writes=[B_PS[m]])
                            op("dve", lambda e, tt=tt, cbk=cbk, m=m: e.tensor_tensor(
                                out=XS[:, tt, cbk * 512:(cbk + 1) * 512], in0=PS[m][:, :],
                                in1=XS[:, tt, cbk * 512:(cbk + 1) * 512], op=ALU.add),
                               reads=[B_PS[m], B_X[tt]], writes=[B_X[tt]])
                kb.barrier()

        def mixer_sb(l):
            load_wout(l, 0)

            def evac_v(tt, ps, bps):
                evac_copy(vst(tt), ps, reads=[bps], writes=[B_VST[tt]])
            proj_tm(l, TM_VA[0], 256, evac_v)
            proj_fm(l, QA0, 0)
            proj_fm(l, QA1, 1)
            proj_fm(l, KA0, 2)
            proj_fm(l, KA1, 3)
            U8 = cbv("u8")
            O8 = cbv("ones8")
            for qb in range(NQB):
                for pair in range(2):
                    obs = [obank(), obank()]
                    R32 = [ws32(o_un + i * 1024, 512) for i in range(2)]
                    RB = [ws16(o_un + 2048 + i * 512, 512) for i in range(2)]
                    E32 = [ws32(o_un + 3072 + i * 1024, 512) for i in range(2)]
                    bR32 = [B_UN[0], B_UN[1]]
                    bRB = [B_UN[2], B_UN[3]]
                    bE = [B_UN[4], B_UN[5]]
                    for half in range(2):
                        op("pool", lambda e, half=half: e.memset(R32[half], 0.0), writes=[bR32[half]])
                    kts = list(range(4 * qb + 3, -1, -1))
                    for idx, kt in enumerate(kts):
                        jlo = max(0, kt - 4 * qb)
                        dj = jlo if kt >= 4 * qb else None
                        c0, c1 = jlo * 128, 512
                        for half in range(2):
                            z = zbank()
                            zt = PS[z]
                            q = fmo(pair)[half * 64:(half + 1) * 64, qb * 512:(qb + 1) * 512]
                            k = fmo(2 + pair)[half * 64:(half + 1) * 64, kt * 128:(kt + 1) * 128]
                            mm(zt[:, c0:c1], k, q[:, c0:c1], True, False, reads=[B_FMO[pair][qb], B_FMO[2 + pair][kt // 4]],
                               writes=[B_PS[z]])
                            if dj is not None:
                                mm(zt[:, dj * 128:(dj + 1) * 128], IDENT, cbv("mstrict"), False, False, reads=[B_CONST],
                                   writes=[B_PS[z]])
                            e32 = E32[half]
                            op("act", lambda e, e32=e32, zt=zt, c0=c0, c1=c1: e.activation(
                                out=e32[:, c0:c1], in_=zt[:, c0:c1], func=AF.Exp, scale=SC64),
                               reads=[B_PS[z]], writes=[bE[half]])
                            p = nxt("pt", 4)
                            sp = ws16(o_pt + p * 512, 512)
                            op("act", lambda e, e32=e32, sp=sp, c0=c0, c1=c1: e.activation(
                                out=sp[:, c0:c1], in_=e32[:, c0:c1], func=AF.Ln, bias=cfv("one")),
                               reads=[bE[half], B_CONST], writes=[B_PT[p]])
                            last_u = (idx == 0)
                            mm(zt[:, c0:c1], U8, sp[:, c0:c1], False, last_u, reads=[B_PT[p], B_CONST], writes=[B_PS[z]])
                            if idx > 0:
                                mm(zt[:, :], O8, RB[half], False, True, reads=[bRB[half], B_CONST], writes=[B_PS[z]])
                            p2 = nxt("pt", 4)
                            at = ws16(o_pt + p2 * 512, 512)
                            op("act", lambda e, at=at, zt=zt, c0=c0, c1=c1: e.activation(
                                out=at[:, c0:c1], in_=zt[:, c0:c1], func=AF.Exp, scale=SC64),
                               reads=[B_PS[z]], writes=[B_PT[p2]])
                            mm(PS[obs[half]][:, c0:c1], vst(kt)[:, pair * 128:(pair + 1) * 128], at[:, c0:c1], idx == 0,
                               idx == len(kts) - 1, reads=[B_PT[p2], B_VST[kt]], writes=[B_PS[obs[half]]])
                            if idx < len(kts) - 1:
                                op("pool", lambda e, half=half, sp=sp, c0=c0, c1=c1: e.tensor_tensor(
                                    out=R32[half][:, c0:c1], in0=R32[half][:, c0:c1], in1=sp[:, c0:c1], op=ALU.add),
                                   reads=[B_PT[p], bR32[half]], writes=[bR32[half]])
                                op("pool", lambda e, half=half: e.tensor_copy(out=RB[half], in_=R32[half]),
                                   reads=[bR32[half]], writes=[bRB[half]])
                    for half in range(2):
                        dst = ost()[half * 64:(half + 1) * 64, pair, :]
                        op("dve", lambda e, dst=dst, half=half, obs=obs: e.tensor_copy(
                            out=dst, in_=PS[obs[half]][half * 64:(half + 1) * 64, :]),
                           reads=[B_PS[obs[half]]], writes=[B_OST])
                group_norm_and_project(l, 0, qb)

        def mixer_nsa(l):
            load_wout(l, 1)
            vones()

            def evac_v(tt, ps, bps):
                v = vst(tt).rearrange("p (b c) -> p b c", b=2)
                evac_copy(v[:, :, 0:64], ps.rearrange("p (b c) -> p b c", b=2), reads=[bps], writes=[B_VST[tt]])
            proj_tm(l, TM_VS[0], 128, evac_v)
            proj_fm(l, QN0, 0)
            proj_fm(l, QN1, 1)
            proj_fm(l, KVC, 2)
            wgn = ws16(o_wgn, 1024).rearrange("p (k n) -> p k n", k=8)
            r0 = (l * NFM + GN) * 128
            dma(C_WGN, wgn, wfm_b[r0:r0 + 128, :].rearrange("p (k n) -> p k n", k=8), writes=[B_WGN])
            blk = ws16(o_un, 1016).rearrange("p (a i) -> p a i", a=8)
            gel = [ws16(o_un + 1024, 128), ws16(o_un + 1152, 128)]
            KC = ws16(o_un + 1280, 128)
            VC = ws16(o_un + 1408, 128)
            w2 = ws16(o_un + 3584, 192)
            C_W2 = C_W2X
            dma(C_W2, w2, w2_b[l * 128:(l + 1) * 128, :], writes=[B_UN[7]])
            hb = [mbank(), mbank()]
            kvc = fmo(2)
            for quarter in range(4):
                s = nxt("wfm", 3)
                w1q = ws16(o_wfm + s * 1024, 1024).rearrange("p (a n) -> p a n", a=8)
                dma(C_WFM[s], w1q, w1_b[l * 128:(l + 1) * 128, quarter * 1024:(quarter + 1) * 1024].rearrange(
                    "p (a n) -> p a n", a=8), writes=[B_WFM[s]])
                src = _ap(kvc, kvc.offset + 8 * quarter, [kvc.ap[0], [1, 8], [16, 127]])
                pe = PET[:, l * 32 + 8 * quarter:l * 32 + 8 * quarter + 8].unsqueeze(2).broadcast_to([128, 8, 127])
                op("dve", lambda e, src=src, pe=pe: e.tensor_tensor(out=blk, in0=src, in1=pe, op=ALU.add),
                   reads=[B_FMO[2][0], B_FMO[2][1], B_FMO[2][2], B_FMO[2][3], B_CONST], writes=[B_UN[0]])
                for a in range(8):
                    pidx = quarter * 8 + a
                    for kv in range(2):
                        mm(PS[hb[kv]][:, 0:127], w1q[kv * 64:(kv + 1) * 64, a, :], blk[kv * 64:(kv + 1) * 64, a, :],
                           pidx == 0, pidx == 31, reads=[B_WFM[s], B_UN[0]], writes=[B_PS[hb[kv]]])
            tmpa = ws32(o_un + 3840, 512)
            for kv in range(2):
                xh = tmpa[:, 256:383]
                ta = tmpa[:, 0:127]
                tb_ = tmpa[:, 128:255]
                op("dve", lambda e, xh=xh, kv=kv: e.tensor_copy(out=xh, in_=PS[hb[kv]][:, 0:127]),
                   reads=[B_PS[hb[kv]]], writes=[B_UN[8]])
                op("dve", lambda e, xh=xh, ta=ta: e.tensor_tensor(out=ta, in0=xh, in1=xh, op=ALU.mult),
                   reads=[B_UN[8]], writes=[B_UN[8]])
                op("dve", lambda e, ta=ta: e.tensor_scalar(out=ta, in0=ta, scalar1=0.044715, scalar2=1.0, op0=ALU.mult, op1=ALU.add),
                   reads=[B_UN[8]], writes=[B_UN[8]])
                op("dve", lambda e, xh=xh, ta=ta: e.tensor_tensor(out=ta, in0=xh, in1=ta, op=ALU.mult),
                   reads=[B_UN[8]], writes=[B_UN[8]])
                op("act", lambda e, ta=ta, tb_=tb_: e.activation(out=tb_, in_=ta, func=AF.Exp, scale=-1.5957691216),
                   reads=[B_UN[8]], writes=[B_UN[9]])
                op("dve", lambda e, tb_=tb_: e.tensor_scalar(out=tb_, in0=tb_, scalar1=1.0, scalar2=None, op0=ALU.add),
                   reads=[B_UN[9]], writes=[B_UN[9]])
                op("dve", lambda e, tb_=tb_: e.reciprocal(out=tb_, in_=tb_), reads=[B_UN[9]], writes=[B_UN[9]])
                op("dve", lambda e, xh=xh, tb_=tb_, kv=kv: e.tensor_tensor(out=gel[kv][:, 0:127], in0=xh, in1=tb_, op=ALU.mult),
                   reads=[B_UN[8], B_UN[9]], writes=[B_UN[1 + kv]])
            m = mbank()
            mm(PS[m][:, 0:127], w2[:, 0:128], gel[0][:, 0:127], True, True, reads=[B_UN[7], B_UN[1]], writes=[B_PS[m]])
            op("dve", lambda e, m=m: e.tensor_copy(out=KC[:, 0:127], in_=PS[m][:, 0:127]), reads=[B_PS[m]], writes=[B_UN[3]])
            m = mbank()
            mm(PS[m][0:127, 0:64], gel[1][:, 0:127], w2[:, 128:192], True, True, reads=[B_UN[7], B_UN[2]], writes=[B_PS[m]])
            op("pool", lambda e: e.memset(VC[:, 64:128], 1.0), writes=[B_UN[4]])
            op("dve", lambda e, m=m: e.tensor_copy(out=VC[0:127, 0:64], in_=PS[m][0:127, 0:64]), reads=[B_PS[m], B_UN[4]],
               writes=[B_UN[4]])
            proj_fm(l, KSD, 2)
            proj_fm(l, KWD, 3)
            gate = ws16(o_un + 1536, 512)
            selbT = ws16(o_un + 2048, 512)
            imp = ws32(o_un + 2560, 512)
            for qb in range(NQB):
                m = mbank()
                for kc in range(8):
                    mm(PS[m][0:12, :], wgn[:, kc, 0:12], HT[:, kc, qb * 512:(qb + 1) * 512], kc == 0, kc == 7,
                       reads=[B_WGN, B_HT[qb]], writes=[B_PS[m]])
                tg = tmpa
                op("act", lambda e, m=m: e.activation(out=tg[0:12, :], in_=PS[m][0:12, :], func=AF.Exp, scale=-1.0),
                   reads=[B_PS[m]], writes=[B_UN[8]])
                op("dve", lambda e: e.tensor_scalar(out=tg[0:12, :], in0=tg[0:12, :], scalar1=1.0, scalar2=None, op0=ALU.add),
                   reads=[B_UN[8]], writes=[B_UN[8]])
                op("dve", lambda e: e.reciprocal(out=tg[0:12, :], in_=tg[0:12, :]), reads=[B_UN[8]], writes=[B_UN[8]])
                op("dve", lambda e: e.tensor_copy(out=gate[0:12, :], in_=tg[0:12, :]), reads=[B_UN[8]], writes=[B_UN[5]])
                nk = min(127, 32 * qb + 31)
                cmp_norm = []
                for h in range(4):
                    pair, half = h // 2, h % 2
                    z = zbank()
                    zt = PS[z]
                    q = fmo(pair)[half * 64:(half + 1) * 64, qb * 512:(qb + 1) * 512]
                    mm(zt[0:nk, :], KC[half * 64:(half + 1) * 64, 0:nk], q, True, False, reads=[B_FMO[pair][qb], B_UN[3]],
                       writes=[B_PS[z]])
                    if USE_ROW[h]:
                        mm(zt[0:nk, :], cbv("rowsel", 0, 12, h * 128, h * 128 + nk), cbv("rrow", 0, 12), False, False,
                           reads=[B_CONST], writes=[B_PS[z]])
                    mm(zt[0:nk, :], cbv("ident", 0, 128, 0, nk), cbv("mc", 0, 128, qb * 512, (qb + 1) * 512), False, True,
                       reads=[B_CONST], writes=[B_PS[z]])
                    p = nxt("pt", 4)
                    pt = ws16(o_pt + p * 512, 512)
                    bias = cfv("biasc" if USE_ROW[h] else "biascm", 0, nk, h * 4 + qb, h * 4 + qb + 1)
                    op("act", lambda e, pt=pt, zt=zt, bias=bias, nk=nk: e.activation(
                        out=pt[0:nk, :], in_=zt[0:nk, :], func=AF.Exp, scale=SC64, bias=bias),
                       reads=[B_PS[z], B_CONST], writes=[B_PT[p]])
                    ob = obank()
                    mm(PS[ob][:, :], VC[0:nk, :], pt[0:nk, :], True, True, reads=[B_PT[p], B_UN[4]], writes=[B_PS[ob]])
                    m = mbank()
                    mm(PS[m][0:64, :], cbv("gaug", 0, nk), pt[0:nk, :], True, True, reads=[B_PT[p], B_CONST], writes=[B_PS[m]])
                    r = nxt("rd", 2)
                    rd = ws32(o_rd + r * 1024, 512)
                    op("dve", lambda e, rd=rd, ob=ob: e.tensor_scalar(out=rd[0:64, :], in0=PS[ob][64:128, :], scalar1=1e-30,
                                                                      scalar2=None, op0=ALU.max),
                       reads=[B_PS[ob]], writes=[B_RD[r]])
                    op("dve", lambda e, rd=rd: e.reciprocal(out=rd[0:64, :], in_=rd[0:64, :]), reads=[B_RD[r]], writes=[B_RD[r]])
                    if h == 0:
                        op("dve", lambda e, rd=rd, m=m: e.tensor_tensor(out=imp[0:32, :], in0=PS[m][0:32, :], in1=rd[0:32, :],
                                                                        op=ALU.mult),
                           reads=[B_PS[m], B_RD[r]], writes=[B_UN[6]])
                    else:
                        tq_ = tmpa
                        op("dve", lambda e, rd=rd, m=m: e.tensor_tensor(out=tq_[0:32, :], in0=PS[m][0:32, :], in1=rd[0:32, :],
                                                                        op=ALU.mult),
                           reads=[B_PS[m], B_RD[r]], writes=[B_UN[8]])
                        op("dve", lambda e: e.tensor_tensor(out=imp[0:32, :], in0=imp[0:32, :], in1=tq_[0:32, :], op=ALU.add),
                           reads=[B_UN[8], B_UN[6]], writes=[B_UN[6]])
                    gm_ = mbank()
                    mm(PS[gm_][0:64, :], cbv("gatesel", 0, 12, (h * 3 + 0) * 64, (h * 3 + 1) * 64), gate[0:12, :], True, True,
                       reads=[B_UN[5], B_CONST], writes=[B_PS[gm_]])
                    wgt = ws32(o_un + 4864, 512)
                    op("dve", lambda e, rd=rd, wgt=wgt, gm_=gm_: e.tensor_tensor(out=wgt[0:64, :], in0=PS[gm_][0:64, :],
                                                                               in1=rd[0:64, :], op=ALU.mult),
                       reads=[B_PS[gm_], B_RD[r]], writes=[B_UN[9]])
                    dst = ost()[half * 64:(half + 1) * 64, pair, :]
                    op("dve", lambda e, dst=dst, ob=ob, wgt=wgt: e.tensor_tensor(out=dst, in0=PS[ob][0:64, :], in1=wgt[0:64, :],
                                                                               op=ALU.mult),
                       reads=[B_PS[ob], B_UN[9]], writes=[B_OST])
                for j in range(4):
                    tt = 4 * qb + j
                    m = mbank()
                    op("pe", lambda e, m=m, j=j: e.transpose(out=PS[m][:, 0:32], in_=imp[0:32, j * 128:(j + 1) * 128],
                                                             identity=cfv("identf", 0, 32, 0, 32)),
                       reads=[B_UN[6], B_CONST], writes=[B_PS[m]])
                    ta = tmpa[:, 0:32]
                    t8 = tmpa[:, 32:40]
                    tsel = tmpa[:, 64:96]
                    cand = cbv("cand", 0, 128, tt * 32, (tt + 1) * 32)
                    candm1 = cbv("candm1", 0, 128, tt * 32, (tt + 1) * 32)
                    forced = cbv("forced", 0, 128, tt * 32, (tt + 1) * 32)
                    op("dve", lambda e, m=m, cand=cand: e.tensor_tensor(out=ta, in0=PS[m][:, 0:32], in1=cand, op=ALU.mult),
                       reads=[B_PS[m], B_CONST], writes=[B_UN[8]])
                    op("dve", lambda e, candm1=candm1: e.tensor_tensor(out=ta, in0=ta, in1=candm1, op=ALU.add),
                       reads=[B_UN[8], B_CONST], writes=[B_UN[8]])
                    op("dve", lambda e: e.max(out=t8, in_=ta), reads=[B_UN[8]], writes=[B_UN[9]])
                    op("dve", lambda e: e.tensor_scalar(out=tsel, in0=ta, scalar1=t8[:, 4:5], scalar2=None, op0=ALU.is_ge),
                       reads=[B_UN[8], B_UN[9]], writes=[B_UN[10]])
                    op("dve", lambda e, cand=cand: e.tensor_tensor(out=tsel, in0=tsel, in1=cand, op=ALU.mult),
                       reads=[B_UN[10], B_CONST], writes=[B_UN[10]])
                    op("dve", lambda e, forced=forced: e.tensor_tensor(out=tsel, in0=tsel, in1=forced, op=ALU.add),
                       reads=[B_UN[10], B_CONST], writes=[B_UN[10]])
                    op("dve", lambda e: e.tensor_scalar(out=tsel, in0=tsel, scalar1=-1.0, scalar2=-NEG, op0=ALU.add, op1=ALU.mult),
                       reads=[B_UN[10]], writes=[B_UN[10]])
                    m2 = mbank()
                    op("pe", lambda e, m2=m2: e.transpose(out=PS[m2][0:32, 0:128], in_=tsel, identity=cfv("identf")),
                       reads=[B_UN[10], B_CONST], writes=[B_PS[m2]])
                    op("dve", lambda e, m2=m2, j=j: e.tensor_copy(out=selbT[0:32, j * 128:(j + 1) * 128], in_=PS[m2][0:32, 0:128]),
                       reads=[B_PS[m2]], writes=[B_UN[11]])
                for h in range(4):
                    pair, half = h // 2, h % 2
                    for br in (1, 2):
                        ob = obank()
                        kslot = 2 if br == 1 else 3

                        def k_lhsT(kt, kslot=kslot, half=half):
                            return fmo(kslot)[half * 64:(half + 1) * 64, kt * 128:(kt + 1) * 128]

                        def k_reads(kt, kslot=kslot):
                            return [B_FMO[kslot][kt // 4]]

                        def v_lhsT(kt, br=br):
                            return vst(kt)[:, (br - 1) * 128:br * 128]

                        def v_reads(kt):
                            return [B_VST[kt]]
                        extra = None
                        if br == 1:
                            def extra(kt, zt, c0, c1):
                                return [(zt[:, c0:c1], cbv("esel", 0, 32, kt * 128, (kt + 1) * 128), selbT[0:32, c0:c1],
                                         [B_UN[11], B_CONST])]
                        softmax_map(qb, tiles_for(qb, None if br == 1 else 4, CUT[h]), pair, half * 64, k_lhsT, k_reads, h, SC64,
                                    v_lhsT, v_reads, ob, extra=extra)
                        r = nxt("rd", 2)
                        rd = ws32(o_rd + r * 1024, 512)
                        op("dve", lambda e, rd=rd, ob=ob: e.reciprocal(out=rd[0:64, :], in_=PS[ob][64:128, :]),
                           reads=[B_PS[ob]], writes=[B_RD[r]])
                        gm_ = mbank()
                        mm(PS[gm_][0:64, :], cbv("gatesel", 0, 12, (h * 3 + br) * 64, (h * 3 + br + 1) * 64), gate[0:12, :],
                           True, True, reads=[B_UN[5], B_CONST], writes=[B_PS[gm_]])
                        op("dve", lambda e, rd=rd, gm_=gm_: e.tensor_tensor(out=rd[0:64, :], in0=PS[gm_][0:64, :], in1=rd[0:64, :],
                                                                            op=ALU.mult),
                           reads=[B_PS[gm_], B_RD[r]], writes=[B_RD[r]])
                        tq_ = tmpa[half * 64:(half + 1) * 64, :]
                        op("dve", lambda e, rd=rd, ob=ob, tq_=tq_: e.tensor_tensor(out=tq_, in0=PS[ob][0:64, :], in1=rd[0:64, :],
                                                                                   op=ALU.mult),
                           reads=[B_PS[ob], B_RD[r]], writes=[B_UN[8]])
                        dst = ost()[half * 64:(half + 1) * 64, pair, :]
                        op("dve", lambda e, dst=dst, tq_=tq_: e.tensor_tensor(out=dst, in0=dst, in1=tq_, op=ALU.add),
                           reads=[B_UN[8], B_OST], writes=[B_OST])
                group_norm_and_project(l, 1, qb)

        def ffn_phase(l):
            norm_to_ht(l * 24 + 8, o_hn2, o_junk2)
            ut = ws16(o_ut, 16 * 512).rearrange("p (f n) -> p f n", f=16)
            rq = [0, 0]
            for tb in range(NQB):
                for hf in range(2):
                    for blk4 in range(4):
                        blk = hf * 4 + blk4
                        s = rq[0]
                        rq[0] = (s + 1) % 2
                        wup = ws16(o_wup + s * 4096, 4096).rearrange("p (k n) -> p k n", k=8)
                        r0 = (l * 8 + blk) * 128
                        dma(C_WUP[s], wup, wup_b[r0:r0 + 128, :].rearrange("p (k n) -> p k n", k=8), writes=[B_WUP[s]])
                        for f4 in range(4):
                            fl = blk4 * 4 + f4
                            m = mbank()
                            for kc in range(8):
                                mm(PS[m][:, :], wup[:, kc, f4 * 128:(f4 + 1) * 128], HT[:, kc, tb * 512:(tb + 1) * 512],
                                   kc == 0, kc == 7, reads=[B_WUP[s], B_HT[tb]], writes=[B_PS[m]])
                            ri = nxt("rl", 2)
                            rl = ws32(o_rl + ri * 1024, 512)
                            op("act", lambda e, rl=rl, m=m: e.activation(out=rl, in_=PS[m][:, :], func=AF.Relu),
                               reads=[B_PS[m]], writes=[B_RL[ri]])
                            op("pool", lambda e, rl=rl, fl=fl: e.tensor_tensor(out=ut[:, fl, :], in0=rl, in1=rl, op=ALU.mult),
                               reads=[B_RL[ri]], writes=[B_UT])
                    for cbk in range(2):
                        slots = []
                        for g2 in range(2):
                            s = rq[1]
                            rq[1] = (s + 1) % 4
                            wdn = ws16(o_wdn + s * 4096, 4096).rearrange("p (f n) -> p f n", f=8)
                            grp = hf * 2 + g2
                            r0 = (l * 8 + cbk * 4 + grp) * 128
                            dma(C_WDN[s], wdn, wdn_b[r0:r0 + 128, :].rearrange("p (f n) -> p f n", f=8), writes=[B_WDN[s]])
                            slots.append((s, wdn))
                        for j in range(4):
                            tt = 4 * tb + j
                            m = mbank()
                            for g2 in range(2):
                                s, wdn = slots[g2]
                                for f8 in range(8):
                                    fl = g2 * 8 + f8
                                    mm(PS[m][:, :], ut[:, fl, j * 128:(j + 1) * 128], wdn[:, f8, :], fl == 0, fl == 15,
                                       reads=[B_UT, B_WDN[s]], writes=[B_PS[m]])
                            op("dve", lambda e, tt=tt, cbk=cbk, m=m: e.tensor_tensor(
                                out=XS[:, tt, cbk * 512:(cbk + 1) * 512], in0=PS[m][:, :],
                                in1=XS[:, tt, cbk * 512:(cbk + 1) * 512], op=ALU.add),
                               reads=[B_PS[m], B_X[tt]], writes=[B_X[tt]])

        def final_norm(si):
            nf = ws32(o_nf, 1024)
            dma(C_MISC, nf, nf_d[:, :], writes=[B_NF])
            junk = ws16(o_junk2, 1024)
            for tb in range(NQB):
                for j in range(4):
                    tt = 4 * tb + j
                    op("act", lambda e, tt=tt: e.activation(out=junk, in_=XS[:, tt, :], func=AF.Square,
                                                            accum_out=SSQ[:, tt:tt + 1]),
                       reads=[B_X[tt]], writes=[B_JUNK, B_SSQ[tb]])
                sl = slice(4 * tb, 4 * tb + 4)
                op("act", lambda e, sl=sl: e.activation(out=RSTD[:, sl], in_=SSQ[:, sl], func=AF.Ln,
                                                        scale=1.0 / D, bias=cfv("epsc")),
                   reads=[B_SSQ[tb], B_CONST], writes=[B_RSTD[tb]])
                op("act", lambda e, sl=sl: e.activation(out=RSTD[:, sl], in_=RSTD[:, sl], func=AF.Exp, scale=-0.5),
                   reads=[B_RSTD[tb]], writes=[B_RSTD[tb]])
                for j in range(4):
                    tt = 4 * tb + j
                    fi = 0
                    fin = ws32(o_fin, 1024)
                    op("dve", lambda e, tt=tt, fin=fin: e.scalar_tensor_tensor(
                        out=fin, in0=XS[:, tt, :], scalar=RSTD[:, tt:tt + 1], in1=nf, op0=ALU.mult, op1=ALU.mult),
                       reads=[B_X[tt], B_RSTD[tb], B_NF], writes=[B_FIN[fi]])
                    dma(C_FIN[fi], out_d[si, tt * 128:(tt + 1) * 128, :], fin, reads=[B_FIN[fi]])

        for si in range(nseq):
            for q4 in range(4):
                dma(C_X[q4], XS[:, 4 * q4:4 * q4 + 4, :],
                    x_d[si, q4 * 512:(q4 + 1) * 512, :].rearrange("(t p) d -> p t d", p=128),
                    writes=[B_X[4 * q4 + i] for i in range(4)])
            for l in range(nlayer):
                norm_to_ht(l * 24, o_hn, o_junk)
                for mx in mixers:
                    if mx == "sb":
                        mixer_sb(l)
                    elif mx == "nsa":
                        mixer_nsa(l)
                    elif mx == "diff":
                        mixer_diff2(l)
                    elif mx == "swa":
                        mixer_swa(l)
                    kb.barrier()
                if ffn:
                    ffn_phase(l)
                    kb.barrier()
            kb.barrier()
            final_norm(si)
            kb.barrier()

        build_program.last_counts = {k: len(v) for k, v in kb.prog.items()}
        @block.tensor
        def _(e):
            for f in kb.prog["pe"]:
                f(e)

        @block.scalar
        def _(e):
            for f in kb.prog["act"]:
                f(e)

        @block.vector
        def _(e):
            for f in kb.prog["dve"]:
                f(e)

        @block.gpsimd
        def _(e):
            for f in kb.prog["pool"]:
                f(e)

        @block.sync
        def _(e):
            for f in kb.prog["sp"]:
                f(e)
    return nc, (cbarr, cfarr)


def prep_weights(inp, nlayer=DEPTH):
    w_in = np.asarray(inp["w_in"], np.float32)
    L = nlayer
    wfm = np.zeros((L, NFM, 1024, 128), np.float32)

    def cols(a, b):
        return w_in[:L, :, a:b]
    wfm[:, QA0] = cols(0, 128)
    wfm[:, QA1] = cols(128, 256)
    wfm[:, KA0] = cols(256, 384)
    wfm[:, KA1] = cols(384, 512)
    wfm[:, QN0] = cols(768, 896)
    wfm[:, QN1] = cols(896, 1024)
    wfm[:, KVC] = cols(1024, 1152)
    wfm[:, KSD, :, 0:64] = cols(1152, 1216)
    wfm[:, KSD, :, 64:128] = cols(1152, 1216)
    wfm[:, KWD, :, 0:64] = cols(1280, 1344)
    wfm[:, KWD, :, 64:128] = cols(1280, 1344)
    wfm[:, GN, :, 0:12] = cols(1408, 1420)
    wfm[:, QC0] = cols(1420, 1548)
    wfm[:, QC1] = cols(1548, 1676)
    kc0 = 1676
    for pair in range(2):
        for half in range(2):
            h = 2 * pair + half
            base = kc0 + h * 64
            wfm[:, KC00 + 2 * pair, :, half * 64:half * 64 + 32] = cols(base, base + 32)
            wfm[:, KC10 + 2 * pair, :, half * 64 + 32:half * 64 + 64] = cols(base + 32, base + 64)
    wfm[:, QD0] = cols(2188, 2316)
    wfm[:, QD1] = cols(2316, 2444)
    wfm[:, KD0, :, 0:64] = cols(2444, 2508)
    wfm[:, KD0, :, 64:128] = cols(2444, 2508)
    wfm[:, KD1, :, 0:64] = cols(2508, 2572)
    wfm[:, KD1, :, 64:128] = cols(2508, 2572)
    wfm = wfm.reshape(L, NFM, 8, 128, 128).transpose(0, 1, 3, 2, 4).reshape(L * NFM * 128, 1024)

    wtm = np.concatenate([cols(512, 768), cols(1216, 1280), cols(1344, 1408), cols(1932, 2188), cols(2572, 2700)], axis=2)
    wtm = wtm.reshape(L, 8, 128, NTM).transpose(0, 2, 1, 3).reshape(L * 128, 8 * NTM)

    w_out = np.asarray(inp["w_out"], np.float32)[:L]
    wout = w_out.reshape(L, 4, 2, 128, 1024).transpose(0, 1, 3, 2, 4).reshape(L * 4 * 128, 2048)
    w_up = np.asarray(inp["w_up"], np.float32)[:L]
    wup = w_up.reshape(L, 8, 128, 8, 512).transpose(0, 3, 2, 1, 4).reshape(L * 8 * 128, 4096)
    w_dn = np.asarray(inp["w_down"], np.float32)[:L]
    wdn = w_dn.reshape(L, 4, 8, 128, 2, 512).transpose(0, 4, 1, 3, 2, 5).reshape(L * 8 * 128, 4096)
    w1k = np.asarray(inp["cmp_w1_k"], np.float32)[:L].reshape(L, 32, 64, 128).transpose(0, 2, 1, 3)
    w1v = np.asarray(inp["cmp_w1_v"], np.float32)[:L].reshape(L, 32, 64, 128).transpose(0, 2, 1, 3)
    w1 = np.concatenate([w1k, w1v], axis=1).reshape(L * 128, 4096)
    w2k = np.asarray(inp["cmp_w2_k"], np.float32)[:L]
    w2v = np.asarray(inp["cmp_w2_v"], np.float32)[:L]
    w2 = np.concatenate([w2k, w2k, w2v], axis=2).reshape(L * 128, 192)
    gt = np.zeros((128, L * 24), np.float32)
    for l in range(L):
        gt[:, l * 24 + 0:l * 24 + 8] = np.asarray(inp["norm_attn"], np.float32)[l].reshape(8, 128).T
        gt[:, l * 24 + 8:l * 24 + 16] = np.asarray(inp["norm_mlp"], np.float32)[l].reshape(8, 128).T
        gt[:, l * 24 + 16:l * 24 + 24] = np.asarray(inp["g_mix"], np.float32)[l].reshape(8, 128).T
    lamv = np.zeros((128, L * 128), np.float32)
    for l in range(L):
        for i, k in enumerate(("diff_lq1", "diff_lk1", "diff_lq2", "diff_lk2")):
            lamv[:, l * 128 + i * 32:l * 128 + (i + 1) * 32] = np.asarray(inp[k], np.float32)[l][None, :]
    sinkt = np.ascontiguousarray(np.asarray(inp["sinks"], np.float32)[:L].T)
    pet = np.zeros((128, L * 32), np.float32)
    for l in range(L):
        pet[0:64, l * 32:(l + 1) * 32] = np.asarray(inp["cmp_pe_k"], np.float32)[l].T
        pet[64:128, l * 32:(l + 1) * 32] = np.asarray(inp["cmp_pe_v"], np.float32)[l].T
    normf = np.ascontiguousarray(np.broadcast_to(np.asarray(inp["norm_final"], np.float32)[None, :], (128, 1024)))
    return dict(wfm=np.ascontiguousarray(wfm), wtm=np.ascontiguousarray(wtm), wout=np.ascontiguousarray(wout),
                wup=np.ascontiguousarray(wup), wdn=np.ascontiguousarray(wdn), w1=np.ascontiguousarray(w1),
                w2=np.ascontiguousarray(w2), gt=gt, lamv=lamv, sinkt=sinkt, pet=pet, normf=normf)


_CACHE = {}


def kernel(**inputs):
    x = np.asarray(inputs["x"], np.float32)
    ncores = 8
    nseq = x.shape[0] // ncores
    if "prog" not in _CACHE:
        _CACHE["prog"] = build_program(nseq=nseq)
    nc, (cbarr, cfarr) = _CACHE["prog"]
    w = prep_weights(inputs)
    in_maps = []
    for c in range(ncores):
        in_maps.append(core_inputs(w, x[c * nseq:(c + 1) * nseq], cbarr, cfarr, c))
    res = run_bass_kernel_spmd(nc, in_maps, core_ids=list(range(ncores)))
    out = np.concatenate([np.asarray(r["out"]) for r in res.results], axis=0)
    return out.astype(np.float32)


def core_inputs(w, xs, cbarr, cfarr, c):
    big = ("wfm", "wtm", "wout", "wup", "wdn", "w1")
    if True:
        m = dict(w)
        for k in big:
            a = np.empty((w[k].shape[0] + 1, w[k].shape[1]), np.float32)
            a[:-1] = w[k]
            a[-1] = float(c)
            m[k] = a
        cb = np.empty((129, cbarr.shape[1]), cbarr.dtype)
        cb[:128] = cbarr
        cb[128] = float(c)
        m["cb16"] = cb
        m["x"] = np.ascontiguousarray(xs)
        m["cf32"] = cfarr
    return m
```

```python
import math
import numpy as np
import ml_dtypes
import concourse.bass as bass
import concourse.mybir as mybir
from concourse.bass_utils import run_bass_kernel_spmd

F32 = mybir.dt.float32
BF16 = mybir.dt.bfloat16
AF = mybir.ActivationFunctionType
ALU = mybir.AluOpType
AX = mybir.AxisListType

S = 2048
D = 1024
NT = 16
NQB = 4
DEPTH = 4
DFF = 4096
EPS = 1e-6
NEG = -30000.0
NFM = 20
NTM = 768
SC64 = 64 ** -0.5
SC32 = 32 ** -0.5

QA0, QA1, KA0, KA1, QN0, QN1, KVC, KSD, KWD, GN, QC0, QC1, KC00, KC10, KC01, KC11, QD0, QD1, KD0, KD1 = range(20)
TM_VA, TM_VS, TM_VW, TM_VC, TM_VD = (0, 256), (256, 320), (320, 384), (384, 640), (640, 768)


def _slopes():
    m = 2.0 ** (-8.0 * np.arange(1, 13) / 12.0)
    m = m.astype(np.float32).reshape(4, 3)
    return m[:, 0].copy(), m[:, 1].copy(), m[:, 2].copy()


class _Cols:
    def __init__(self):
        self.off = {}
        self.n = 0
        self.parts = []

    def add(self, name, arr):
        arr = np.asarray(arr, dtype=np.float32)
        assert arr.shape[0] <= 128
        a = np.zeros((128, arr.shape[1]), np.float32)
        a[:arr.shape[0]] = arr
        self.off[name] = (self.n, arr.shape[1])
        self.n += arr.shape[1]
        self.parts.append(a)

    def build(self):
        return np.concatenate(self.parts, axis=1)


def _alibi_heads():
    sn, sd, ss = _slopes()
    slopes = np.concatenate([sn, sd, ss]).astype(np.float64)
    scales = np.array([SC64] * 4 + [SC32] * 4 + [SC64] * 4)
    return slopes, scales


USE_ROW = [True, False, False, False, True, False, False, False, True, False, False, False]
CUT = [1, 3, 13, None, 2, 5, None, None, None, None, None, None]


def _build_consts():
    slopes, scales = _alibi_heads()
    cb = _Cols()
    p = np.arange(128)
    cb.add("ident", np.eye(128))
    cb.add("ones", np.ones((128, 128)))
    cb.add("u8", np.where(p[:, None] >= p[None, :], -8.0, 0.0))
    cb.add("ones8", np.full((128, 128), -8.0))
    cb.add("mdiag", np.where(p[:, None] <= p[None, :], 0.0, NEG))
    cb.add("mstrict", np.where(p[:, None] < p[None, :], 0.0, NEG))
    cb.add("manti", np.where(p[:, None] > p[None, :], 0.0, NEG))
    sel = np.zeros((12, 12 * 128))
    for h in range(12):
        sel[h, h * 128:(h + 1) * 128] = 1.0
    cb.add("rowsel", sel)
    tq = np.arange(512)
    rrow = np.stack([-(slopes[h] * tq) / scales[h] for h in range(12)], 0)
    rrow_bf = rrow.astype(np.float32).astype(ml_dtypes.bfloat16).astype(np.float32)
    cb.add("rrow", rrow_bf)
    i = np.arange(127)
    t = np.arange(S)
    cb.add("mc", np.where((16 * i[:, None] + 31) <= t[None, :], 0.0, NEG))
    g = np.zeros((127, 64))
    g[i, i // 4] = 1.0
    g[:, 32:64] = 1.0
    cb.add("gaug", g)
    m = np.arange(S)
    cb.add("esel", (m[None, :] // 64 == np.arange(32)[:, None]).astype(np.float32))
    cand = np.zeros((128, 16, 32))
    forced = np.zeros((128, 16, 32))
    j = np.arange(32)
    for tt in range(16):
        cur = (tt * 128 + p) // 64
        cand[:, tt, :] = ((j[None, :] >= 1) & (j[None, :] <= cur[:, None] - 2))
        forced[:, tt, :] = ((j[None, :] == 0) | (j[None, :] == cur[:, None]) | (j[None, :] == cur[:, None] - 1))
    cb.add("cand", cand.reshape(128, 512))
    cb.add("candm1", cand.reshape(128, 512) - 1.0)
    cb.add("forced", forced.reshape(128, 512))
    gs = np.zeros((12, 12 * 64))
    for k in range(12):
        gs[k, k * 64:(k + 1) * 64] = 1.0
    cb.add("gatesel", gs)
    bd = np.zeros((128, 128))
    bd[:64, :64] = 1.0
    bd[64:, 64:] = 1.0
    cb.add("blockdiag", bd)
    ss = np.zeros((4, 4 * 128))
    for h in range(4):
        ss[h, h * 128 + 64:(h + 1) * 128] = 1.0
    cb.add("sinksel", ss)
    cbarr = cb.build()

    cf = _Cols()
    cf.add("identf", np.eye(128))
    bp = np.zeros((128, 12 * 19))
    for h in range(12):
        for r in range(19):
            bp[:, h * 19 + r] = slopes[h] * ((r - 15) * 128 + p)
    cf.add("biasp", bp)
    cf.add("biasm", bp - 256.0 * np.repeat(slopes[:, None], 19, axis=1).reshape(1, 12 * 19))
    bc = np.zeros((128, 16))
    for h in range(4):
        for qb in range(4):
            bc[:, h * 4 + qb] = slopes[h] * (16 * p + 31 - 512 * qb)
    cf.add("biasc", bc)
    bcm = bc.copy()
    for h in range(4):
        bcm[:, h * 4:(h + 1) * 4] -= 256.0 * slopes[h]
    cf.add("biascm", bcm)
    er = np.stack([(scales[8 + h] * rrow_bf[8 + h] + slopes[8 + h] * tq) if USE_ROW[8 + h]
                   else slopes[8 + h] * (tq - 256.0) for h in range(4)], 0)
    cf.add("epsrow", er)
    cf.add("epsc", np.full((128, 1), EPS))
    cf.add("one", np.full((128, 1), 1.0))
    cfarr = cf.build()
    return cb.off, cbarr.astype(ml_dtypes.bfloat16), cf.off, cfarr.astype(np.float32)


class _Rec:
    def __init__(self):
        self.calls = []

    def __getattr__(self, name):
        def f(*a, **k):
            self.calls.append((name, a, k))
            return self
        return f


class Buf:
    __slots__ = ("name", "w", "r")

    def __init__(self, name):
        self.name = name
        self.w = None
        self.r = {}


class Chan:
    def __init__(self, sem):
        self.sem = sem
        self.count = 0


class KB:
    CE = ("pe", "act", "dve", "pool")

    def __init__(self, nc, sems):
        self.nc = nc
        self.sems = list(sems)
        self.streams = ("pe", "act", "dve", "pool", "sp")
        self.prog = {k: [] for k in self.streams}
        self.ccount = {k: 0 for k in self.CE}
        self.csem = {k: self.sems.pop() for k in self.CE}
        self.waited = {k: {} for k in self.streams}
        self.chans = []
        self.bufs = []

    def buf(self, name):
        b = Buf(name)
        self.bufs.append(b)
        return b

    def chan(self):
        c = Chan(self.sems.pop())
        self.chans.append(c)
        return c

    def _sem(self, k):
        return self.csem[k] if isinstance(k, str) else k.sem

    def _waits(self, stream, reads, writes, dma):
        need = {}
        for b in reads:
            if b.w is not None and b.w[1] > need.get(b.w[0], 0):
                need[b.w[0]] = b.w[1]
        for b in writes:
            if b.w is not None and b.w[1] > need.get(b.w[0], 0):
                need[b.w[0]] = b.w[1]
            for k, v in b.r.items():
                if v > need.get(k, 0):
                    need[k] = v
        out = []
        wd = self.waited[stream]
        for k, v in need.items():
            if k == "pe" and stream == "pe":
                continue
            if dma is not None and k is dma:
                continue
            if wd.get(k, 0) >= v:
                continue
            wd[k] = v
            out.append((self._sem(k), v))
        return out

    def op(self, stream, fn, reads=(), writes=(), dma=None):
        waits = self._waits(stream, reads, writes, dma)
        if dma is None:
            self.ccount[stream] += 1
            key, val, sem, inc = stream, self.ccount[stream], self.csem[stream], 1
        else:
            dma.count += 16
            key, val, sem, inc = dma, dma.count, dma.sem, 16
        for b in reads:
            if b.r.get(key, 0) < val:
                b.r[key] = val
        for b in writes:
            b.w = (key, val)
            b.r = {}

        rec = _Rec()
        fn(rec)
        assert len(rec.calls) == 1
        name, a, k = rec.calls[0]

        def run(e, waits=waits, name=name, a=a, k=k, sem=sem, inc=inc):
            for (sm, v) in waits:
                e.wait_ge(sm, v)
            try:
                inst = getattr(e, name)(*a, **k)
            except Exception:
                print("FAILED OP", name, k.keys(), [(kk, getattr(v, 'shape', v), getattr(v, 'dtype', None)) for kk, v in k.items()])
                raise
            inst.then_inc(sem, inc)
        self.prog[stream].append(run)

    def barrier(self):
        for st in self.streams:
            waits = []
            wd = self.waited[st]
            for k in self.CE:
                if k == "pe" and st == "pe":
                    continue
                v = self.ccount[k]
                if v > wd.get(k, 0):
                    wd[k] = v
                    waits.append((self.csem[k], v))
            for c in self.chans:
                if c.count > wd.get(c, 0):
                    wd[c] = c.count
                    waits.append((c.sem, c.count))

            def run(e, waits=waits):
                for (sm, v) in waits:
                    e.wait_ge(sm, v)
            self.prog[st].append(run)
        for b in self.bufs:
            b.w = None
            b.r = {}


def _ap(t, offset, pattern):
    return bass.AP(tensor=t.tensor, offset=offset, ap=[list(x) for x in pattern])


def build_program(nseq=4, nlayer=DEPTH, mixers=("sb", "nsa", "diff", "swa"), ffn=True, dbg=False):
    cboff, cbarr, cfoff, cfarr = _build_consts()
    NCB = cbarr.shape[1]
    NCF = cfarr.shape[1]
    sn, sd, ssw = _slopes()

    nc = bass.Bass("TRN2", target_bir_lowering=False)
    x_d = nc.dram_tensor("x", [nseq, S, D], F32, kind="ExternalInput").ap()
    out_d = nc.dram_tensor("out", [nseq, S, D], F32, kind="ExternalOutput").ap()
    cb_d = nc.dram_tensor("cb16", [129, NCB], BF16, kind="ExternalInput").ap()[0:128, :]
    cf_d = nc.dram_tensor("cf32", [128, NCF], F32, kind="ExternalInput").ap()
    wfm_d = nc.dram_tensor("wfm", [nlayer * NFM * 128 + 1, 1024], F32, kind="ExternalInput").ap()[0:nlayer * NFM * 128, :]
    wtm_d = nc.dram_tensor("wtm", [nlayer * 128 + 1, 8 * NTM], F32, kind="ExternalInput").ap()[0:nlayer * 128, :]
    wout_d = nc.dram_tensor("wout", [nlayer * 4 * 128 + 1, 2048], F32, kind="ExternalInput").ap()[0:nlayer * 4 * 128, :]
    wup_d = nc.dram_tensor("wup", [nlayer * 8 * 128 + 1, 4096], F32, kind="ExternalInput").ap()[0:nlayer * 8 * 128, :]
    wdn_d = nc.dram_tensor("wdn", [nlayer * 8 * 128 + 1, 4096], F32, kind="ExternalInput").ap()[0:nlayer * 8 * 128, :]
    w1_d = nc.dram_tensor("w1", [nlayer * 128 + 1, 4096], F32, kind="ExternalInput").ap()[0:nlayer * 128, :]
    w2_d = nc.dram_tensor("w2", [nlayer * 128, 192], F32, kind="ExternalInput").ap()
    gt_d = nc.dram_tensor("gt", [128, nlayer * 24], F32, kind="ExternalInput").ap()
    lam_d = nc.dram_tensor("lamv", [128, nlayer * 128], F32, kind="ExternalInput").ap()
    sink_d = nc.dram_tensor("sinkt", [4, nlayer], F32, kind="ExternalInput").ap()
    pe_d = nc.dram_tensor("pet", [128, nlayer * 32], F32, kind="ExternalInput").ap()
    nf_d = nc.dram_tensor("normf", [128, 1024], F32, kind="ExternalInput").ap()
    dbg16_d = nc.dram_tensor("dbg16", [128, 4096], BF16, kind="ExternalOutput").ap() if dbg else None
    dbg32_d = nc.dram_tensor("dbg32", [128, 2048], F32, kind="ExternalOutput").ap() if dbg else None
    wfm_b = nc.dram_tensor("wfm_b", [nlayer * NFM * 128, 1024], BF16, kind="Internal").ap()
    wtm_b = nc.dram_tensor("wtm_b", [nlayer * 128, 8 * NTM], BF16, kind="Internal").ap()
    wout_b = nc.dram_tensor("wout_b", [nlayer * 4 * 128, 2048], BF16, kind="Internal").ap()
    wup_b = nc.dram_tensor("wup_b", [nlayer * 8 * 128, 4096], BF16, kind="Internal").ap()
    wdn_b = nc.dram_tensor("wdn_b", [nlayer * 8 * 128, 4096], BF16, kind="Internal").ap()
    w1_b = nc.dram_tensor("w1_b", [nlayer * 128, 4096], BF16, kind="Internal").ap()
    w2_b = nc.dram_tensor("w2_b", [nlayer * 128, 192], BF16, kind="Internal").ap()

    import contextlib
    es = contextlib.ExitStack()
    with es:
        def sb(name, shape, dt):
            return es.enter_context(nc.sbuf_tensor(name, shape, dt))

        XS = sb("XS", [128, NT, D], F32)
        HT = sb("HT", [128, 8, S], BF16)
        CB = sb("CB", [128, NCB], BF16)
        CF = sb("CF", [128, NCF], F32)
        GT = sb("GT", [128, nlayer * 24], F32)
        LAMV = sb("LAMV", [128, nlayer * 128], F32)
        LAMS = sb("LAMS", [128, 4 * nlayer], F32)
        LTMP = sb("LTMP", [128, 64], F32)
        SINKT = sb("SINKT", [4, nlayer], F32)
        PET = sb("PET", [128, nlayer * 32], F32)
        SSQ = sb("SSQ", [128, 16], F32)
        RSTD = sb("RSTD", [128, 16], F32)
        WS = sb("WS", [128, 40960], BF16)
        PS = [es.enter_context(nc.psum_tensor("ps%d" % i, [128, 512], F32)) for i in range(8)]
        sems = [es.enter_context(nc.semaphore("s%d" % i)) for i in range(60)]
        block = es.enter_context(nc.Block())

        kb = KB(nc, sems)
        op = kb.op

        class Carver:
            def __init__(self):
                self.o = 0

            def take(self, nbf16):
                o = self.o
                self.o += nbf16
                assert self.o <= 40960, self.o
                return o

        def ws16(o, n):
            return WS[:, o:o + n]

        def ws32(o, n):
            return WS[:, o:o + 2 * n].bitcast(F32)

        def cbv(name, r0=0, r1=128, c0=0, c1=None):
            o, n = cboff[name]
            if c1 is None:
                c1 = n
            return CB[r0:r1, o + c0:o + c1]

        def cfv(name, r0=0, r1=128, c0=0, c1=None):
            o, n = cfoff[name]
            if c1 is None:
                c1 = n
            return CF[r0:r1, o + c0:o + c1]

        cv = Carver()
        o_fmo = cv.take(5 * S)
        o_vst = cv.take(16 * 256)
        o_wfm = cv.take(3 * 1024)
        o_wtm = cv.take(8 * 256)
        o_wgn = cv.take(1024)
        o_pt = cv.take(4 * 512)
        o_un = cv.take(6144)
        o_ost = cv.take(2 * 2 * 512)
        o_nt = cv.take(2048)
        o_mix = cv.take(2 * 2 * 512)
        o_wo = cv.take(2 * 1024)
        o_rd = cv.take(2 * 2 * 512)
        o_hn = cv.take(1024)
        o_junk = cv.take(1024)
        att_end = cv.o
        cv2 = Carver()
        o_ut = cv2.take(16 * 512)
        o_wup = cv2.take(2 * 8 * 512)
        o_wdn = cv2.take(4 * 8 * 512)
        o_rl = cv2.take(2 * 2 * 512)
        o_hn2 = cv2.take(1024)
        o_junk2 = cv2.take(1024)
        o_fin = cv2.take(2 * 1024)
        o_nf = cv2.take(2 * 1024)

        B_X = [kb.buf("x%d" % i) for i in range(NT)]
        B_HT = [kb.buf("ht%d" % i) for i in range(NQB)]
        B_PS = [kb.buf("ps%d" % i) for i in range(8)]
        B_CONST = kb.buf("const")
        B_SSQ = [kb.buf("ssq%d" % i) for i in range(4)]
        B_RSTD = [kb.buf("rstd%d" % i) for i in range(4)]
        B_HN = kb.buf("hn")
        B_JUNK = kb.buf("junk")
        B_FMO = [[kb.buf("fmo%d_%d" % (i, j)) for j in range(NQB)] for i in range(5)]
        B_VST = [kb.buf("vst%d" % i) for i in range(NT)]
        B_WFM = [kb.buf("wfm%d" % i) for i in range(3)]
        C_WFM = [kb.chan() for _ in range(3)]
        B_WTM = kb.buf("wtm")
        C_WTM = kb.chan()
        B_WGN = kb.buf("wgn")
        C_WGN = kb.chan()
        C_W2X = kb.chan()
        B_PT = [kb.buf("pt%d" % i) for i in range(4)]
        B_OST = kb.buf("ost")
        B_NT = kb.buf("nt")
        B_MIX = [kb.buf("mix%d" % i) for i in range(2)]
        B_WO = kb.buf("wo")
        C_WO = kb.chan()
        B_RD = [kb.buf("rd%d" % i) for i in range(2)]
        B_UN = [kb.buf("un%d" % i) for i in range(12)]
        B_UT = kb.buf("ut")
        B_WUP = [kb.buf("wup%d" % i) for i in range(2)]
        C_WUP = [kb.chan() for _ in range(2)]
        B_WDN = [kb.buf("wdn%d" % i) for i in range(4)]
        C_WDN = [kb.chan() for _ in range(4)]
        B_RL = [kb.buf("rl%d" % i) for i in range(2)]
        B_FIN = [kb.buf("fin%d" % i) for i in range(2)]
        C_FIN = [kb.chan() for _ in range(2)]
        B_NF = kb.buf("nf")
        C_X = [kb.chan() for _ in range(4)]
        C_MISC = kb.chan()
        C_CASTS = [kb.chan() for _ in range(4)]
        B_LAM = kb.buf("lam")
        B_SINK = kb.buf("sink")

        ring = {"z": 0, "o": 0, "m": 0, "pt": 0, "wfm": 0, "rd": 0, "mix": 0, "rl": 0}
        ZB, OB, MB = (0, 1, 2), (3, 4), (5, 6, 7)

        def nxt(kind, n):
            v = ring[kind]
            ring[kind] = (v + 1) % n
            return v

        def zbank():
            return ZB[nxt("z", 3)]

        def obank():
            return OB[nxt("o", 2)]

        def mbank():
            return MB[nxt("m", 3)]

        def mm(out, lhsT, rhs, start, stop, reads, writes, skip=False):
            op("pe", lambda e: e.matmul(out, lhsT=lhsT, rhs=rhs, start=start, stop=stop, skip_group_check=skip),
               reads=reads, writes=writes)

        def dma(chan, out, in_, reads=(), writes=(), stream="sp"):
            op(stream, lambda e: e.dma_start(out=out, in_=in_), reads=reads, writes=writes, dma=chan)

        IDENT = cbv("ident")
        ONESB = cbv("ones")
        C_DBG = kb.chan()
        dbg_state = {}

        def tap16(name, src, reads, c0=0):
            if dbg and dbg == name and name not in dbg_state:
                dbg_state[name] = 1
                n = src.shape[-1]
                dma(C_DBG, dbg16_d[0:src.shape[0], c0:c0 + n], src, reads=reads)

        def tap32(name, src, reads, c0=0):
            if dbg and dbg == name and name not in dbg_state:
                dbg_state[name] = 1
                n = src.shape[-1]
                dma(C_DBG, dbg32_d[0:src.shape[0], c0:c0 + n], src, reads=reads)

        dma(C_MISC, CB[:], cb_d[:, :], writes=[B_CONST])
        dma(C_MISC, CF[:], cf_d[:, :], writes=[B_CONST])
        dma(C_MISC, GT[:], gt_d[:, :], writes=[B_CONST])
        dma(C_MISC, LAMV[:], lam_d[:, :], writes=[B_CONST])
        dma(C_MISC, SINKT[:], sink_d[:, :], writes=[B_CONST])
        dma(C_MISC, PET[:], pe_d[:, :], writes=[B_CONST])

        B_WB = kb.buf("wcast")

        cast_i = [0]

        def cast_dma(o_, i_):
            ch = C_CASTS[cast_i[0] % 4]
            cast_i[0] += 1
            if ch.count > 0:
                kb.prog["pool"].append(lambda e, sem=ch.sem, v=ch.count: e.wait_ge(sem, v))
            dma(ch, o_, i_, writes=[B_WB], stream="pool")

        def cast_all(src, dst):
            rows, cols = src.shape
            step = 128
            for r0 in range(0, rows, step):
                r1 = min(rows, r0 + step)
                if cols <= 2048:
                    cast_dma(dst[r0:r1, :], src[r0:r1, :])
                else:
                    for c0 in range(0, cols, 2048):
                        c1 = min(cols, c0 + 2048)
                        cast_dma(dst[r0:r1, c0:c1], src[r0:r1, c0:c1])

        for (a, b) in ((wfm_d, wfm_b), (wtm_d, wtm_b), (wout_d, wout_b), (w1_d, w1_b), (w2_d, w2_b),
                       (wup_d, wup_b), (wdn_d, wdn_b)):
            cast_all(a, b)

        for l in range(nlayer):
            lam_init = 0.8 - 0.6 * math.exp(-0.3 * l)
            lv = LAMV[:, l * 128:(l + 1) * 128]
            op("dve", lambda e, lv=lv: e.tensor_tensor(out=LTMP[:, 0:32], in0=lv[:, 0:32], in1=lv[:, 32:64], op=ALU.mult),
               reads=[B_CONST], writes=[B_LAM])
            op("dve", lambda e, l=l: e.tensor_reduce(out=LAMS[:, 4 * l + 2:4 * l + 3], in_=LTMP[:, 0:32], axis=AX.X, op=ALU.add),
               reads=[B_LAM], writes=[B_LAM])
            op("dve", lambda e, lv=lv: e.tensor_tensor(out=LTMP[:, 32:64], in0=lv[:, 64:96], in1=lv[:, 96:128], op=ALU.mult),
               reads=[B_CONST, B_LAM], writes=[B_LAM])
            op("dve", lambda e, l=l: e.tensor_reduce(out=LAMS[:, 4 * l + 3:4 * l + 4], in_=LTMP[:, 32:64], axis=AX.X, op=ALU.add),
               reads=[B_LAM], writes=[B_LAM])
            op("act", lambda e, l=l: e.activation(out=LAMS[:, 4 * l + 2:4 * l + 4], in_=LAMS[:, 4 * l + 2:4 * l + 4], func=AF.Exp),
               reads=[B_LAM], writes=[B_LAM])
            op("dve", lambda e, l=l: e.tensor_tensor(out=LAMS[:, 4 * l:4 * l + 1], in0=LAMS[:, 4 * l + 2:4 * l + 3],
                                                      in1=LAMS[:, 4 * l + 3:4 * l + 4], op=ALU.subtract),
               reads=[B_LAM], writes=[B_LAM])
            op("dve", lambda e, l=l, li=lam_init: e.tensor_scalar(out=LAMS[:, 4 * l + 1:4 * l + 2], in0=LAMS[:, 4 * l:4 * l + 1],
                                                                   scalar1=li, scalar2=-1.0, op0=ALU.add, op1=ALU.mult),
               reads=[B_LAM], writes=[B_LAM])
        kb.barrier()

        def norm_to_ht(g0, o_hn_, o_junk_):
            hn = ws16(o_hn_, 1024)
            junk = ws16(o_junk_, 1024)
            for tb in range(NQB):
                for j in range(4):
                    tt = 4 * tb + j
                    op("act", lambda e, tt=tt: e.activation(out=junk, in_=XS[:, tt, :], func=AF.Square,
                                                            accum_out=SSQ[:, tt:tt + 1]),
                       reads=[B_X[tt]], writes=[B_JUNK, B_SSQ[tb]])
                sl = slice(4 * tb, 4 * tb + 4)
                op("act", lambda e, sl=sl: e.activation(out=RSTD[:, sl], in_=SSQ[:, sl], func=AF.Ln,
                                                        scale=1.0 / D, bias=cfv("epsc")),
                   reads=[B_SSQ[tb], B_CONST], writes=[B_RSTD[tb]])
                op("act", lambda e, sl=sl: e.activation(out=RSTD[:, sl], in_=RSTD[:, sl], func=AF.Exp, scale=-0.5),
                   reads=[B_RSTD[tb]], writes=[B_RSTD[tb]])
                for j in range(4):
                    tt = 4 * tb + j
                    op("dve", lambda e, tt=tt: e.tensor_scalar(out=hn, in0=XS[:, tt, :], scalar1=RSTD[:, tt:tt + 1],
                                                               scalar2=None, op0=ALU.mult),
                       reads=[B_X[tt], B_RSTD[tb]], writes=[B_HN])
                    m = mbank()
                    psb = PS[m][:].bitcast(BF16)
                    for c in range(8):
                        op("pe", lambda e, c=c, psb=psb: e.transpose(out=psb[:, c * 128:(c + 1) * 128],
                                                                     in_=hn[:, c * 128:(c + 1) * 128], identity=IDENT),
                           reads=[B_HN, B_CONST], writes=[B_PS[m]])
                    gain = GT[:, g0:g0 + 8].unsqueeze(2).broadcast_to([128, 8, 128])
                    op("dve", lambda e, tt=tt, psb=psb, gain=gain: e.tensor_tensor(
                        out=HT[:, :, tt * 128:(tt + 1) * 128],
                        in0=psb.rearrange("p (c n) -> p c n", c=8), in1=gain, op=ALU.mult),
                       reads=[B_PS[m], B_CONST], writes=[B_HT[tb]])

        evac_flip = [0]

        def evac_copy(out, in_, reads, writes):
            evac_flip[0] ^= 1
            if evac_flip[0]:
                op("dve", lambda e: e.tensor_copy(out=out, in_=in_), reads=reads, writes=writes)
            else:
                op("act", lambda e: e.activation(out=out, in_=in_, func=AF.Copy), reads=reads, writes=writes)

        def load_wfm(l, chunk):
            s = nxt("wfm", 3)
            w = ws16(o_wfm + s * 1024, 1024).rearrange("p (k n) -> p k n", k=8)
            r0 = (l * NFM + chunk) * 128
            dma(C_WFM[s], w, wfm_b[r0:r0 + 128, :].rearrange("p (k n) -> p k n", k=8), writes=[B_WFM[s]])
            return w, B_WFM[s]

        def fmo(slot):
            return ws16(o_fmo + slot * S, S)

        def proj_fm(l, chunk, slot, M=128):
            w, bw = load_wfm(l, chunk)
            for tb in range(NQB):
                m = mbank()
                for kc in range(8):
                    mm(PS[m][0:M, :], w[:, kc, 0:M], HT[:, kc, tb * 512:(tb + 1) * 512], kc == 0, kc == 7,
                       reads=[bw, B_HT[tb]], writes=[B_PS[m]])
                evac_copy(fmo(slot)[0:M, tb * 512:(tb + 1) * 512], PS[m][0:M, :], reads=[B_PS[m]], writes=[B_FMO[slot][tb]])

        def vst(tt):
            return WS[:, o_vst + tt * 256:o_vst + (tt + 1) * 256]

        def proj_tm(l, c0, ncols, evac):
            w = ws16(o_wtm, 8 * 256).rearrange("p (k n) -> p k n", k=8)
            src = wtm_b[l * 128:(l + 1) * 128, :].rearrange("p (k n) -> p k n", k=8)
            dma(C_WTM, w[:, :, 0:ncols], src[:, :, c0:c0 + ncols], writes=[B_WTM])
            for tt in range(NT):
                m = mbank()
                for kc in range(8):
                    mm(PS[m][:, 0:ncols], HT[:, kc, tt * 128:(tt + 1) * 128], w[:, kc, 0:ncols], kc == 0, kc == 7,
                       reads=[B_WTM, B_HT[tt // 4]], writes=[B_PS[m]])
                evac(tt, PS[m][:, 0:ncols], B_PS[m])

        def vones():
            v = WS[:, o_vst:o_vst + 16 * 256].rearrange("p (t b c) -> p t b c", t=16, b=2)
            for b in range(2):
                op("pool", lambda e, b=b: e.memset(v[:, :, b, 64:128], 1.0), writes=B_VST)

        def tiles_for(qb, wt, cut=None):
            res = []
            lo = 0 if wt is None else max(0, 4 * qb - wt)
            for kt in range(lo, 4 * qb + 4):
                jlo, jhi, dj, aj = None, None, None, None
                for j in range(4):
                    delta = 4 * qb + j - kt
                    if delta < 0 or (wt is not None and delta > wt):
                        continue
                    if cut is not None and delta > cut:
                        continue
                    if jlo is None:
                        jlo = j
                    jhi = j
                    if delta == 0:
                        dj = j
                    if wt is not None and delta == wt:
                        aj = j
                if jlo is not None:
                    res.append((kt, jlo, jhi, dj, aj))
            return res

        def softmax_map(qb, tiles, qslot, qbase, k_lhsT, k_reads, ah, scale, v_lhsT, v_reads, ob, extra=None,
                        diag_name="mdiag"):
            q = fmo(qslot)[qbase:qbase + 64, qb * 512:(qb + 1) * 512]
            rowsel = cbv("rowsel", 0, 12, ah * 128, (ah + 1) * 128)
            rrow = cbv("rrow", 0, 12)
            n = len(tiles)
            for idx, (kt, jlo, jhi, dj, aj) in enumerate(tiles):
                c0, c1 = jlo * 128, (jhi + 1) * 128
                z = zbank()
                zt = PS[z]
                steps = [(zt[:, c0:c1], k_lhsT(kt), q[:, c0:c1], [B_FMO[qslot][qb]] + k_reads(kt))]
                if USE_ROW[ah]:
                    steps.append((zt[:, c0:c1], rowsel, rrow[:, c0:c1], [B_CONST]))
                if extra is not None:
                    steps += extra(kt, zt, c0, c1)
                if dj is not None:
                    steps.append((zt[:, dj * 128:(dj + 1) * 128], IDENT, cbv(diag_name), [B_CONST]))
                if aj is not None:
                    steps.append((zt[:, aj * 128:(aj + 1) * 128], IDENT, cbv("manti"), [B_CONST]))
                for si, (o_, l_, r_, rd_) in enumerate(steps):
                    mm(o_, l_, r_, si == 0, si == len(steps) - 1, reads=rd_, writes=[B_PS[z]])
                p = nxt("pt", 4)
                pt = ws16(o_pt + p * 512, 512)
                rel = kt - 4 * qb + 15
                bias = cfv("biasp" if USE_ROW[ah] else "biasm", 0, 128, ah * 19 + rel, ah * 19 + rel + 1)
                op("act", lambda e, pt=pt, zt=zt, c0=c0, c1=c1, bias=bias: e.activation(
                    out=pt[:, c0:c1], in_=zt[:, c0:c1], func=AF.Exp, scale=scale, bias=bias),
                   reads=[B_PS[z], B_CONST], writes=[B_PT[p]])
                tap16("pt", pt, [B_PT[p]], 0)
                mm(PS[ob][:, c0:c1], v_lhsT(kt), pt[:, c0:c1], idx == 0, idx == n - 1,
                   reads=[B_PT[p]] + v_reads(kt), writes=[B_PS[ob]], skip=True)

        def ost():
            return ws32(o_ost, 1024).rearrange("p (c n) -> p c n", c=2)

        def normalize_to_ost(ob, c, half, first=True, wmul=None):
            r = nxt("rd", 2)
            rd = ws32(o_rd + r * 1024, 512)
            op("dve", lambda e: e.reciprocal(out=rd[0:64, :], in_=PS[ob][64:128, :]), reads=[B_PS[ob]], writes=[B_RD[r]])
            dst = ost()[half * 64:(half + 1) * 64, c, :]
            op("dve", lambda e: e.tensor_tensor(out=dst, in0=PS[ob][0:64, :], in1=rd[0:64, :], op=ALU.mult),
               reads=[B_PS[ob], B_RD[r]], writes=[B_OST])

        def group_norm_and_project(l, g, qb, headwise=False, post=1.0):
            o2 = ost()
            sq = ws16(o_nt, 1024).rearrange("p (c n) -> p c n", c=2)
            rs = ws32(o_nt + 1024, 512)
            for c in range(2):
                op("act", lambda e, c=c: e.activation(out=sq[:, c, :], in_=o2[:, c, :], func=AF.Square),
                   reads=[B_OST], writes=[B_NT])
            gm = GT[:, l * 24 + 16 + 2 * g:l * 24 + 16 + 2 * g + 2]
            mi = nxt("mix", 2)
            mix = ws16(o_mix + mi * 1024, 1024).rearrange("p (c n) -> p c n", c=2)
            nfeat = 64.0 if headwise else 256.0
            if not headwise:
                m = mbank()
                for c in range(2):
                    mm(PS[m][:, :], ONESB, sq[:, c, :], c == 0, c == 1, reads=[B_NT, B_CONST], writes=[B_PS[m]])
                tap16("sq", ws16(o_nt, 1024), [B_NT], 0)
                if dbg == "ssq" and "ssq" not in dbg_state:
                    tmpd = ws32(o_un + 1024, 512)
                    op("dve", lambda e, m=m: e.tensor_copy(out=tmpd, in_=PS[m][:, :]), reads=[B_PS[m]], writes=[B_UN[2]])
                    tap32("ssq", tmpd, [B_UN[2]], 0)
                op("act", lambda e: e.activation(out=rs, in_=PS[m][:, :], func=AF.Ln, scale=1.0 / nfeat, bias=cfv("epsc")),
                   reads=[B_PS[m], B_CONST], writes=[B_NT])
                if dbg == "lnv":
                    tap32("lnv", rs, [B_NT], 0)
                op("act", lambda e: e.activation(out=rs, in_=rs, func=AF.Exp, scale=-0.5), reads=[B_NT], writes=[B_NT])
                for c in range(2):
                    op("dve", lambda e, c=c: e.scalar_tensor_tensor(out=mix[:, c, :], in0=o2[:, c, :], scalar=gm[:, c:c + 1],
                                                                    in1=rs, op0=ALU.mult, op1=ALU.mult),
                       reads=[B_OST, B_NT, B_CONST], writes=[B_MIX[mi]])
            else:
                for c in range(2):
                    m = mbank()
                    mm(PS[m][:, :], cbv("blockdiag"), sq[:, c, :], True, True, reads=[B_NT, B_CONST], writes=[B_PS[m]])
                    op("act", lambda e, m=m: e.activation(out=rs, in_=PS[m][:, :], func=AF.Ln, scale=1.0 / nfeat,
                                                          bias=cfv("epsc")),
                       reads=[B_PS[m], B_CONST], writes=[B_NT])
                    op("act", lambda e: e.activation(out=rs, in_=rs, func=AF.Exp, scale=-0.5), reads=[B_NT], writes=[B_NT])
                    op("dve", lambda e: e.tensor_scalar(out=rs, in0=rs, scalar1=post, scalar2=None, op0=ALU.mult),
                       reads=[B_NT], writes=[B_NT])
                    op("dve", lambda e, c=c: e.scalar_tensor_tensor(out=mix[:, c, :], in0=o2[:, c, :], scalar=gm[:, c:c + 1],
                                                                    in1=rs, op0=ALU.mult, op1=ALU.mult),
                       reads=[B_OST, B_NT, B_CONST], writes=[B_MIX[mi]])
            tap32("rs", rs, [B_NT], 0)
            tap16("mix", ws16(o_mix + mi * 1024, 1024), [B_MIX[mi]], 0)
            tap16("wo", ws16(o_wo, 2048), [B_WO], 0)
            wo = ws16(o_wo, 2048).rearrange("p (c n) -> p c n", c=2)
            for j in range(4):
                tt = 4 * qb + j
                for cbk in range(2):
                    m = mbank()
                    for c in range(2):
                        mm(PS[m][:, :], mix[:, c, j * 128:(j + 1) * 128], wo[:, c, cbk * 512:(cbk + 1) * 512], c == 0, c == 1,
                           reads=[B_MIX[mi], B_WO], writes=[B_PS[m]])
                    op("dve", lambda e, tt=tt, cbk=cbk, m=m: e.tensor_tensor(
                        out=XS[:, tt, cbk * 512:(cbk + 1) * 512], in0=PS[m][:, :], in1=XS[:, tt, cbk * 512:(cbk + 1) * 512],
                        op=ALU.add), reads=[B_PS[m], B_X[tt]], writes=[B_X[tt]])

        def load_wout(l, g):
            wo = ws16(o_wo, 2048).rearrange("p (c n) -> p c n", c=2)
            r0 = (l * 4 + g) * 128
            dma(C_WO, wo, wout_b[r0:r0 + 128, :].rearrange("p (c n) -> p c n", c=2), writes=[B_WO])

        def mixer_swa(l):
            load_wout(l, 3)
            vones()

            def evac_v(tt, ps, bps):
                v = vst(tt).rearrange("p (b c) -> p b c", b=2)
                evac_copy(v[:, :, 0:64], ps.rearrange("p (b c) -> p b c", b=2), reads=[bps], writes=[B_VST[tt]])
            proj_tm(l, TM_VD[0], 128, evac_v)
            proj_fm(l, QD0, 0)
            proj_fm(l, QD1, 1)
            proj_fm(l, KD0, 2)
            proj_fm(l, KD1, 3)
            tap16("fmo", fmo(0), [B_FMO[0][i] for i in range(4)], 0)
            dbg_state.pop("fmo", None)
            tap16("fmo", fmo(2), [B_FMO[2][i] for i in range(4)], 2048)
            tap16("vst", WS[:, o_vst:o_vst + 4096], B_VST, 0)
            sinkrow = ws16(o_un, 512)
            op("act", lambda e: e.activation(out=sinkrow[0:4, :], in_=cfv("epsrow", 0, 4), func=AF.Exp,
                                             bias=SINKT[0:4, l:l + 1]),
               reads=[B_CONST], writes=[B_UN[0]])
            for qb in range(NQB):
                tiles = tiles_for(qb, 1)
                for h in range(4):
                    pair, half = h // 2, h % 2
                    kslot = 2 + pair
                    ob = obank()

                    def k_lhsT(kt, kslot=kslot, half=half):
                        return fmo(kslot)[half * 64:(half + 1) * 64, kt * 128:(kt + 1) * 128]

                    def k_reads(kt, kslot=kslot):
                        return [B_FMO[kslot][kt // 4]]

                    def v_lhsT(kt, pair=pair):
                        return vst(kt)[:, pair * 128:(pair + 1) * 128]

                    def v_reads(kt):
                        return [B_VST[kt]]
                    softmax_map(qb, tiles, pair, half * 64, k_lhsT, k_reads, 8 + h, SC64, v_lhsT, v_reads, ob)
                    mm(PS[ob][:, :], cbv("sinksel", 0, 4, h * 128, (h + 1) * 128), sinkrow[0:4, :], False, True,
                       reads=[B_UN[0], B_CONST], writes=[B_PS[ob]], skip=True)
                    if dbg == "psob" and "psob" not in dbg_state:
                        tmpd = ws32(o_un + 1024, 512)
                        op("dve", lambda e, ob=ob: e.tensor_copy(out=tmpd, in_=PS[ob][:, :]), reads=[B_PS[ob]], writes=[B_UN[2]])
                        tap32("psob", tmpd, [B_UN[2]], 0)
                    normalize_to_ost(ob, pair, half)
                tap32("ost", ws32(o_ost, 1024), [B_OST], 0)
                group_norm_and_project(l, 3, qb)

        def mixer_diff2(l):
            lam_init = 0.8 - 0.6 * math.exp(-0.3 * l)
            post = 1.0 - lam_init
            neglam = LAMS[:, 4 * l + 1:4 * l + 2]
            for pair in range(2):
                wo = ws16(o_wo, 2048).rearrange("p (c n) -> p c n", c=2)
                r0 = (l * 4 + 2) * 128
                src = wout_b[r0:r0 + 128, :].rearrange("p (c n) -> p c n", c=2)
                dma(C_WO, wo[:, 0, :], src[:, pair, :], writes=[B_WO])
                vones()

                def evac_v(tt, ps, bps):
                    v = vst(tt).rearrange("p (b c) -> p b c", b=2)
                    evac_copy(v[:, :, 0:64], ps.rearrange("p (b c) -> p b c", b=2), reads=[bps], writes=[B_VST[tt]])
                proj_tm(l, TM_VC[0] + pair * 128, 128, evac_v)
                proj_fm(l, QC0 + pair, 0)
                proj_fm(l, KC00 + 2 * pair, 1)
                proj_fm(l, KC10 + 2 * pair, 2)
                for qb in range(NQB):
                    for half in range(2):
                        h = 2 * pair + half
                        tiles = tiles_for(qb, None, CUT[4 + h])
                        obs = []
                        for cmap in range(2):
                            ob = obank()
                            obs.append(ob)
                            kslot = 1 + cmap

                            def k_lhsT(kt, kslot=kslot, half=half):
                                return fmo(kslot)[half * 64:(half + 1) * 64, kt * 128:(kt + 1) * 128]

                            def k_reads(kt, kslot=kslot):
                                return [B_FMO[kslot][kt // 4]]

                            def v_lhsT(kt, half=half):
                                return vst(kt)[:, half * 128:(half + 1) * 128]

                            def v_reads(kt):
                                return [B_VST[kt]]
                            softmax_map(qb, tiles, 0, half * 64, k_lhsT, k_reads, 4 + h, SC32, v_lhsT, v_reads, ob)
                        t0 = ws32(o_un, 512)
                        t1 = ws32(o_un + 1024, 512)
                        rds = []
                        for i in range(2):
                            r = nxt("rd", 2)
                            rd = ws32(o_rd + r * 1024, 512)
                            rds.append((r, rd))
                            op("dve", lambda e, rd=rd, ob=obs[i]: e.reciprocal(out=rd[0:64, :], in_=PS[ob][64:128, :]),
                               reads=[B_PS[obs[i]]], writes=[B_RD[r]])
                        op("dve", lambda e, obs=obs, rds=rds: e.tensor_tensor(out=t0[0:64, :], in0=PS[obs[0]][0:64, :],
                                                                              in1=rds[0][1][0:64, :], op=ALU.mult),
                           reads=[B_PS[obs[0]], B_RD[rds[0][0]]], writes=[B_UN[0]])
                        op("dve", lambda e, obs=obs, rds=rds: e.tensor_tensor(out=t1[0:64, :], in0=PS[obs[1]][0:64, :],
                                                                              in1=rds[1][1][0:64, :], op=ALU.mult),
                           reads=[B_PS[obs[1]], B_RD[rds[1][0]]], writes=[B_UN[1]])
                        dst = ost()[half * 64:(half + 1) * 64, 0, :]
                        op("dve", lambda e, dst=dst: e.scalar_tensor_tensor(out=dst, in0=t1[0:64, :], scalar=neglam[0:64, :],
                                                                            in1=t0[0:64, :], op0=ALU.mult, op1=ALU.add),
                           reads=[B_UN[0], B_UN[1], B_LAM], writes=[B_OST])
                    o2 = ost()
                    sq = ws16(o_nt, 1024).rearrange("p (c n) -> p c n", c=2)
                    rs = ws32(o_nt + 1024, 512)
                    op("act", lambda e: e.activation(out=sq[:, 0, :], in_=o2[:, 0, :], func=AF.Square),
                       reads=[B_OST], writes=[B_NT])
                    m = mbank()
                    mm(PS[m][:, :], cbv("blockdiag"), sq[:, 0, :], True, True, reads=[B_NT, B_CONST], writes=[B_PS[m]])
                    op("act", lambda e, m=m: e.activation(out=rs, in_=PS[m][:, :], func=AF.Ln, scale=1.0 / 64.0,
                                                          bias=cfv("epsc")),
                       reads=[B_PS[m], B_CONST], writes=[B_NT])
                    op("act", lambda e: e.activation(out=rs, in_=rs, func=AF.Exp, scale=-0.5), reads=[B_NT], writes=[B_NT])
                    op("dve", lambda e: e.tensor_scalar(out=rs, in0=rs, scalar1=post, scalar2=None, op0=ALU.mult),
                       reads=[B_NT], writes=[B_NT])
                    gm = GT[:, l * 24 + 16 + 4 + pair:l * 24 + 16 + 4 + pair + 1]
                    mi = nxt("mix", 2)
                    mix = ws16(o_mix + mi * 1024, 1024).rearrange("p (c n) -> p c n", c=2)
                    op("dve", lambda e, mix=mix, gm=gm: e.scalar_tensor_tensor(out=mix[:, 0, :], in0=o2[:, 0, :], scalar=gm,
                                                                                 in1=rs, op0=ALU.mult, op1=ALU.mult),
                       reads=[B_OST, B_NT, B_CONST], writes=[B_MIX[mi]])
                    for j in range(4):
                        tt = 4 * qb + j
                        for cbk in range(2):
                            m = mbank()
                            mm(PS[m][:, :], mix[:, 0, j * 128:(j + 1) * 128], wo[:, 0, cbk * 512:(cbk + 1) * 512], True, True,
                               reads=[B_MIX[mi], B_WO], writes=[B_PS[m]])
                            op("dve", lambda e, tt=tt, cbk=cbk, m=m: e.tensor_tensor(
                                out=XS[:, tt, cbk * 512:(cbk + 1) * 512], in0=PS[m][:, :],
                                in1=XS[:, tt, cbk * 512:(cbk + 1) * 512], op=ALU.add),
                               reads=[B_PS[m], B_X[tt]], writes=[B_X[tt]])
                kb.barrier()

        def mixer_sb(l):
            load_wout(l, 0)

            def evac_v(tt, ps, bps):
                evac_copy(vst(tt), ps, reads=[bps], writes=[B_VST[tt]])
            proj_tm(l, TM_VA[0], 256, evac_v)
            proj_fm(l, QA0, 0)
            proj_fm(l, QA1, 1)
            proj_fm(l, KA0, 2)
            proj_fm(l, KA1, 3)
            U8 = cbv("u8")
            O8 = cbv("ones8")
            for qb in range(NQB):
                for pair in range(2):
                    obs = [obank(), obank()]
                    R32 = [ws32(o_un + i * 1024, 512) for i in range(2)]
                    RB = [ws16(o_un + 2048 + i * 512, 512) for i in range(2)]
                    E32 = [ws32(o_un + 3072 + i * 1024, 512) for i in range(2)]
                    bR32 = [B_UN[0], B_UN[1]]
                    bRB = [B_UN[2], B_UN[3]]
                    bE = [B_UN[4], B_UN[5]]
                    for half in range(2):
                        op("pool", lambda e, half=half: e.memset(R32[half], 0.0), writes=[bR32[half]])
                    kts = list(range(4 * qb + 3, -1, -1))
                    for idx, kt in enumerate(kts):
                        jlo = max(0, kt - 4 * qb)
                        dj = jlo if kt >= 4 * qb else None
                        c0, c1 = jlo * 128, 512
                        for half in range(2):
                            z = zbank()
                            zt = PS[z]
                            q = fmo(pair)[half * 64:(half + 1) * 64, qb * 512:(qb + 1) * 512]
                            k = fmo(2 + pair)[half * 64:(half + 1) * 64, kt * 128:(kt + 1) * 128]
                            mm(zt[:, c0:c1], k, q[:, c0:c1], True, dj is None, reads=[B_FMO[pair][qb], B_FMO[2 + pair][kt // 4]],
                               writes=[B_PS[z]])
                            if dj is not None:
                                mm(zt[:, dj * 128:(dj + 1) * 128], IDENT, cbv("mstrict"), False, True, reads=[B_CONST],
                                   writes=[B_PS[z]])
                            e32 = E32[half]
                            op("act", lambda e, e32=e32, zt=zt, c0=c0, c1=c1: e.activation(
                                out=e32[:, c0:c1], in_=zt[:, c0:c1], func=AF.Exp, scale=SC64),
                               reads=[B_PS[z]], writes=[bE[half]])
                            p = nxt("pt", 4)
                            sp = ws16(o_pt + p * 512, 512)
                            op("act", lambda e, e32=e32, sp=sp, c0=c0, c1=c1: e.activation(
                                out=sp[:, c0:c1], in_=e32[:, c0:c1], func=AF.Ln, bias=cfv("one")),
                               reads=[bE[half], B_CONST], writes=[B_PT[p]])
                            last_u = (idx == 0)
                            mm(zt[:, c0:c1], U8, sp[:, c0:c1], False, True, reads=[B_PT[p], B_CONST], writes=[B_PS[z]], skip=True)
                            if idx > 0:
                                mm(zt[:, c0:c1], O8, RB[half][:, c0:c1], False, True, reads=[bRB[half], B_CONST], writes=[B_PS[z]],
                                   skip=True)
                            p2 = nxt("pt", 4)
                            at = ws16(o_pt + p2 * 512, 512)
                            op("act", lambda e, at=at, zt=zt, c0=c0, c1=c1: e.activation(
                                out=at[:, c0:c1], in_=zt[:, c0:c1], func=AF.Exp, scale=SC64),
                               reads=[B_PS[z]], writes=[B_PT[p2]])
                            mm(PS[obs[half]][:, c0:c1], vst(kt)[:, pair * 128:(pair + 1) * 128], at[:, c0:c1], idx == 0,
                               idx == len(kts) - 1, reads=[B_PT[p2], B_VST[kt]], writes=[B_PS[obs[half]]], skip=True)
                            if idx < len(kts) - 1:
                                op("pool", lambda e, half=half, sp=sp, c0=c0, c1=c1: e.tensor_tensor(
                                    out=R32[half][:, c0:c1], in0=R32[half][:, c0:c1], in1=sp[:, c0:c1], op=ALU.add),
                                   reads=[B_PT[p], bR32[half]], writes=[bR32[half]])
                                op("pool", lambda e, half=half: e.tensor_copy(out=RB[half], in_=R32[half]),
                                   reads=[bR32[half]], writes=[bRB[half]])
                    for half in range(2):
                        dst = ost()[half * 64:(half + 1) * 64, pair, :]
                        op("dve", lambda e, dst=dst, half=half, obs=obs: e.tensor_copy(
                            out=dst, in_=PS[obs[half]][half * 64:(half + 1) * 64, :]),
                           reads=[B_PS[obs[half]]], writes=[B_OST])
                group_norm_and_project(l, 0, qb)

        def mixer_nsa(l):
            load_wout(l, 1)
            vones()

            def evac_v(tt, ps, bps):
                v = vst(tt).rearrange("p (b c) -> p b c", b=2)
                evac_copy(v[:, :, 0:64], ps.rearrange("p (b c) -> p b c", b=2), reads=[bps], writes=[B_VST[tt]])
            proj_tm(l, TM_VS[0], 128, evac_v)
            proj_fm(l, QN0, 0)
            proj_fm(l, QN1, 1)
            proj_fm(l, KVC, 2)
            wgn = ws16(o_wgn, 1024).rearrange("p (k n) -> p k n", k=8)
            r0 = (l * NFM + GN) * 128
            dma(C_WGN, wgn, wfm_b[r0:r0 + 128, :].rearrange("p (k n) -> p k n", k=8), writes=[B_WGN])
            blk = ws16(o_un, 1016).rearrange("p (a i) -> p a i", a=8)
            gel = [ws16(o_un + 1024, 128), ws16(o_un + 1152, 128)]
            KC = ws16(o_un + 1280, 128)
            VC = ws16(o_un + 1408, 128)
            w2 = ws16(o_un + 3584, 192)
            C_W2 = C_W2X
            dma(C_W2, w2, w2_b[l * 128:(l + 1) * 128, :], writes=[B_UN[7]])
            hb = [mbank(), mbank()]
            kvc = fmo(2)
            for quarter in range(4):
                s = nxt("wfm", 3)
                w1q = ws16(o_wfm + s * 1024, 1024).rearrange("p (a n) -> p a n", a=8)
                dma(C_WFM[s], w1q, w1_b[l * 128:(l + 1) * 128, quarter * 1024:(quarter + 1) * 1024].rearrange(
                    "p (a n) -> p a n", a=8), writes=[B_WFM[s]])
                src = _ap(kvc, kvc.offset + 8 * quarter, [kvc.ap[0], [1, 8], [16, 127]])
                pe = PET[:, l * 32 + 8 * quarter:l * 32 + 8 * quarter + 8].unsqueeze(2).broadcast_to([128, 8, 127])
                op("dve", lambda e, src=src, pe=pe: e.tensor_tensor(out=blk, in0=src, in1=pe, op=ALU.add),
                   reads=[B_FMO[2][0], B_FMO[2][1], B_FMO[2][2], B_FMO[2][3], B_CONST], writes=[B_UN[0]])
                for a in range(8):
                    pidx = quarter * 8 + a
                    for kv in range(2):
                        mm(PS[hb[kv]][:, 0:127], w1q[kv * 64:(kv + 1) * 64, a, :], blk[kv * 64:(kv + 1) * 64, a, :],
                           pidx == 0, pidx == 31, reads=[B_WFM[s], B_UN[0]], writes=[B_PS[hb[kv]]])
            tmpa = ws32(o_un + 3840, 512)
            for kv in range(2):
                xh = tmpa[:, 256:383]
                ta = tmpa[:, 0:127]
                tb_ = tmpa[:, 128:255]
                op("dve", lambda e, xh=xh, kv=kv: e.tensor_copy(out=xh, in_=PS[hb[kv]][:, 0:127]),
                   reads=[B_PS[hb[kv]]], writes=[B_UN[8]])
                op("dve", lambda e, xh=xh, ta=ta: e.tensor_tensor(out=ta, in0=xh, in1=xh, op=ALU.mult),
                   reads=[B_UN[8]], writes=[B_UN[8]])
                op("dve", lambda e, ta=ta: e.tensor_scalar(out=ta, in0=ta, scalar1=0.044715, scalar2=1.0, op0=ALU.mult, op1=ALU.add),
                   reads=[B_UN[8]], writes=[B_UN[8]])
                op("dve", lambda e, xh=xh, ta=ta: e.tensor_tensor(out=ta, in0=xh, in1=ta, op=ALU.mult),
                   reads=[B_UN[8]], writes=[B_UN[8]])
                op("act", lambda e, ta=ta, tb_=tb_: e.activation(out=tb_, in_=ta, func=AF.Exp, scale=-1.5957691216),
                   reads=[B_UN[8]], writes=[B_UN[9]])
                op("dve", lambda e, tb_=tb_: e.tensor_scalar(out=tb_, in0=tb_, scalar1=1.0, scalar2=None, op0=ALU.add),
                   reads=[B_UN[9]], writes=[B_UN[9]])
                op("dve", lambda e, tb_=tb_: e.reciprocal(out=tb_, in_=tb_), reads=[B_UN[9]], writes=[B_UN[9]])
                op("dve", lambda e, xh=xh, tb_=tb_, kv=kv: e.tensor_tensor(out=gel[kv][:, 0:127], in0=xh, in1=tb_, op=ALU.mult),
                   reads=[B_UN[8], B_UN[9]], writes=[B_UN[1 + kv]])
            m = mbank()
            mm(PS[m][:, 0:127], w2[:, 0:128], gel[0][:, 0:127], True, True, reads=[B_UN[7], B_UN[1]], writes=[B_PS[m]])
            op("dve", lambda e, m=m: e.tensor_copy(out=KC[:, 0:127], in_=PS[m][:, 0:127]), reads=[B_PS[m]], writes=[B_UN[3]])
            m = mbank()
            mm(PS[m][0:127, 0:64], gel[1][:, 0:127], w2[:, 128:192], True, True, reads=[B_UN[7], B_UN[2]], writes=[B_PS[m]])
            op("pool", lambda e: e.memset(VC[:, 64:128], 1.0), writes=[B_UN[4]])
            op("dve", lambda e, m=m: e.tensor_copy(out=VC[0:127, 0:64], in_=PS[m][0:127, 0:64]), reads=[B_PS[m], B_UN[4]],
               writes=[B_UN[4]])
            proj_fm(l, KSD, 2)
            proj_fm(l, KWD, 3)
            gate = ws16(o_un + 1536, 512)
            selbT = ws16(o_un + 2048, 512)
            imp = ws32(o_un + 2560, 512)
            for qb in range(NQB):
                m = mbank()
                for kc in range(8):
                    mm(PS[m][0:12, :], wgn[:, kc, 0:12], HT[:, kc, qb * 512:(qb + 1) * 512], kc == 0, kc == 7,
                       reads=[B_WGN, B_HT[qb]], writes=[B_PS[m]])
                tg = tmpa
                op("act", lambda e, m=m: e.activation(out=tg[0:12, :], in_=PS[m][0:12, :], func=AF.Exp, scale=-1.0),
                   reads=[B_PS[m]], writes=[B_UN[8]])
                op("dve", lambda e: e.tensor_scalar(out=tg[0:12, :], in0=tg[0:12, :], scalar1=1.0, scalar2=None, op0=ALU.add),
                   reads=[B_UN[8]], writes=[B_UN[8]])
                op("dve", lambda e: e.reciprocal(out=tg[0:12, :], in_=tg[0:12, :]), reads=[B_UN[8]], writes=[B_UN[8]])
                op("dve", lambda e: e.tensor_copy(out=gate[0:12, :], in_=tg[0:12, :]), reads=[B_UN[8]], writes=[B_UN[5]])
                nk = min(127, 32 * qb + 31)
                cmp_norm = []
                for h in range(4):
                    pair, half = h // 2, h % 2
                    z = zbank()
                    zt = PS[z]
                    q = fmo(pair)[half * 64:(half + 1) * 64, qb * 512:(qb + 1) * 512]
                    mm(zt[0:nk, :], KC[half * 64:(half + 1) * 64, 0:nk], q, True, False, reads=[B_FMO[pair][qb], B_UN[3]],
                       writes=[B_PS[z]])
                    if USE_ROW[h]:
                        mm(zt[0:nk, :], cbv("rowsel", 0, 12, h * 128, h * 128 + nk), cbv("rrow", 0, 12), False, False,
                           reads=[B_CONST], writes=[B_PS[z]])
                    mm(zt[0:nk, :], cbv("ident", 0, 128, 0, nk), cbv("mc", 0, 128, qb * 512, (qb + 1) * 512), False, True,
                       reads=[B_CONST], writes=[B_PS[z]])
                    p = nxt("pt", 4)
                    pt = ws16(o_pt + p * 512, 512)
                    bias = cfv("biasc" if USE_ROW[h] else "biascm", 0, nk, h * 4 + qb, h * 4 + qb + 1)
                    op("act", lambda e, pt=pt, zt=zt, bias=bias, nk=nk: e.activation(
                        out=pt[0:nk, :], in_=zt[0:nk, :], func=AF.Exp, scale=SC64, bias=bias),
                       reads=[B_PS[z], B_CONST], writes=[B_PT[p]])
                    ob = obank()
                    mm(PS[ob][:, :], VC[0:nk, :], pt[0:nk, :], True, True, reads=[B_PT[p], B_UN[4]], writes=[B_PS[ob]])
                    m = mbank()
                    mm(PS[m][0:64, :], cbv("gaug", 0, nk), pt[0:nk, :], True, True, reads=[B_PT[p], B_CONST], writes=[B_PS[m]])
                    r = nxt("rd", 2)
                    rd = ws32(o_rd + r * 1024, 512)
                    op("dve", lambda e, rd=rd, ob=ob: e.tensor_scalar(out=rd[0:64, :], in0=PS[ob][64:128, :], scalar1=1e-30,
                                                                      scalar2=None, op0=ALU.max),
                       reads=[B_PS[ob]], writes=[B_RD[r]])
                    op("dve", lambda e, rd=rd: e.reciprocal(out=rd[0:64, :], in_=rd[0:64, :]), reads=[B_RD[r]], writes=[B_RD[r]])
                    if h == 0:
                        op("dve", lambda e, rd=rd, m=m: e.tensor_tensor(out=imp[0:32, :], in0=PS[m][0:32, :], in1=rd[0:32, :],
                                                                        op=ALU.mult),
                           reads=[B_PS[m], B_RD[r]], writes=[B_UN[6]])
                    else:
                        tq_ = tmpa
                        op("dve", lambda e, rd=rd, m=m: e.tensor_tensor(out=tq_[0:32, :], in0=PS[m][0:32, :], in1=rd[0:32, :],
                                                                        op=ALU.mult),
                           reads=[B_PS[m], B_RD[r]], writes=[B_UN[8]])
                        op("dve", lambda e: e.tensor_tensor(out=imp[0:32, :], in0=imp[0:32, :], in1=tq_[0:32, :], op=ALU.add),
                           reads=[B_UN[8], B_UN[6]], writes=[B_UN[6]])
                    gm_ = mbank()
                    mm(PS[gm_][0:64, :], cbv("gatesel", 0, 12, (h * 3 + 0) * 64, (h * 3 + 1) * 64), gate[0:12, :], True, True,
                       reads=[B_UN[5], B_CONST], writes=[B_PS[gm_]])
                    wgt = ws32(o_un + 4864, 512)
                    op("dve", lambda e, rd=rd, wgt=wgt, gm_=gm_: e.tensor_tensor(out=wgt[0:64, :], in0=PS[gm_][0:64, :],
                                                                               in1=rd[0:64, :], op=ALU.mult),
                       reads=[B_PS[gm_], B_RD[r]], writes=[B_UN[9]])
                    dst = ost()[half * 64:(half + 1) * 64, pair, :]
                    op("dve", lambda e, dst=dst, ob=ob, wgt=wgt: e.tensor_tensor(out=dst, in0=PS[ob][0:64, :], in1=wgt[0:64, :],
                                                                               op=ALU.mult),
                       reads=[B_PS[ob], B_UN[9]], writes=[B_OST])
                for j in range(4):
                    tt = 4 * qb + j
                    m = mbank()
                    op("pe", lambda e, m=m, j=j: e.transpose(out=PS[m][:, 0:32], in_=imp[0:32, j * 128:(j + 1) * 128],
                                                             identity=cfv("identf", 0, 32, 0, 32)),
                       reads=[B_UN[6], B_CONST], writes=[B_PS[m]])
                    ta = tmpa[:, 0:32]
                    t8 = tmpa[:, 32:40]
                    tsel = tmpa[:, 64:96]
                    cand = cbv("cand", 0, 128, tt * 32, (tt + 1) * 32)
                    candm1 = cbv("candm1", 0, 128, tt * 32, (tt + 1) * 32)
                    forced = cbv("forced", 0, 128, tt * 32, (tt + 1) * 32)
                    op("dve", lambda e, m=m, cand=cand: e.tensor_tensor(out=ta, in0=PS[m][:, 0:32], in1=cand, op=ALU.mult),
                       reads=[B_PS[m], B_CONST], writes=[B_UN[8]])
                    op("dve", lambda e, candm1=candm1: e.tensor_tensor(out=ta, in0=ta, in1=candm1, op=ALU.add),
                       reads=[B_UN[8], B_CONST], writes=[B_UN[8]])
                    op("dve", lambda e: e.max(out=t8, in_=ta), reads=[B_UN[8]], writes=[B_UN[9]])
                    op("dve", lambda e: e.tensor_scalar(out=tsel, in0=ta, scalar1=t8[:, 4:5], scalar2=None, op0=ALU.is_ge),
                       reads=[B_UN[8], B_UN[9]], writes=[B_UN[10]])
                    op("dve", lambda e, cand=cand: e.tensor_tensor(out=tsel, in0=tsel, in1=cand, op=ALU.mult),
                       reads=[B_UN[10], B_CONST], writes=[B_UN[10]])
                    op("dve", lambda e, forced=forced: e.tensor_tensor(out=tsel, in0=tsel, in1=forced, op=ALU.add),
                       reads=[B_UN[10], B_CONST], writes=[B_UN[10]])
                    op("dve", lambda e: e.tensor_scalar(out=tsel, in0=tsel, scalar1=-1.0, scalar2=-NEG, op0=ALU.add, op1=ALU.mult),
                       reads=[B_UN[10]], writes=[B_UN[10]])
                    m2 = mbank()
                    op("pe", lambda e, m2=m2: e.transpose(out=PS[m2][0:32, 0:128], in_=tsel, identity=cfv("identf")),
                       reads=[B_UN[10], B_CONST], writes=[B_PS[m2]])
                    op("dve", lambda e, m2=m2, j=j: e.tensor_copy(out=selbT[0:32, j * 128:(j + 1) * 128], in_=PS[m2][0:32, 0:128]),
                       reads=[B_PS[m2]], writes=[B_UN[11]])
                for h in range(4):
                    pair, half = h // 2, h % 2
                    for br in (1, 2):
                        ob = obank()
                        kslot = 2 if br == 1 else 3

                        def k_lhsT(kt, kslot=kslot, half=half):
                            return fmo(kslot)[half * 64:(half + 1) * 64, kt * 128:(kt + 1) * 128]

                        def k_reads(kt, kslot=kslot):
                            return [B_FMO[kslot][kt // 4]]

                        def v_lhsT(kt, br=br):
                            return vst(kt)[:, (br - 1) * 128:br * 128]

                        def v_reads(kt):
                            return [B_VST[kt]]
                        extra = None
                        if br == 1:
                            def extra(kt, zt, c0, c1):
                                return [(zt[:, c0:c1], cbv("esel", 0, 32, kt * 128, (kt + 1) * 128), selbT[0:32, c0:c1],
                                         [B_UN[11], B_CONST])]
                        softmax_map(qb, tiles_for(qb, None if br == 1 else 4, CUT[h]), pair, half * 64, k_lhsT, k_reads, h, SC64,
                                    v_lhsT, v_reads, ob, extra=extra)
                        r = nxt("rd", 2)
                        rd = ws32(o_rd + r * 1024, 512)
                        op("dve", lambda e, rd=rd, ob=ob: e.reciprocal(out=rd[0:64, :], in_=PS[ob][64:128, :]),
                           reads=[B_PS[ob]], writes=[B_RD[r]])
                        gm_ = mbank()
                        mm(PS[gm_][0:64, :], cbv("gatesel", 0, 12, (h * 3 + br) * 64, (h * 3 + br + 1) * 64), gate[0:12, :],
                           True, True, reads=[B_UN[5], B_CONST], writes=[B_PS[gm_]])
                        op("dve", lambda e, rd=rd, gm_=gm_: e.tensor_tensor(out=rd[0:64, :], in0=PS[gm_][0:64, :], in1=rd[0:64, :],
                                                                            op=ALU.mult),
                           reads=[B_PS[gm_], B_RD[r]], writes=[B_RD[r]])
                        tq_ = tmpa[half * 64:(half + 1) * 64, :]
                        op("dve", lambda e, rd=rd, ob=ob, tq_=tq_: e.tensor_tensor(out=tq_, in0=PS[ob][0:64, :], in1=rd[0:64, :],
                                                                                   op=ALU.mult),
                           reads=[B_PS[ob], B_RD[r]], writes=[B_UN[8]])
                        dst = ost()[half * 64:(half + 1) * 64, pair, :]
                        op("dve", lambda e, dst=dst, tq_=tq_: e.tensor_tensor(out=dst, in0=dst, in1=tq_, op=ALU.add),
                           reads=[B_UN[8], B_OST], writes=[B_OST])
                group_norm_and_project(l, 1, qb)

        def ffn_phase(l):
            norm_to_ht(l * 24 + 8, o_hn2, o_junk2)
            ut = ws16(o_ut, 16 * 512).rearrange("p (f n) -> p f n", f=16)
            rq = [0, 0]
            for tb in range(NQB):
                for hf in range(2):
                    for blk4 in range(4):
                        blk = hf * 4 + blk4
                        s = rq[0]
                        rq[0] = (s + 1) % 2
                        wup = ws16(o_wup + s * 4096, 4096).rearrange("p (k n) -> p k n", k=8)
                        r0 = (l * 8 + blk) * 128
                        dma(C_WUP[s], wup, wup_b[r0:r0 + 128, :].rearrange("p (k n) -> p k n", k=8), writes=[B_WUP[s]])
                        for f4 in range(4):
                            fl = blk4 * 4 + f4
                            m = mbank()
                            for kc in range(8):
                                mm(PS[m][:, :], wup[:, kc, f4 * 128:(f4 + 1) * 128], HT[:, kc, tb * 512:(tb + 1) * 512],
                                   kc == 0, kc == 7, reads=[B_WUP[s], B_HT[tb]], writes=[B_PS[m]])
                            ri = nxt("rl", 2)
                            rl = ws32(o_rl + ri * 1024, 512)
                            op("act", lambda e, rl=rl, m=m: e.activation(out=rl, in_=PS[m][:, :], func=AF.Relu),
                               reads=[B_PS[m]], writes=[B_RL[ri]])
                            op("pool", lambda e, rl=rl, fl=fl: e.tensor_tensor(out=ut[:, fl, :], in0=rl, in1=rl, op=ALU.mult),
                               reads=[B_RL[ri]], writes=[B_UT])
                    for cbk in range(2):
                        slots = []
                        for g2 in range(2):
                            s = rq[1]
                            rq[1] = (s + 1) % 4
                            wdn = ws16(o_wdn + s * 4096, 4096).rearrange("p (f n) -> p f n", f=8)
                            grp = hf * 2 + g2
                            r0 = (l * 8 + cbk * 4 + grp) * 128
                            dma(C_WDN[s], wdn, wdn_b[r0:r0 + 128, :].rearrange("p (f n) -> p f n", f=8), writes=[B_WDN[s]])
                            slots.append((s, wdn))
                        for j in range(4):
                            tt = 4 * tb + j
                            m = mbank()
                            for g2 in range(2):
                                s, wdn = slots[g2]
                                for f8 in range(8):
                                    fl = g2 * 8 + f8
                                    mm(PS[m][:, :], ut[:, fl, j * 128:(j + 1) * 128], wdn[:, f8, :], fl == 0, fl == 15,
                                       reads=[B_UT, B_WDN[s]], writes=[B_PS[m]])
                            op("dve", lambda e, tt=tt, cbk=cbk, m=m: e.tensor_tensor(
                                out=XS[:, tt, cbk * 512:(cbk + 1) * 512], in0=PS[m][:, :],
                                in1=XS[:, tt, cbk * 512:(cbk + 1) * 512], op=ALU.add),
                               reads=[B_PS[m], B_X[tt]], writes=[B_X[tt]])

        def final_norm(si):
            nf = ws32(o_nf, 1024)
            dma(C_MISC, nf, nf_d[:, :], writes=[B_NF])
            junk = ws16(o_junk2, 1024)
            for tb in range(NQB):
                for j in range(4):
                    tt = 4 * tb + j
                    op("act", lambda e, tt=tt: e.activation(out=junk, in_=XS[:, tt, :], func=AF.Square,
                                                            accum_out=SSQ[:, tt:tt + 1]),
                       reads=[B_X[tt]], writes=[B_JUNK, B_SSQ[tb]])
                sl = slice(4 * tb, 4 * tb + 4)
                op("act", lambda e, sl=sl: e.activation(out=RSTD[:, sl], in_=SSQ[:, sl], func=AF.Ln,
                                                        scale=1.0 / D, bias=cfv("epsc")),
                   reads=[B_SSQ[tb], B_CONST], writes=[B_RSTD[tb]])
                op("act", lambda e, sl=sl: e.activation(out=RSTD[:, sl], in_=RSTD[:, sl], func=AF.Exp, scale=-0.5),
                   reads=[B_RSTD[tb]], writes=[B_RSTD[tb]])
                for j in range(4):
                    tt = 4 * tb + j
                    fi = 0
                    fin = ws32(o_fin, 1024)
                    op("dve", lambda e, tt=tt, fin=fin: e.scalar_tensor_tensor(
                        out=fin, in0=XS[:, tt, :], scalar=RSTD[:, tt:tt + 1], in1=nf, op0=ALU.mult, op1=ALU.mult),
                       reads=[B_X[tt], B_RSTD[tb], B_NF], writes=[B_FIN[fi]])
                    dma(C_FIN[fi], out_d[si, tt * 128:(tt + 1) * 128, :], fin, reads=[B_FIN[fi]])

        for si in range(nseq):
            for q4 in range(4):
                dma(C_X[q4], XS[:, 4 * q4:4 * q4 + 4, :],
                    x_d[si, q4 * 512:(q4 + 1) * 512, :].rearrange("(t p) d -> p t d", p=128),
                    writes=[B_X[4 * q4 + i] for i in range(4)])
            for l in range(nlayer):
                norm_to_ht(l * 24, o_hn, o_junk)
                for mx in mixers:
                    if mx == "sb":
                        mixer_sb(l)
                    elif mx == "nsa":
                        mixer_nsa(l)
                    elif mx == "diff":
                        mixer_diff2(l)
                    elif mx == "swa":
                        mixer_swa(l)
                    kb.barrier()
                if ffn:
                    ffn_phase(l)
                    kb.barrier()
            kb.barrier()
            final_norm(si)
            kb.barrier()

        build_program.last_counts = {k: len(v) for k, v in kb.prog.items()}
        @block.tensor
        def _(e):
            for f in kb.prog["pe"]:
                f(e)

        @block.scalar
        def _(e):
            for f in kb.prog["act"]:
                f(e)

        @block.vector
        def _(e):
            for f in kb.prog["dve"]:
                f(e)

        @block.gpsimd
        def _(e):
            for f in kb.prog["pool"]:
                f(e)

        @block.sync
        def _(e):
            for f in kb.prog["sp"]:
                f(e)
    return nc, (cbarr, cfarr)


def prep_weights(inp, nlayer=DEPTH):
    w_in = np.asarray(inp["w_in"], np.float32)
    L = nlayer
    wfm = np.zeros((L, NFM, 1024, 128), np.float32)

    def cols(a, b):
        return w_in[:L, :, a:b]
    wfm[:, QA0] = cols(0, 128)
    wfm[:, QA1] = cols(128, 256)
    wfm[:, KA0] = cols(256, 384)
    wfm[:, KA1] = cols(384, 512)
    wfm[:, QN0] = cols(768, 896)
    wfm[:, QN1] = cols(896, 1024)
    wfm[:, KVC] = cols(1024, 1152)
    wfm[:, KSD, :, 0:64] = cols(1152, 1216)
    wfm[:, KSD, :, 64:128] = cols(1152, 1216)
    wfm[:, KWD, :, 0:64] = cols(1280, 1344)
    wfm[:, KWD, :, 64:128] = cols(1280, 1344)
    wfm[:, GN, :, 0:12] = cols(1408, 1420)
    wfm[:, QC0] = cols(1420, 1548)
    wfm[:, QC1] = cols(1548, 1676)
    kc0 = 1676
    for pair in range(2):
        for half in range(2):
            h = 2 * pair + half
            base = kc0 + h * 64
            wfm[:, KC00 + 2 * pair, :, half * 64:half * 64 + 32] = cols(base, base + 32)
            wfm[:, KC10 + 2 * pair, :, half * 64 + 32:half * 64 + 64] = cols(base + 32, base + 64)
    wfm[:, QD0] = cols(2188, 2316)
    wfm[:, QD1] = cols(2316, 2444)
    wfm[:, KD0, :, 0:64] = cols(2444, 2508)
    wfm[:, KD0, :, 64:128] = cols(2444, 2508)
    wfm[:, KD1, :, 0:64] = cols(2508, 2572)
    wfm[:, KD1, :, 64:128] = cols(2508, 2572)
    wfm = wfm.reshape(L, NFM, 8, 128, 128).transpose(0, 1, 3, 2, 4).reshape(L * NFM * 128, 1024)

    wtm = np.concatenate([cols(512, 768), cols(1216, 1280), cols(1344, 1408), cols(1932, 2188), cols(2572, 2700)], axis=2)
    wtm = wtm.reshape(L, 8, 128, NTM).transpose(0, 2, 1, 3).reshape(L * 128, 8 * NTM)

    w_out = np.asarray(inp["w_out"], np.float32)[:L]
    wout = w_out.reshape(L, 4, 2, 128, 1024).transpose(0, 1, 3, 2, 4).reshape(L * 4 * 128, 2048)
    w_up = np.asarray(inp["w_up"], np.float32)[:L]
    wup = w_up.reshape(L, 8, 128, 8, 512).transpose(0, 3, 2, 1, 4).reshape(L * 8 * 128, 4096)
    w_dn = np.asarray(inp["w_down"], np.float32)[:L]
    wdn = w_dn.reshape(L, 4, 8, 128, 2, 512).transpose(0, 4, 1, 3, 2, 5).reshape(L * 8 * 128, 4096)
    w1k = np.asarray(inp["cmp_w1_k"], np.float32)[:L].reshape(L, 32, 64, 128).transpose(0, 2, 1, 3)
    w1v = np.asarray(inp["cmp_w1_v"], np.float32)[:L].reshape(L, 32, 64, 128).transpose(0, 2, 1, 3)
    w1 = np.concatenate([w1k, w1v], axis=1).reshape(L * 128, 4096)
    w2k = np.asarray(inp["cmp_w2_k"], np.float32)[:L]
    w2v = np.asarray(inp["cmp_w2_v"], np.float32)[:L]
    w2 = np.concatenate([w2k, w2k, w2v], axis=2).reshape(L * 128, 192)
    gt = np.zeros((128, L * 24), np.float32)
    for l in range(L):
        gt[:, l * 24 + 0:l * 24 + 8] = np.asarray(inp["norm_attn"], np.float32)[l].reshape(8, 128).T
        gt[:, l * 24 + 8:l * 24 + 16] = np.asarray(inp["norm_mlp"], np.float32)[l].reshape(8, 128).T
        gt[:, l * 24 + 16:l * 24 + 24] = np.asarray(inp["g_mix"], np.float32)[l].reshape(8, 128).T
    lamv = np.zeros((128, L * 128), np.float32)
    for l in range(L):
        for i, k in enumerate(("diff_lq1", "diff_lk1", "diff_lq2", "diff_lk2")):
            lamv[:, l * 128 + i * 32:l * 128 + (i + 1) * 32] = np.asarray(inp[k], np.float32)[l][None, :]
    sinkt = np.ascontiguousarray(np.asarray(inp["sinks"], np.float32)[:L].T)
    pet = np.zeros((128, L * 32), np.float32)
    for l in range(L):
        pet[0:64, l * 32:(l + 1) * 32] = np.asarray(inp["cmp_pe_k"], np.float32)[l].T
        pet[64:128, l * 32:(l + 1) * 32] = np.asarray(inp["cmp_pe_v"], np.float32)[l].T
    normf = np.ascontiguousarray(np.broadcast_to(np.asarray(inp["norm_final"], np.float32)[None, :], (128, 1024)))
    return dict(wfm=np.ascontiguousarray(wfm), wtm=np.ascontiguousarray(wtm), wout=np.ascontiguousarray(wout),
                wup=np.ascontiguousarray(wup), wdn=np.ascontiguousarray(wdn), w1=np.ascontiguousarray(w1),
                w2=np.ascontiguousarray(w2), gt=gt, lamv=lamv, sinkt=sinkt, pet=pet, normf=normf)


_CACHE = {}


def kernel(**inputs):
    x = np.asarray(inputs["x"], np.float32)
    ncores = 8
    nseq = x.shape[0] // ncores
    if "prog" not in _CACHE:
        _CACHE["prog"] = build_program(nseq=nseq)
    nc, (cbarr, cfarr) = _CACHE["prog"]
    w = prep_weights(inputs)
    in_maps = []
    for c in range(ncores):
        in_maps.append(core_inputs(w, x[c * nseq:(c + 1) * nseq], cbarr, cfarr, c))
    res = run_bass_kernel_spmd(nc, in_maps, core_ids=list(range(ncores)))
    out = np.concatenate([np.asarray(r["out"]) for r in res.results], axis=0)
    return out.astype(np.float32)


def core_inputs(w, xs, cbarr, cfarr, c):
    big = ("wfm", "wtm", "wout", "wup", "wdn", "w1")
    if True:
        m = dict(w)
        for k in big:
            a = np.empty((w[k].shape[0] + 1, w[k].shape[1]), np.float32)
            a[:-1] = w[k]
            a[-1] = float(c)
            m[k] = a
        cb = np.empty((129, cbarr.shape[1]), cbarr.dtype)
        cb[:128] = cbarr
        cb[128] = float(c)
        m["cb16"] = cb
        m["x"] = np.ascontiguousarray(xs)
        m["cf32"] = cfarr
    return m
```
